# Optimizing a Trainium2 kernel written in Bass

```python
import math
import jax, jax.numpy as jnp
from jax import lax
import numpy as np

D_MODEL = 4096
BATCH = 16
SEQ = 256
DEPTH = 1
DEC_BATCH = 2
DEC_SEQ = 1024
PAST_LEN = 512

GRID_W = 64
N_HEADS = 16
NOPE_DIM = 128
ROPE_DIM = 64
V_DIM = 128
QK_DIM = NOPE_DIM + ROPE_DIM
Q_LORA = D_MODEL // 4
KV_LORA = D_MODEL // 8
MLA_W = N_HEADS * V_DIM
ROPE_BASE = 10000.0
AXIS_PAIRS = ROPE_DIM // 4
Q_BLOCK = 128
S5_W = D_MODEL // 2
S5_GROUP_CH = 16
S5_GROUPS = S5_W // S5_GROUP_CH
S5_STATE = 64
N_DIR = 2
ALPHA = (2.0 * DEPTH) ** 0.25
BETA = (8.0 * DEPTH) ** -0.25
LN_EPS = 1e-6
IN_SEGMENTS = (Q_LORA, KV_LORA, ROPE_DIM, MLA_W, S5_W, S5_W, D_MODEL, D_MODEL)
IN_COLS = sum(IN_SEGMENTS)

kernel_name = "hybrid_mla_s5_diffusion_step"


def _split_points():
    return [int(v) for v in np.cumsum(IN_SEGMENTS)[:-1]]


def _layernorm(x):
    xf = x.astype(jnp.float32)
    mu = jnp.mean(xf, axis=-1, keepdims=True)
    var = jnp.mean(jnp.square(xf - mu), axis=-1, keepdims=True)
    return ((xf - mu) * lax.rsqrt(var + LN_EPS)).astype(x.dtype)


def _rmsnorm(x, g):
    xf = x.astype(jnp.float32)
    y = xf * lax.rsqrt(jnp.mean(jnp.square(xf), axis=-1, keepdims=True) + LN_EPS)
    return y.astype(x.dtype) * g


def _axial_tables(n_tokens):
    rows = n_tokens // GRID_W
    t = jnp.arange(rows * GRID_W)
    row = (t // GRID_W).astype(jnp.float32)
    col = (t % GRID_W).astype(jnp.float32)
    inv = jnp.power(ROPE_BASE, -jnp.arange(AXIS_PAIRS, dtype=jnp.float32) / AXIS_PAIRS)
    ang_r = row[:, None] * inv[None, :]
    ang_c = col[:, None] * inv[None, :]
    return (jnp.cos(ang_r), jnp.sin(ang_r), jnp.cos(ang_c), jnp.sin(ang_c))


def _rot_half(x, cos, sin):
    x1, x2 = jnp.split(x, 2, axis=-1)
    return jnp.concatenate([x1 * cos - x2 * sin, x2 * cos + x1 * sin], axis=-1)


def _axial_rope(x, tabs):
    cr, sr, cc, sc = [t.astype(x.dtype) for t in tabs]
    xr, xc = jnp.split(x, 2, axis=-1)
    return jnp.concatenate([_rot_half(xr, cr, sr), _rot_half(xc, cc, sc)], axis=-1)


def _mla_attend(q_nope, q_pe, k_nope, k_pe, v):
    B, Lq = q_nope.shape[0], q_nope.shape[1]
    nb = Lq // Q_BLOCK
    scale = QK_DIM ** -0.5

    def block(args):
        qn, qp = args
        s = jnp.einsum("bqhd,bkhd->bhqk", qn, k_nope) + jnp.einsum("bqhr,bkr->bhqk", qp, k_pe)
        p = jax.nn.softmax(s.astype(jnp.float32) * scale, axis=-1).astype(v.dtype)
        return jnp.einsum("bhqk,bkhd->bqhd", p, v)

    qn_b = q_nope.reshape(B, nb, Q_BLOCK, N_HEADS, NOPE_DIM).swapaxes(0, 1)
    qp_b = q_pe.reshape(B, nb, Q_BLOCK, N_HEADS, ROPE_DIM).swapaxes(0, 1)
    out = lax.map(block, (qn_b, qp_b))
    return out.swapaxes(0, 1).reshape(B, Lq, N_HEADS * V_DIM)


def _s5_discretize(a_re, a_im, log_dt, b_re, b_im):
    a_re = a_re.astype(jnp.float32)
    a_im = a_im.astype(jnp.float32)
    dt = jnp.exp(log_dt.astype(jnp.float32))[:, None]
    mag = jnp.exp(dt * a_re)
    ab_re = mag * jnp.cos(dt * a_im)
    ab_im = mag * jnp.sin(dt * a_im)
    den = jnp.square(a_re) + jnp.square(a_im)
    p_re = ab_re - 1.0
    q_re = (p_re * a_re + ab_im * a_im) / den
    q_im = (ab_im * a_re - p_re * a_im) / den
    b_re = b_re.astype(jnp.float32)
    b_im = b_im.astype(jnp.float32)
    bb_re = q_re[..., None] * b_re - q_im[..., None] * b_im
    bb_im = q_re[..., None] * b_im + q_im[..., None] * b_re
    return ab_re, ab_im, bb_re, bb_im


def _affine_combine(e1, e2):
    a1r, a1i, b1r, b1i = e1
    a2r, a2i, b2r, b2i = e2
    return (a2r * a1r - a2i * a1i, a2r * a1i + a2i * a1r,
            a2r * b1r - a2i * b1i + b2r, a2r * b1i + a2i * b1r + b2i)


def _s5_scan(u, h0, ab_re, ab_im, bb_re, bb_im, c_re, c_im, reverse):
    bu_re = jnp.einsum("blgc,gnc->blgn", u, bb_re)
    bu_im = jnp.einsum("blgc,gnc->blgn", u, bb_im)
    a_re = jnp.broadcast_to(ab_re, bu_re.shape)
    a_im = jnp.broadcast_to(ab_im, bu_im.shape)
    pa_re, pa_im, x_re, x_im = lax.associative_scan(
        _affine_combine, (a_re, a_im, bu_re, bu_im), reverse=reverse, axis=1)
    if h0 is None:
        idx = 0 if reverse else -1
        final = (x_re[:, idx], x_im[:, idx])
    else:
        h_re = h0[0][:, None]
        h_im = h0[1][:, None]
        x_re, x_im = (x_re + pa_re * h_re - pa_im * h_im,
                      x_im + pa_re * h_im + pa_im * h_re)
        final = None
    y = (jnp.einsum("blgn,gcn->blgc", x_re, c_re)
         - jnp.einsum("blgn,gcn->blgc", x_im, c_im))
    return y, final


def _s5_branch(u, h0, a_re, a_im, log_dt, b_re, b_im, c_re, c_im, d_skip, w_glu, b_glu):
    B, L, W = u.shape
    uf = u.astype(jnp.float32)
    ug = uf.reshape(B, L, S5_GROUPS, S5_GROUP_CH)
    y = d_skip.astype(jnp.float32) * uf
    fin_re, fin_im = [], []
    for d in range(N_DIR):
        ab_re, ab_im, bb_re, bb_im = _s5_discretize(a_re[d], a_im[d], log_dt[d], b_re[d], b_im[d])
        h0_d = None if h0 is None else (h0[0][:, d].astype(jnp.float32), h0[1][:, d].astype(jnp.float32))
        y_d, fin = _s5_scan(ug, h0_d, ab_re, ab_im, bb_re, bb_im,
                            c_re[d].astype(jnp.float32), c_im[d].astype(jnp.float32), reverse=(d == 1))
        y = y + y_d.reshape(B, L, W)
        if fin is not None:
            fin_re.append(fin[0])
            fin_im.append(fin[1])
    y = jax.nn.gelu(y).astype(u.dtype)
    y = y * jax.nn.sigmoid(y @ w_glu + b_glu)
    if h0 is None:
        return y, (jnp.stack(fin_re, axis=1).astype(u.dtype), jnp.stack(fin_im, axis=1).astype(u.dtype))
    return y, None


def _layer(x, cond, ctx, w_ada, b_ada, w_in, g_qn, w_uq, g_kvn, w_ukv,
           s5_a_re, s5_a_im, s5_log_dt, s5_b_re, s5_b_im, s5_c_re, s5_c_im, s5_d,
           w_glu, b_glu, w_pa, w_pb, w_o, ln_g, ln_b):
    B, L, _ = x.shape
    mod = (jax.nn.silu(cond) @ w_ada + b_ada).reshape(-1, 1, 3 * D_MODEL)
    shift, scale, gate = jnp.split(mod, 3, axis=-1)
    h = _layernorm(x) * (1.0 + scale) + shift
    proj = h @ w_in
    cq, ckv_raw, kpe, za, ub, zb, ga, gb = jnp.split(proj, _split_points(), axis=-1)
    ckv = _rmsnorm(ckv_raw, g_kvn)
    q = (_rmsnorm(cq, g_qn) @ w_uq).reshape(B, L, N_HEADS, QK_DIM)
    q_nope, q_pe = q[..., :NOPE_DIM], q[..., NOPE_DIM:]
    kv = (ckv @ w_ukv).reshape(B, L, N_HEADS, NOPE_DIM + V_DIM)
    k_nope, v = kv[..., :NOPE_DIM], kv[..., NOPE_DIM:]
    s5_params = (s5_a_re, s5_a_im, s5_log_dt, s5_b_re, s5_b_im, s5_c_re, s5_c_im, s5_d, w_glu, b_glu)
    if ctx is None:
        attn = _mla_attend(q_nope, q_pe, k_nope, kpe, v)
        s5_y, s5_fin = _s5_branch(ub, None, *s5_params)
        new_state = (ckv, kpe, s5_fin[0], s5_fin[1])
    else:
        c_ckv, c_kpe, s_re, s_im = ctx
        tabs = _axial_tables(L)
        q_pe = _axial_rope(q_pe, [t[:, None, :] for t in tabs])
        k_pe = _axial_rope(kpe, tabs)
        P = c_ckv.shape[1]
        ckv_up = (c_ckv @ w_ukv).reshape(B, P, N_HEADS, NOPE_DIM + V_DIM)
        k_all = jnp.concatenate([k_nope, ckv_up[..., :NOPE_DIM]], axis=1)
        v_all = jnp.concatenate([v, ckv_up[..., NOPE_DIM:]], axis=1)
        kpe_all = jnp.concatenate([k_pe, c_kpe], axis=1)
        attn = _mla_attend(q_nope, q_pe, k_all, kpe_all, v_all)
        s5_y, _ = _s5_branch(ub, (s_re, s_im), *s5_params)
        new_state = None
    branch_a = attn * jax.nn.silu(za)
    branch_b = s5_y * jax.nn.silu(zb)
    merged = jax.nn.sigmoid(ga) * (branch_a @ w_pa) + jax.nn.sigmoid(gb) * (branch_b @ w_pb)
    y = gate * (merged @ w_o)
    x = _layernorm(ALPHA * x + y) * ln_g + ln_b
    return x, new_state


def setup_inputs(seed: int = 0) -> dict:
    key = jax.random.key(seed)
    ks = jax.random.split(key, 32)
    f32 = jnp.float32

    def nrm(k, shape, s):
        return jax.random.normal(k, shape, f32) * s

    n_idx = jnp.arange(S5_STATE, dtype=f32)
    return {
        "x_prompt": nrm(ks[0], (BATCH, SEQ, D_MODEL), 1.0),
        "x_sample": nrm(ks[1], (DEC_BATCH, DEC_SEQ, D_MODEL), 1.0),
        "cache_ckv": nrm(ks[2], (DEC_BATCH, DEPTH, PAST_LEN, KV_LORA), 1.0),
        "cache_kpe": nrm(ks[3], (DEC_BATCH, DEPTH, PAST_LEN, ROPE_DIM), 1.0),
        "state_s5_re": nrm(ks[4], (DEC_BATCH, DEPTH, N_DIR, S5_GROUPS, S5_STATE), 0.3),
        "state_s5_im": nrm(ks[5], (DEC_BATCH, DEPTH, N_DIR, S5_GROUPS, S5_STATE), 0.3),
        "c": nrm(ks[6], (DEC_BATCH, D_MODEL), 1.0),
        "c_ctx": nrm(ks[7], (D_MODEL,), 1.0),
        "w_ada": nrm(ks[8], (DEPTH, D_MODEL, 3 * D_MODEL), 0.5 * D_MODEL ** -0.5),
        "b_ada": nrm(ks[9], (DEPTH, 3 * D_MODEL), 0.02),
        "w_in": nrm(ks[10], (DEPTH, D_MODEL, IN_COLS), D_MODEL ** -0.5),
        "g_qn": 1.0 + nrm(ks[11], (DEPTH, Q_LORA), 0.02),
        "w_uq": nrm(ks[12], (DEPTH, Q_LORA, N_HEADS * QK_DIM), Q_LORA ** -0.5),
        "g_kvn": 1.0 + nrm(ks[13], (DEPTH, KV_LORA), 0.02),
        "w_ukv": nrm(ks[14], (DEPTH, KV_LORA, N_HEADS * (NOPE_DIM + V_DIM)), KV_LORA ** -0.5),
        "s5_a_re": -0.5 + nrm(ks[15], (DEPTH, N_DIR, S5_GROUPS, S5_STATE), 0.01),
        "s5_a_im": math.pi * n_idx + nrm(ks[16], (DEPTH, N_DIR, S5_GROUPS, S5_STATE), 0.01),
        "s5_log_dt": jax.random.uniform(ks[17], (DEPTH, N_DIR, S5_GROUPS), f32,
                                        math.log(1e-3), math.log(1e-1)),
        "s5_b_re": nrm(ks[18], (DEPTH, N_DIR, S5_GROUPS, S5_STATE, S5_GROUP_CH), (2.0 * S5_GROUP_CH) ** -0.5),
        "s5_b_im": nrm(ks[19], (DEPTH, N_DIR, S5_GROUPS, S5_STATE, S5_GROUP_CH), (2.0 * S5_GROUP_CH) ** -0.5),
        "s5_c_re": nrm(ks[20], (DEPTH, N_DIR, S5_GROUPS, S5_GROUP_CH, S5_STATE), (2.0 * S5_STATE) ** -0.5),
        "s5_c_im": nrm(ks[21], (DEPTH, N_DIR, S5_GROUPS, S5_GROUP_CH, S5_STATE), (2.0 * S5_STATE) ** -0.5),
        "s5_d": nrm(ks[22], (DEPTH, S5_W), 1.0),
        "w_glu": nrm(ks[23], (DEPTH, S5_W, S5_W), S5_W ** -0.5),
        "b_glu": nrm(ks[24], (DEPTH, S5_W), 0.02),
        "w_pa": nrm(ks[25], (DEPTH, MLA_W, D_MODEL), BETA * MLA_W ** -0.5),
        "w_pb": nrm(ks[26], (DEPTH, S5_W, D_MODEL), BETA * S5_W ** -0.5),
        "w_o": nrm(ks[27], (DEPTH, D_MODEL, D_MODEL), BETA * D_MODEL ** -0.5),
        "ln_g": 1.0 + nrm(ks[28], (DEPTH, D_MODEL), 0.02),
        "ln_b": nrm(ks[29], (DEPTH, D_MODEL), 0.02),
    }


def reference(x_prompt, x_sample, cache_ckv, cache_kpe, state_s5_re, state_s5_im, c, c_ctx,
              w_ada, b_ada, w_in, g_qn, w_uq, g_kvn, w_ukv,
              s5_a_re, s5_a_im, s5_log_dt, s5_b_re, s5_b_im, s5_c_re, s5_c_im, s5_d,
              w_glu, b_glu, w_pa, w_pb, w_o, ln_g, ln_b):
    weights = (w_ada, b_ada, w_in, g_qn, w_uq, g_kvn, w_ukv,
               s5_a_re, s5_a_im, s5_log_dt, s5_b_re, s5_b_im, s5_c_re, s5_c_im, s5_d,
               w_glu, b_glu, w_pa, w_pb, w_o, ln_g, ln_b)
    yp = x_prompt
    ys = x_sample
    ckv_l, kpe_l, sre_l, sim_l = [], [], [], []
    for l in range(DEPTH):
        lw = [w[l] for w in weights]
        yp, st = _layer(yp, c_ctx, None, *lw)
        ckv_l.append(st[0])
        kpe_l.append(st[1])
        sre_l.append(st[2])
        sim_l.append(st[3])
        ctx = (cache_ckv[:, l], cache_kpe[:, l], state_s5_re[:, l], state_s5_im[:, l])
        ys, _ = _layer(ys, c, ctx, *lw)
    new_cache_ckv = jnp.stack(ckv_l, axis=1)
    new_cache_kpe = jnp.stack(kpe_l, axis=1)
    new_state_s5_re = jnp.stack(sre_l, axis=1)
    new_state_s5_im = jnp.stack(sim_l, axis=1)
    return (yp, ys, new_cache_ckv, new_cache_kpe, new_state_s5_re, new_state_s5_im)
```

```python
import contextlib
import numpy as np
import concourse.bass as bass
import concourse.mybir as mybir
from concourse.bass_utils import run_bass_kernel_spmd

F32 = mybir.dt.float32
BF16 = mybir.dt.bfloat16
AF = mybir.ActivationFunctionType
ALU = mybir.AluOpType
AX = mybir.AxisListType

PE, ACT, DVE, POOL, SP = "tensor", "scalar", "vector", "gpsimd", "sync"
ENGS = [PE, ACT, DVE, POOL, SP]
SEM_ROLL = 12000

D = 4096
KC = 32
TO = 768
TA = 1024
NKEY = 2048
IN_COLS = 15936
C_CQ, C_CKV, C_KPE, C_ZA, C_UB, C_ZB, C_GA, C_GB = 0, 1024, 1536, 1600, 3648, 5696, 7744, 11840
ALPHA = 2.0 ** 0.25
EPS = 1e-6
DEBUG = False
import os
STAGE = int(os.environ.get('KSTAGE', '99'))


_NC_CACHE = {}


class _Done(Exception):
    pass


class Res:
    __slots__ = ("name", "w", "r", "wl")

    def __init__(self, name=""):
        self.name = name
        self.w = None
        self.r = {}
        self.wl = {}


class Prog:
    def __init__(self, nc, n_dma_sems=24):
        self.nc = nc
        self.ops = {e: [] for e in ENGS}
        self.sem_handles = {}
        self.cur = {}
        self.prev = {}
        self.sem_n = {e: 0 for e in ENGS}
        self.seen = {e: {} for e in ENGS}
        for e in [PE, ACT, DVE, POOL]:
            self._new_counter(e)
        self.n_dma_sems = n_dma_sems
        self.dma_sems = []
        self.dma_pool = {}
        for q in [SP, POOL, ACT]:
            self.dma_pool[q] = []
            for i in range(n_dma_sems if q == SP else 8):
                k = ("dma", q, i)
                self.sem_handles[k] = None
                ent = [k, 0]
                self.dma_sems.append(ent)
                self.dma_pool[q].append(ent)
        self.dma_q = {SP: 0, POOL: 0, ACT: 0}
        self.dma_i = 0
        self.slot_epoch = {}

    def _new_counter(self, e):
        k = ("cnt", e, self.sem_n[e])
        self.sem_n[e] += 1
        self.sem_handles[k] = None
        if e in self.cur:
            self.prev[e] = tuple(self.cur[e])
        self.cur[e] = [k, 0]

    def _need(self, eng, dep, waits, allow_same=False):
        if dep is None:
            return
        k, v, src = dep
        if src == eng and not (allow_same and eng != PE):
            return
        if self.seen[eng].get(k, 0) >= v:
            return
        waits[k] = max(waits.get(k, 0), v)

    def _collect(self, eng, reads, writes, relax=False):
        waits = {}
        for r in reads:
            self._need(eng, r.w, waits, allow_same=True)
            for k_, v_ in r.wl.items():
                self._need(eng, (k_, v_, "dma"), waits)
        for w in writes:
            self._need(eng, w.w, waits, allow_same=not relax)
            for k_, v_ in w.wl.items():
                self._need(eng, (k_, v_, "dma"), waits)
            for re_, d in w.r.items():
                self._need(eng, (d[0], d[1], re_), waits, allow_same=True)
        for k, v in waits.items():
            self.seen[eng][k] = v
        return list(waits.items())

    def op(self, eng, fn, reads=(), writes=(), signal=True, relax=False):
        if eng != PE:
            signal = True
        waits = self._collect(eng, reads, writes, relax=relax)
        k, v = self.cur[eng]
        nv = v + 1
        if signal:
            self.cur[eng][1] = nv
        for r in reads:
            r.r[eng] = (k, nv)
        for w in writes:
            w.w = (k, nv, eng)
            w.r = {}
            w.wl = {}
        self.ops[eng].append((waits, fn, (k, 1) if signal else None))
        if signal and nv >= SEM_ROLL:
            self._new_counter(eng)

    def dma_slot(self, eng, fn, slot_id, reads=(), writes=()):
        waits = self._collect(eng, reads, writes)
        ep = self.slot_epoch.get(slot_id, 0)
        self.slot_epoch[slot_id] = ep + 1
        k = ("slot", slot_id, ep)
        self.sem_handles.setdefault(("slot", slot_id), None)
        if ep > 0:
            waits = list(waits) + [(("slot", slot_id, ep - 1), 16)]
            self.ops[eng].append((waits, ("clear", ("slot", slot_id)), None))
            waits = []
        self.dma_i += 1
        for r in reads:
            r.r["dma%d" % self.dma_i] = (k, 16)
        for w in writes:
            w.w = (k, 16, "dma")
            w.r = {}
        self.ops[eng].append((waits, fn, (k, 16)))

    def dma(self, eng, fn, reads=(), writes=()):
        waits = self._collect(eng, reads, writes)
        pool = self.dma_pool[eng]
        slot = pool[self.dma_q[eng] % len(pool)]
        self.dma_q[eng] += 1
        self.dma_i += 1
        k, v = slot
        if v > 0 and self.seen[eng].get(k, 0) < v:
            waits.append((k, v))
            self.seen[eng][k] = v
        nv = v + 16
        slot[1] = nv
        for r in reads:
            r.r["dma%d" % self.dma_i] = (k, nv)
        for w in writes:
            if w.w is not None and w.w[2] == "dma":
                w.wl[w.w[0]] = max(w.wl.get(w.w[0], 0), w.w[1])
            else:
                w.wl = {}
            w.w = (k, nv, "dma")
            w.r = {}
        self.ops[eng].append((waits, fn, (k, 16)))

    def wait_on(self, eng, res_list):
        waits = self._collect(eng, res_list, ())
        self.ops[eng].append((waits, None, None))

    def barrier(self):
        targets = []
        for e in [PE, ACT, DVE, POOL]:
            k, v = self.cur[e]
            if v > 0:
                targets.append((k, v))
            elif e in self.prev:
                targets.append(self.prev[e])
        for k, v in self.dma_sems:
            if v > 0:
                targets.append((k, v))
        for sid, ep in self.slot_epoch.items():
            targets.append((("slot", sid, ep - 1), 16))
        for e in ENGS:
            waits = []
            for k, v in targets:
                if k[0] == "cnt" and k[1] == e:
                    continue
                if self.seen[e].get(k, 0) < v:
                    waits.append((k, v))
                    self.seen[e][k] = v
            if waits:
                self.ops[e].append((waits, None, None))

    def emit(self):
        nc = self.nc
        with contextlib.ExitStack() as st:
            for k in self.sem_handles:
                self.sem_handles[k] = st.enter_context(nc.semaphore("s_" + "_".join(str(x) for x in k)))
            block = st.enter_context(nc.Block())
            HH = self.sem_handles

            def H(k):
                return HH[k[:2]] if k[0] == "slot" else HH[k]

            def run(engine_name):
                def body(e):
                    for waits, fn, inc in self.ops[engine_name]:
                        for k, v in waits:
                            e.wait_ge(H(k), v)
                        if fn is None:
                            continue
                        if isinstance(fn, tuple):
                            e.sem_clear(H(fn[1]))
                            continue
                        ins = fn(e)
                        if inc is not None:
                            ins.then_inc(H(inc[0]), inc[1])
                return body

            block.sync(run(SP))
            block.scalar(run(ACT))
            block.vector(run(DVE))
            block.gpsimd(run(POOL))
            block.tensor(run(PE))


class Arena:
    def __init__(self, t, nwords):
        self.t = t
        self.n = nwords
        self.top = 0
        self.marks = []

    def f32(self, shape):
        n = int(np.prod(shape[1:]))
        assert self.top + n <= self.n, ("arena overflow", self.top, n, self.n)
        v = self.t[0:shape[0], self.top:self.top + n]
        self.top += n
        return self._shape(v, shape)

    def bf16(self, shape):
        n = int(np.prod(shape[1:]))
        nw = (n + 1) // 2
        assert self.top + nw <= self.n, ("arena overflow", self.top, nw, self.n)
        v = self.t[0:shape[0], self.top:self.top + nw].bitcast(BF16)[:, 0:n]
        self.top += nw
        return self._shape(v, shape)

    @staticmethod
    def _shape(v, shape):
        if len(shape) == 2:
            return v
        if len(shape) == 3:
            return v.rearrange("p (a b) -> p a b", a=shape[1])
        if len(shape) == 4:
            return v.rearrange("p (a b c) -> p a b c", a=shape[1], b=shape[2])
        if len(shape) == 5:
            return v.rearrange("p (a b c d) -> p a b c d", a=shape[1], b=shape[2], c=shape[3])
        raise ValueError(shape)

    def hi(self, shape, dt):
        n = int(np.prod(shape[1:]))
        nw = n if dt == F32 else (n + 1) // 2
        assert self.top + nw <= self.n, ("arena overflow(hi)", self.top, nw, self.n)
        self.n -= nw
        v = self.t[0:shape[0], self.n:self.n + nw]
        if dt != F32:
            v = v.bitcast(BF16)[:, 0:n]
        return self._shape(v, shape)

    def mark(self):
        self.marks.append(self.top)

    def release(self):
        self.top = self.marks.pop()


def build_program(stage=99):
    nc = bass.Bass("TRN2", target_bir_lowering=False)
    _NC_CACHE["nc"] = nc

    def din(name, shape, dt=F32):
        return nc.dram_tensor(name, list(shape), dt, kind="ExternalInput").ap()

    def dout(name, shape, dt=F32):
        return nc.dram_tensor(name, list(shape), dt, kind="ExternalOutput").ap()

    def dscr(name, shape, dt=F32):
        return nc.dram_tensor(name, list(shape), dt, kind="Internal").ap()

    x_own = din("x_own", [TO, D])
    x_aux = din("x_aux", [TA, D])
    condT = din("condT", [128, KC, 2])
    b_fm = din("b_fm", [128, 64])
    b_gate = din("b_gate", [2, D])
    w_ada = din("w_ada", [D, 3 * D])
    w_in = din("w_in", [D, IN_COLS])
    g_kvn = din("g_kvn", [1, 512])
    g_qn_fm = din("g_qn_fm", [128, 8])
    cache_ckv = din("cache_ckv", [512, 512])
    cache_kpe = din("cache_kpe", [512, 64])

    o_ckv = dout("o_ckv", [512, 512])
    o_kpe = dout("o_kpe", [512, 64])

    rope_k = din("rope_k", [TA, 64])
    s5A = din("s5A", [128, 3, 256])
    s5B = din("s5B", [128, 2, 256, 16])
    s5C = din("s5C", [128, 2, 256, 16])
    kt_in = din("kt_in", [128, 2, 59])
    sgn_in = din("sgn_in", [128, 1])
    mask_in = din("mask_in", [128, 2, 128])
    dcol_in = din("dcol_in", [128, 128])
    h0_in = din("h0_in", [64, 2, 2, 128])
    oh_in = din("oh_in", [64, 4])
    w_uq = din("w_uq", [1024, 3072])
    w_ukv = din("w_ukv", [512, 4096])
    w_glu = din("w_glu", [2048, 2048])
    w_pa = din("w_pa", [2048, D])
    w_pb = din("w_pb", [2048, D])
    w_o = din("w_o", [D, D])
    bglu_fm = din("bglu_fm", [128, 16])
    rope_q = din("rope_q", [TO, 64])
    ln_g = din("ln_g", [1, D])
    ln_b = din("ln_b", [1, D])
    y_out = dout("y_out", [TO, D])
    o_sre = dout("o_sre", [2, 2, 128, 64])
    o_sim = dout("o_sim", [2, 2, 128, 64])
    gate_d = dscr("gate_d", [2, D])
    Uall = dscr("Uall", [224, 128, 128], BF16)
    ZA = dscr("ZA", [16, 128, TO], BF16)
    ZB = dscr("ZB", [16, 128, TO], BF16)
    GA = dscr("GA", [32, 128, TO], BF16)
    GB = dscr("GB", [32, 128, TO], BF16)

    P = Prog(nc)
    with contextlib.ExitStack() as st:
        NW = 53000
        arena_t = st.enter_context(nc.sbuf_tensor("arena", [128, NW], F32))
        A = Arena(arena_t, NW)
        psf = [st.enter_context(nc.psum_tensor("psf%d" % i, [128, 512], F32)) for i in range(6)]
        psb = [st.enter_context(nc.psum_tensor("psb%d" % i, [128, 1024], BF16)) for i in range(2)]
        r_psf = [Res("psf%d" % i) for i in range(6)]
        r_psb = [Res("psb%d" % i) for i in range(2)]
        ctr = {"mm": 0, "tr": 0, "w": 0, "ev": 0}

        def next_mm():
            i = ctr["mm"] % ctr.get("mm_n", 3)
            ctr["mm"] += 1
            return psf[i], r_psf[i]

        def next_tr():
            i = ctr["tr"] % 2
            ctr["tr"] += 1
            return psb[i], r_psb[i]

        ident_f = A.f32([128, 128]); r_ident_f = Res()
        ident_b = A.bf16([128, 128]); r_ident_b = Res()
        ones_b = A.bf16([128, 128]); r_ones = Res()
        P.op(POOL, lambda e: e.memset(ident_f, 0.0), writes=[r_ident_f], signal=False)
        P.op(POOL, lambda e: e.affine_select(out=ident_f, in_=ident_f, pattern=[[-1, 128]], compare_op=ALU.not_equal,
                                             fill=1.0, base=0, channel_multiplier=1), writes=[r_ident_f])
        P.op(DVE, lambda e: e.tensor_copy(out=ident_b, in_=ident_f), reads=[r_ident_f], writes=[r_ident_b])
        P.op(DVE, lambda e: e.memset(ones_b, 1.0), writes=[r_ones])

        NWB = 3
        wbuf = []
        r_wbuf = [Res("wb%d" % i) for i in range(NWB)]

        def load_w(W, r0, nk, c0, ncols):
            assert nk * ncols <= 8192
            i = ctr["w"] % NWB
            ctr["w"] += 1
            view = wbuf[i][:, 0:nk * ncols].rearrange("p (k c) -> p k c", k=nk)
            src = W[r0:r0 + nk * 128, c0:c0 + ncols].rearrange("(k p) c -> p k c", p=128)
            P.dma(POOL, lambda e: e.dma_start(out=view, in_=src), writes=[r_wbuf[i]])
            return view, r_wbuf[i]

        modT = A.f32([128, 64, 2]); r_mod = Res("mod")
        bfm = A.f32([128, 64]); r_bfm = Res()
        scT = A.bf16([128, KC, 2]); r_scT = Res()
        CONST_TOP = A.top
        ckvT_keys = A.bf16([128, 4, NKEY]); r_ckvT = Res("ckvT")
        kpeT_keys = A.bf16([128, NKEY]); r_kpeT = Res("kpeT")
        gkb = A.f32([128, 512]); r_gkb = Res()
        OVERLAY0 = A.top
        wck = A.bf16([128, KC, 576]); r_wck = Res()
        wbuf.extend([A.bf16([128, 8192]) for _ in range(NWB)])
        AFTER_WBUF = A.top
        A.mark()
        cT = A.f32([128, KC, 2]); r_cT = Res()
        r_gate_d = Res("gate_d")
        P.dma(SP, lambda e: e.dma_start(out=cT, in_=condT), writes=[r_cT])
        P.dma(SP, lambda e: e.dma_start(out=bfm, in_=b_fm), writes=[r_bfm])
        P.op(ACT, lambda e: e.activation(out=scT, in_=cT, func=AF.Silu), reads=[r_cT], writes=[r_scT])
        P.op(DVE, lambda e: e.tensor_scalar_add(out=bfm[:, 32:64], in0=bfm[:, 32:64], scalar1=1.0), reads=[r_bfm], writes=[r_bfm])
        for blk in range(16):
            for kh in range(2):
                wv, rw = load_w(w_ada, kh * 2048, 16, blk * 512, 512)
                for c4 in range(4):
                    ps, rps = psf[c4], r_psf[c4]
                    for k16 in range(16):
                        kc = kh * 16 + k16
                        P.op(PE, (lambda e, ps=ps, wv=wv, k16=k16, kc=kc, c4=c4: e.matmul(ps[:, 0:2], lhsT=wv[:, k16, c4 * 128:(c4 + 1) * 128],
                                                                                         rhs=scT[:, kc, :], start=(kc == 0), stop=(kc == KC - 1))),
                             reads=[rw, r_scT], writes=[rps], signal=(kc == KC - 1))
            for c4 in range(4):
                ch = blk * 4 + c4
                ps, rps = psf[c4], r_psf[c4]
                P.op(DVE, (lambda e, ps=ps, ch=ch: e.tensor_scalar(out=modT[:, ch, :], in0=ps[:, 0:2], scalar1=bfm[:, ch:ch + 1],
                                                                  scalar2=None, op0=ALU.add)),
                     reads=[rps, r_bfm], writes=[r_mod])
        P.barrier()
        A.release()

        P.dma(SP, lambda e: e.dma_start(out=gkb, in_=g_kvn.partition_broadcast(128)), writes=[r_gkb])
        for part in range(3):
            c0 = [0, 256, 512][part]
            n = [256, 256, 64][part]
            wv, rw = load_w(w_in, 0, KC, C_CKV + c0, n)
            P.op(DVE, (lambda e, wv=wv, c0=c0, n=n: e.tensor_copy(out=wck[:, :, c0:c0 + n], in_=wv)), reads=[rw], writes=[r_wck])

        evc = {"i": 0}

        def evac_copy(out, in_, reads, writes):
            evc["i"] += 1
            if evc["i"] % 2:
                P.op(ACT, lambda e: e.copy(out=out, in_=in_), reads=reads, writes=writes)
            else:
                P.op(DVE, lambda e: e.tensor_copy(out=out, in_=in_), reads=reads, writes=writes)

        def make_ln_bufs(nxn=2):
            xns = [A.bf16([128, D]) for _ in range(nxn)]
            rxn = [Res() for _ in range(nxn)]
            return dict(xt=A.f32([128, D]), r_xt=Res(), xn=[xns[k % nxn] for k in range(2)], r_xn=[rxn[k % nxn] for k in range(2)],
                        st=[A.f32([128, 64]) for _ in range(2)], r_st=[Res(), Res()], i=0)

        def ln_tile(B, xsrc, row0, hT, r_hT, t0, m):
            i = B["i"] % 2
            B["i"] += 1
            xt, r_xt, xn, r_xn, stt, r_st = B["xt"], B["r_xt"], B["xn"][i], B["r_xn"][i], B["st"][i], B["r_st"][i]
            P.dma(SP, lambda e: e.dma_start(out=xt, in_=xsrc[row0:row0 + 128, :]), writes=[r_xt])
            for c in range(8):
                P.op(DVE, (lambda e, c=c: e.bn_stats(out=stt[:, c * 6:(c + 1) * 6], in_=xt[:, c * 512:(c + 1) * 512])),
                     reads=[r_xt], writes=[r_st], relax=(c > 0))
            P.op(DVE, lambda e: e.bn_aggr(out=stt[:, 48:50], in_=stt[:, 0:48]), reads=[r_st], writes=[r_st])
            P.op(DVE, lambda e: e.tensor_scalar_add(out=stt[:, 50:51], in0=stt[:, 49:50], scalar1=EPS), reads=[r_st], writes=[r_st])
            P.op(ACT, lambda e: e.activation(out=stt[:, 51:52], in_=stt[:, 50:51], func=AF.Sqrt), reads=[r_st], writes=[r_st])
            P.op(DVE, lambda e: e.reciprocal(out=stt[:, 52:53], in_=stt[:, 51:52]), reads=[r_st], writes=[r_st])
            P.op(DVE, lambda e: e.tensor_scalar(out=stt[:, 53:54], in0=stt[:, 48:49], scalar1=stt[:, 52:53], scalar2=-1.0,
                                                op0=ALU.mult, op1=ALU.mult), reads=[r_st], writes=[r_st])
            P.op(DVE, lambda e: e.tensor_scalar(out=xn, in0=xt, scalar1=stt[:, 52:53], scalar2=stt[:, 53:54],
                                                op0=ALU.mult, op1=ALU.add), reads=[r_xt, r_st], writes=[r_xn])
            for g4 in range(4):
                pt, rpt = next_tr()
                for q in range(8):
                    kc = g4 * 8 + q
                    P.op(PE, (lambda e, pt=pt, q=q, kc=kc: e.transpose(out=pt[:, q * 128:(q + 1) * 128], in_=xn[:, kc * 128:(kc + 1) * 128],
                                                                       identity=ident_b)),
                         reads=[r_xn, r_ident_b], writes=[rpt], signal=(q == 7))
                for q in range(8):
                    kc = g4 * 8 + q
                    if True:
                        P.op(ACT, (lambda e, pt=pt, q=q, kc=kc: e.activation(out=hT[:, kc, t0:t0 + 128], in_=pt[:, q * 128:(q + 1) * 128],
                                                                             func=AF.Identity, scale=modT[:, 32 + kc, m:m + 1],
                                                                             bias=modT[:, kc, m:m + 1])),
                             reads=[rpt, r_mod], writes=[r_hT[0]], relax=(q > 0))
                    else:
                        P.op(DVE, (lambda e, pt=pt, q=q, kc=kc: e.tensor_scalar(out=hT[:, kc, t0:t0 + 128], in0=pt[:, q * 128:(q + 1) * 128],
                                                                               scalar1=modT[:, 32 + kc, m:m + 1], scalar2=modT[:, kc, m:m + 1],
                                                                               op0=ALU.mult, op1=ALU.add)),
                             reads=[rpt, r_mod], writes=[r_hT[1]])


        def make_ck_bufs():
            return dict(ck_f=[A.f32([128, 512]) for _ in range(2)], r_ckf=[Res(), Res()],
                        ck_b=[A.bf16([128, 512]) for _ in range(2)], r_ckb=[Res(), Res()],
                        kp_f=[A.f32([128, 64]) for _ in range(2)], r_kpf=[Res(), Res()],
                        kp_b=[A.bf16([128, 64]) for _ in range(2)], r_kpb=[Res(), Res()],
                        sm=[A.f32([128, 8]) for _ in range(2)], r_sm=[Res(), Res()], i=0)

        r_ockv = Res(); r_okpe = Res(); r_osre = Res(); r_osim = Res()

        def ckv_kpe_tile(B, hT, r_hT, t0, key0, out_row0, rope):
            i = B["i"] % 2
            B["i"] += 1
            ck_f, r_ckf, ck_b, r_ckb = B["ck_f"][i], B["r_ckf"][i], B["ck_b"][i], B["r_ckb"][i]
            kp_f, r_kpf, kp_b, r_kpb, s, r_s = B["kp_f"][i], B["r_kpf"][i], B["kp_b"][i], B["r_kpb"][i], B["sm"][i], B["r_sm"][i]
            ps1, r1 = (psf[3], r_psf[3]) if i == 0 else (psf[5], r_psf[5])
            ps2, r2 = psf[4], r_psf[4]
            for kc in range(KC):
                P.op(PE, (lambda e, kc=kc: e.matmul(ps1[:, 0:512], lhsT=hT[:, kc, t0:t0 + 128], rhs=wck[:, kc, 0:512],
                                                    start=(kc == 0), stop=(kc == KC - 1))),
                     reads=[*r_hT, r_wck], writes=[r1], signal=False)
                P.op(PE, (lambda e, kc=kc: e.matmul(ps2[:, 0:64], lhsT=hT[:, kc, t0:t0 + 128], rhs=wck[:, kc, 512:576],
                                                    start=(kc == 0), stop=(kc == KC - 1))),
                     reads=[*r_hT, r_wck], writes=[r2], signal=(kc == KC - 1))
            P.op(ACT, lambda e: e.activation(out=ck_b, in_=ps1[:, 0:512], func=AF.Square, accum_out=s[:, 0:1]),
                 reads=[r1], writes=[r_ckb, r_s])
            P.op(DVE, lambda e: e.tensor_scalar(out=s[:, 1:2], in0=s[:, 0:1], scalar1=1.0 / 512, scalar2=EPS, op0=ALU.mult, op1=ALU.add),
                 reads=[r_s], writes=[r_s])
            P.op(ACT, lambda e: e.activation(out=s[:, 2:3], in_=s[:, 1:2], func=AF.Sqrt), reads=[r_s], writes=[r_s])
            P.op(DVE, lambda e: e.reciprocal(out=s[:, 3:4], in_=s[:, 2:3]), reads=[r_s], writes=[r_s])
            P.op(DVE, lambda e: e.scalar_tensor_tensor(out=ck_f, in0=ps1[:, 0:512], scalar=s[:, 3:4], in1=gkb, op0=ALU.mult, op1=ALU.mult),
                 reads=[r1, r_gkb, r_s], writes=[r_ckf])
            P.op(DVE, lambda e: e.tensor_copy(out=ck_b, in_=ck_f), reads=[r_ckf], writes=[r_ckb])
            P.op(DVE, lambda e: e.tensor_copy(out=kp_f, in_=ps2[:, 0:64]), reads=[r2], writes=[r_kpf])
            if rope is None:
                P.op(DVE, lambda e: e.tensor_copy(out=kp_b, in_=kp_f), reads=[r_kpf], writes=[r_kpb])
            else:
                rope(kp_f, kp_b, r_kpf, r_kpb)
            if out_row0 is not None:
                P.dma(SP, lambda e: e.dma_start(out=o_ckv[out_row0:out_row0 + 128, :], in_=ck_f), reads=[r_ckf], writes=[r_ockv])
                P.dma(SP, lambda e: e.dma_start(out=o_kpe[out_row0:out_row0 + 128, :], in_=kp_f), reads=[r_kpf], writes=[r_okpe])
            pt, rpt = next_tr()
            for kc in range(4):
                P.op(PE, (lambda e, kc=kc: e.transpose(out=pt[:, kc * 128:(kc + 1) * 128], in_=ck_b[:, kc * 128:(kc + 1) * 128], identity=ident_b)),
                     reads=[r_ckb, r_ident_b], writes=[rpt], signal=False)
            P.op(PE, lambda e: e.transpose(out=pt[0:64, 512:640], in_=kp_b, identity=ident_b),
                 reads=[r_kpb, r_ident_b], writes=[rpt])
            P.op(ACT, lambda e: e.copy(out=ckvT_keys[:, :, key0:key0 + 128], in_=pt[:, 0:512].rearrange("p (k t) -> p k t", k=4)),
                 reads=[rpt], writes=[r_ckvT])
            P.op(ACT, lambda e: e.copy(out=kpeT_keys[0:64, key0:key0 + 128], in_=pt[0:64, 512:640]), reads=[rpt], writes=[r_kpeT])

        def make_rope(tab, r_tab, tmp, r_tmp, a):
            def rope(src, dst, r_src, r_dst):
                sv = src.rearrange("p (x h d) -> p x h d", x=2, h=2)
                dv = dst.rearrange("p (x h d) -> p x h d", x=2, h=2)
                cos = tab[:, a, 0:32].rearrange("p (x d) -> p x d", x=2)
                sin = tab[:, a, 32:64].rearrange("p (x d) -> p x d", x=2)
                x1, x2 = sv[:, :, 0, :], sv[:, :, 1, :]
                t = [tmp[:, k, :].rearrange("p (x d) -> p x d", x=2) for k in range(4)]
                P.op(DVE, lambda e: e.tensor_tensor(out=t[0], in0=x1, in1=cos, op=ALU.mult), reads=[r_src, r_tab], writes=[r_tmp])
                P.op(DVE, lambda e: e.tensor_tensor(out=t[1], in0=x2, in1=sin, op=ALU.mult), reads=[r_src, r_tab], writes=[r_tmp])
                P.op(DVE, lambda e: e.tensor_tensor(out=t[2], in0=x2, in1=cos, op=ALU.mult), reads=[r_src, r_tab], writes=[r_tmp])
                P.op(DVE, lambda e: e.tensor_tensor(out=t[3], in0=x1, in1=sin, op=ALU.mult), reads=[r_src, r_tab], writes=[r_tmp])
                P.op(DVE, lambda e: e.tensor_tensor(out=dv[:, :, 0, :], in0=t[0], in1=t[1], op=ALU.subtract), reads=[r_tmp], writes=[r_dst])
                P.op(DVE, lambda e: e.tensor_tensor(out=dv[:, :, 1, :], in0=t[2], in1=t[3], op=ALU.add), reads=[r_tmp], writes=[r_dst])
            return rope

        r_uall = Res("Uall")

        def ub_proj(hTx, r_hTx, ntok, ust, r_ust, dests):
            nj = ntok // 8
            for blk in range(8):
                wv, rw = load_w(w_in, 0, KC, C_UB + blk * 256, 256)
                u = ust[blk % 2]
                ru = r_ust[blk % 2]
                for tau in range(8):
                    ps, rps = next_mm()
                    for kc in range(KC):
                        P.op(PE, (lambda e, ps=ps, wv=wv, kc=kc, tau=tau: e.matmul(ps[0:nj, 0:256], lhsT=hTx[:, kc, tau:ntok:8], rhs=wv[:, kc, :],
                                                                                 start=(kc == 0), stop=(kc == KC - 1))),
                             reads=[rw, *r_hTx], writes=[rps], signal=(kc == KC - 1))
                    evac_copy(u[0:nj, :, tau * 16:(tau + 1) * 16], ps[0:nj, 0:256].rearrange("p (g c) -> p g c", c=16), [rps], [ru])
                for (r0, r1, d0) in dests:
                    P.dma(SP, (lambda e, u=u, r0=r0, r1=r1, d0=d0, blk=blk: e.dma_start(
                        out=Uall[d0:d0 + (r1 - r0), blk * 16:(blk + 1) * 16, :], in_=u[r0:r1, :, :])), reads=[ru], writes=[r_uall])

        def finish(extra=()):
            P.barrier()
            P.wait_on(SP, [r_ockv, r_okpe, r_gate_d, r_uall] + list(extra))
            P.wait_on(ACT, [r_ckvT, r_kpeT])
            P.emit()
            raise _Done()

        if STAGE == 0:
            finish()
        A.mark()
        lnB = make_ln_bufs()
        ckB = make_ck_bufs()
        ropek = A.f32([128, 8, 64]); r_ropek = Res()
        rtmp = A.f32([128, 4, 32]); r_rtmp = Res()
        P.dma(SP, lambda e: e.dma_start(out=ropek, in_=rope_k.rearrange("(a p) d -> p a d", p=128)), writes=[r_ropek])
        hTa = A.bf16([128, KC, 512]); r_hTa = [Res("hTa0"), Res("hTa1")]
        ustA = [A.bf16([64, 16, 128]) for _ in range(2)]; r_ustA = [Res(), Res()]
        bgb = [A.f32([2, 256]) for _ in range(2)]; r_bgb = [Res(), Res()]
        gsb = [A.f32([2, 256]) for _ in range(2)]; r_gsb = [Res(), Res()]
        for half in range(2):
            for t in range(4):
                ln_tile(lnB, x_aux, half * 512 + t * 128, hTa, r_hTa, t * 128, 1)
            if STAGE == 11:
                finish()
            for t in range(4):
                a = half * 4 + t
                ckv_kpe_tile(ckB, hTa, r_hTa, t * 128, 512 + a * 128, None, make_rope(ropek, r_ropek, rtmp, r_rtmp, a))
            if STAGE == 12:
                finish()
            ub_proj(hTa, r_hTa, 512, ustA, r_ustA, [(0, 64, 64 + half * 64)])
            if half == 0:
                for blk in range(16):
                    gi_ = blk % 2
                    cs_ = slice(blk * 256, (blk + 1) * 256)
                    P.dma(SP, (lambda e, gi_=gi_, cs_=cs_: e.dma_start(out=bgb[gi_], in_=b_gate[:, cs_])), writes=[r_bgb[gi_]])
                    wv, rw = load_w(w_ada, 0, KC, 2 * D + blk * 256, 256)
                    ps, rps = next_mm()
                    for kc in range(KC):
                        P.op(PE, (lambda e, ps=ps, wv=wv, kc=kc: e.matmul(ps[0:2, 0:256], lhsT=scT[:, kc, :], rhs=wv[:, kc, :],
                                                                         start=(kc == 0), stop=(kc == KC - 1))),
                             reads=[rw, r_scT], writes=[rps], signal=(kc == KC - 1))
                    P.op(DVE, (lambda e, ps=ps, gi_=gi_: e.tensor_tensor(out=gsb[gi_], in0=ps[0:2, 0:256], in1=bgb[gi_], op=ALU.add)),
                         reads=[rps, r_bgb[gi_]], writes=[r_gsb[gi_]])
                    P.dma(SP, (lambda e, gi_=gi_, cs_=cs_: e.dma_start(out=gate_d[:, cs_], in_=gsb[gi_])), reads=[r_gsb[gi_]], writes=[r_gate_d])
            if STAGE == 13:
                finish()
        for t in range(4):
            i = t % 2
            ck_f, r_ckf, ck_b, r_ckb = ckB["ck_f"][i], ckB["r_ckf"][i], ckB["ck_b"][i], ckB["r_ckb"][i]
            kp_f, r_kpf, kp_b, r_kpb = ckB["kp_f"][i], ckB["r_kpf"][i], ckB["kp_b"][i], ckB["r_kpb"][i]
            P.dma(SP, (lambda e, t=t, ck_f=ck_f: e.dma_start(out=ck_f, in_=cache_ckv[t * 128:(t + 1) * 128, :])), writes=[r_ckf])
            P.dma(SP, (lambda e, t=t, kp_f=kp_f: e.dma_start(out=kp_f, in_=cache_kpe[t * 128:(t + 1) * 128, :])), writes=[r_kpf])
            P.op(DVE, (lambda e, ck_f=ck_f, ck_b=ck_b: e.tensor_copy(out=ck_b, in_=ck_f)), reads=[r_ckf], writes=[r_ckb])
            P.op(DVE, (lambda e, kp_f=kp_f, kp_b=kp_b: e.tensor_copy(out=kp_b, in_=kp_f)), reads=[r_kpf], writes=[r_kpb])
            pt, rpt = next_tr()
            for kc in range(4):
                P.op(PE, (lambda e, kc=kc, pt=pt, ck_b=ck_b: e.transpose(out=pt[:, kc * 128:(kc + 1) * 128], in_=ck_b[:, kc * 128:(kc + 1) * 128],
                                                                        identity=ident_b)),
                     reads=[r_ckb, r_ident_b], writes=[rpt], signal=False)
            P.op(PE, (lambda e, pt=pt, kp_b=kp_b: e.transpose(out=pt[0:64, 512:640], in_=kp_b, identity=ident_b)),
                 reads=[r_kpb, r_ident_b], writes=[rpt])
            key0 = 1536 + t * 128
            P.op(ACT, (lambda e, pt=pt, key0=key0: e.copy(out=ckvT_keys[:, :, key0:key0 + 128],
                                                         in_=pt[:, 0:512].rearrange("p (k t) -> p k t", k=4))), reads=[rpt], writes=[r_ckvT])
            P.op(ACT, (lambda e, pt=pt, key0=key0: e.copy(out=kpeT_keys[0:64, key0:key0 + 128], in_=pt[0:64, 512:640])), reads=[rpt], writes=[r_kpeT])
        if STAGE == 1:
            finish()
        P.barrier()
        A.release()

        cqnT = A.hi([128, 8, TO], BF16); r_cqn = Res("cqn")
        A.mark()
        hT = A.bf16([128, KC, TO]); r_hT = [Res("hT0"), Res("hT1")]
        A.mark()
        lnB = make_ln_bufs(1)
        ckB = make_ck_bufs()
        for t in range(6):
            ln_tile(lnB, x_own, t * 128, hT, r_hT, t * 128, 0 if t < 4 else 1)
        for t in range(4):
            ckv_kpe_tile(ckB, hT, r_hT, t * 128, t * 128, t * 128, None)
        P.barrier()
        A.release()

        if STAGE == 2:
            finish()
        SEGS = [(0, 512), (512, 256)]

        def linear_fm(W, c0, nchunks, xT, r_xT, nk, evac):
            per = max(1, min(2, 8192 // (nk * 128) // 1))
            per = 2 if nk * 256 <= 8192 else 1
            ch = 0
            while ch < nchunks:
                nb = min(per, nchunks - ch)
                wv, rw = load_w(W, 0, nk, c0 + ch * 128, nb * 128)
                for c2 in range(nb):
                    for (n0, n) in SEGS:
                        ps, rps = next_mm()
                        for kc in range(nk):
                            P.op(PE, (lambda e, ps=ps, wv=wv, kc=kc, c2=c2, n0=n0, n=n: e.matmul(
                                ps[:, 0:n], lhsT=wv[:, kc, c2 * 128:(c2 + 1) * 128], rhs=xT[:, kc, n0:n0 + n],
                                start=(kc == 0), stop=(kc == nk - 1))),
                                reads=[rw, *r_xT], writes=[rps], signal=(kc == nk - 1))
                        evac(ch + c2, n0, n, ps, rps)
                ch += nb

        A.mark()
        sq = A.bf16([128, 8, TO]); r_sq = Res()
        rstd_b = A.f32([128, TO]); r_rstd = Res()
        gq = A.f32([128, 8]); r_gq = Res()
        stage = [A.bf16([128, TO]) for _ in range(3)]; r_stage = [Res() for _ in range(3)]
        ustO = [A.bf16([96, 16, 128]) for _ in range(2)]; r_ustO = [Res(), Res()]
        P.dma(SP, lambda e: e.dma_start(out=gq, in_=g_qn_fm), writes=[r_gq])

        def evac_cq(ch, n0, n, ps, rps):
            P.op(ACT, lambda e: e.copy(out=cqnT[:, ch, n0:n0 + n], in_=ps[:, 0:n]), reads=[rps], writes=[r_cqn])
            P.op(ACT, lambda e: e.activation(out=sq[:, ch, n0:n0 + n], in_=ps[:, 0:n], func=AF.Square), reads=[rps], writes=[r_sq])

        linear_fm(w_in, C_CQ, 8, hT, r_hT, KC, evac_cq)
        for (n0, n) in SEGS:
            ps, rps = next_mm()
            for ch in range(8):
                P.op(PE, (lambda e, ps=ps, ch=ch, n0=n0, n=n: e.matmul(ps[:, 0:n], lhsT=ones_b, rhs=sq[:, ch, n0:n0 + n],
                                                                      start=(ch == 0), stop=(ch == 7))),
                     reads=[r_ones, r_sq], writes=[rps], signal=(ch == 7))
            P.op(DVE, (lambda e, ps=ps, n0=n0, n=n: e.tensor_scalar(out=rstd_b[:, n0:n0 + n], in0=ps[:, 0:n], scalar1=1.0 / 1024, scalar2=EPS,
                                                                   op0=ALU.mult, op1=ALU.add)), reads=[rps], writes=[r_rstd])
        P.op(ACT, lambda e: e.activation(out=rstd_b, in_=rstd_b, func=AF.Sqrt), reads=[r_rstd], writes=[r_rstd])
        P.op(DVE, lambda e: e.reciprocal(out=rstd_b, in_=rstd_b), reads=[r_rstd], writes=[r_rstd])
        for ch in range(8):
            P.op(DVE, (lambda e, ch=ch: e.scalar_tensor_tensor(out=cqnT[:, ch, :], in0=cqnT[:, ch, :], scalar=gq[:, ch:ch + 1], in1=rstd_b,
                                                              op0=ALU.mult, op1=ALU.mult)), reads=[r_cqn, r_gq, r_rstd], writes=[r_cqn])

        if STAGE == 3:
            finish()
        stc = {"i": 0}

        def make_evac_act(func, dst, r_dst):
            def evac(ch, n0, n, ps, rps):
                i = stc["i"] % 3
                sg, rsg = stage[i], r_stage[i]
                P.op(ACT, lambda e: e.activation(out=sg[:, n0:n0 + n], in_=ps[:, 0:n], func=func), reads=[rps], writes=[rsg])
                if n0 + n == TO:
                    P.dma(SP, lambda e: e.dma_start(out=dst[ch], in_=sg), reads=[rsg], writes=[r_dst])
                    stc["i"] += 1
            return evac

        r_za, r_zb, r_ga, r_gb = Res("ZA"), Res("ZB"), Res("GA"), Res("GB")
        linear_fm(w_in, C_ZA, 16, hT, r_hT, KC, make_evac_act(AF.Silu, ZA, r_za))
        if STAGE == 4:
            finish([r_za])
        ub_proj(hT, r_hT, TO, ustO, r_ustO, [(0, 64, 0), (64, 96, 192)])
        if STAGE == 5:
            finish([r_za])
        linear_fm(w_in, C_ZB, 16, hT, r_hT, KC, make_evac_act(AF.Silu, ZB, r_zb))
        linear_fm(w_in, C_GA, 32, hT, r_hT, KC, make_evac_act(AF.Sigmoid, GA, r_ga))
        linear_fm(w_in, C_GB, 32, hT, r_hT, KC, make_evac_act(AF.Sigmoid, GB, r_gb))
        P.barrier()
        A.release()
        A.release()

        P.barrier()
        SAVE_TOP = A.top
        A.top = OVERLAY0
        s5yT = A.hi([128, 16, TO], BF16); r_s5y = Res("s5y")
        I32 = mybir.dt.int32
        NK = 59
        G8 = 8
        TWO_PI = 6.283185307179586

        def OP(eng, meth, reads, writes, **kw):
            P.op(eng, (lambda e, meth=meth, kw=kw: getattr(e, meth)(**kw)), reads=reads, writes=writes)

        def TT(eng, out, in0, in1, op, reads, writes):
            OP(eng, "tensor_tensor", reads, writes, out=out, in0=in0, in1=in1, op=op)

        def TS(eng, out, in0, s1, s2, op0, op1, reads, writes):
            if s2 is None:
                OP(eng, "tensor_scalar", reads, writes, out=out, in0=in0, scalar1=s1, scalar2=None, op0=op0)
            else:
                OP(eng, "tensor_scalar", reads, writes, out=out, in0=in0, scalar1=s1, scalar2=s2, op0=op0, op1=op1)

        def bc(ap, axis, shape):
            return ap.unsqueeze(axis).broadcast_to(shape)

        kt_sb = A.f32([128, 2, NK]); sgn = A.f32([128, 1]); maskM = A.f32([128, 2, 128]); dcol = A.f32([128, 128])
        h0_sb = A.f32([64, 2, 2, 128]); oh_sb = A.f32([64, 4]); FIN = A.f32([64, 2, 2, 2, 128])
        r_c5 = Res("c5"); r_fin = Res("fin")
        for dst, src in [(kt_sb, kt_in), (sgn, sgn_in), (maskM, mask_in), (dcol, dcol_in), (h0_sb, h0_in), (oh_sb, oh_in)]:
            P.dma(SP, (lambda e, dst=dst, src=src: e.dma_start(out=dst, in_=src)), writes=[r_c5])
        A3 = A.f32([128, 3, 2, G8]); B12 = A.f32([128, 2, 2, G8 * 16]); C12 = A.f32([128, 2, 2, G8 * 16]); r_par = Res("par")
        T = [A.f32([128, 16, NK]) for _ in range(5)]
        TI = A.f32([128, 16, NK]).bitcast(I32)
        PR = A.f32([128, 16, NK]); PI = A.f32([128, 16, NK])
        PRSq = A.f32([128, 16, 16]); PISp = A.f32([128, 16, 8])
        sm5 = A.f32([128, 12, 16])
        BB = A.f32([128, 16, 16]); BBs = A.f32([128, 16, 16])
        r_tab = Res("tab")
        Pt = A.f32([128, G8, 128]); Qt = A.f32([128, G8, 128]); Ft = A.f32([128, G8, 128]); F2t = A.f32([128, G8, 128])
        t1 = A.f32([128, G8, 128]); t2 = A.f32([128, G8, 128]); r_pq = Res("pq")
        Mb = A.bf16([128, 2, G8, 128]); Eb = A.bf16([128, 2, G8, 128]); Fb = A.bf16([128, 2, G8, 128]); F2b = A.bf16([128, 2, G8, 128])
        r_M = Res("M"); r_E = Res("E"); r_F = Res("F")
        Mtmp = A.f32([128, 4, 128]); r_mtmp = Res()
        U1 = A.bf16([128, 8, 128]); U2 = A.bf16([128, 8, 128]); r_U1 = Res(); r_U2 = Res()
        Ug = A.bf16([128, G8, 224]); r_Ug = Res("Ug")
        Sloc = [A.f32([64, 2, G8, 7, 32]) for _ in range(2)]; r_S = [Res("S0"), Res("S1")]
        stA = [A.f32([64, 2, G8, 7]) for _ in range(2)]; stB = [A.f32([64, 2, G8, 7]) for _ in range(2)]
        XE = [A.f32([64, 4, 2, G8]) for _ in range(2)]; Xin = [A.f32([64, 2, G8]) for _ in range(2)]
        ct1 = [A.f32([64, 2, G8]) for _ in range(2)]; ct2 = [A.f32([64, 2, G8]) for _ in range(2)]
        W1 = [A.f32([64, 2, G8, 32]) for _ in range(2)]; W2 = [A.f32([64, 2, G8, 32]) for _ in range(2)]
        Xp = A.bf16([64, 2, 2, G8, 96]); r_Xp = [Res("Xp0"), Res("Xp1")]
        Ysb = A.f32([128, G8, 96]); r_Ysb = Res()
        YTM = A.f32([96, 8, 128]); r_YTM = Res()
        SE = [DVE, POOL]

        for gb in range(16):
            g0 = gb * G8
            for d in range(2):
                c0 = d * 128 + g0
                P.dma(SP, (lambda e, d=d, c0=c0: e.dma_start(out=A3[:, :, d, :], in_=s5A[:, :, c0:c0 + G8])), writes=[r_par])
                for w in range(2):
                    P.dma(SP, (lambda e, d=d, w=w, c0=c0: e.dma_start(out=B12[:, w, d, :].rearrange("p (g c) -> p g c", c=16),
                                                                     in_=s5B[:, w, c0:c0 + G8, :])), writes=[r_par])
                    P.dma(SP, (lambda e, d=d, w=w, c0=c0: e.dma_start(out=C12[:, w, d, :].rearrange("p (g c) -> p g c", c=16),
                                                                     in_=s5C[:, w, c0:c0 + G8, :])), writes=[r_par])
            RT = [r_par, r_c5, r_tab]
            WT = [r_tab]
            are = A3[:, 0].rearrange("p d g -> p (d g)"); aim = A3[:, 1].rearrange("p d g -> p (d g)"); ldt = A3[:, 2].rearrange("p d g -> p (d g)")
            B1 = B12[:, 0].rearrange("p d (g c) -> p (d g) c", c=16); B2 = B12[:, 1].rearrange("p d (g c) -> p (d g) c", c=16)
            C1 = C12[:, 0].rearrange("p d (g c) -> p (d g) c", c=16); C2 = C12[:, 1].rearrange("p d (g c) -> p (d g) c", c=16)
            dt_, dar, dai = sm5[:, 0], sm5[:, 1], sm5[:, 2]
            OP(ACT, "activation", RT, WT, out=dt_, in_=ldt, func=AF.Exp)
            TT(DVE, dar, dt_, are, ALU.mult, RT, WT)
            TT(DVE, dai, dt_, aim, ALU.mult, RT, WT)
            ktb = kt_sb.unsqueeze(2).broadcast_to([128, 2, G8, NK])
            v4 = lambda x: x.rearrange("p (d g) k -> p d g k", d=2)
            c4 = lambda x: x.rearrange("p (d g) -> p d g", d=2).unsqueeze(3).broadcast_to([128, 2, G8, NK])
            TT(DVE, v4(T[0]), c4(dai), ktb, ALU.mult, RT, WT)
            TT(DVE, v4(T[1]), c4(dar), ktb, ALU.mult, RT, WT)
            OP(ACT, "activation", RT, WT, out=T[1], in_=T[1], func=AF.Exp)
            for (dstT, off) in [(PI, 64.0), (PR, 64.25)]:
                TS(DVE, T[2], T[0], 1.0 / TWO_PI, off, ALU.mult, ALU.add, RT, WT)
                OP(DVE, "tensor_copy", RT, WT, out=TI, in_=T[2])
                OP(DVE, "tensor_copy", RT, WT, out=T[3], in_=TI)
                TT(DVE, T[2], T[2], T[3], ALU.subtract, RT, WT)
                TS(DVE, T[3], T[2], 0.5, None, ALU.is_gt, None, RT, WT)
                TT(DVE, T[2], T[2], T[3], ALU.subtract, RT, WT)
                OP(ACT, "activation", RT, WT, out=T[4], in_=T[2], func=AF.Sin, scale=6.28318)
                TT(DVE, dstT, T[1], T[4], ALU.mult, RT, WT)
            TS(DVE, PRSq, PR[:, :, 8:24], sgn[:, 0:1], None, ALU.mult, None, RT, WT)
            TS(DVE, PISp, PI[:, :, 0:8], sgn[:, 0:1], None, ALU.mult, None, RT, WT)
            abr, abi = PR[:, :, 24], PI[:, :, 24]
            den, pre, qre, qim, u1, u2 = sm5[:, 3], sm5[:, 4], sm5[:, 5], sm5[:, 6], sm5[:, 7], sm5[:, 8]
            TT(DVE, den, are, are, ALU.mult, RT, WT)
            TT(DVE, u1, aim, aim, ALU.mult, RT, WT)
            TT(DVE, den, den, u1, ALU.add, RT, WT)
            OP(DVE, "reciprocal", RT, WT, out=den, in_=den)
            TS(DVE, pre, abr, -1.0, None, ALU.add, None, RT, WT)
            TT(DVE, u1, pre, are, ALU.mult, RT, WT)
            TT(DVE, u2, abi, aim, ALU.mult, RT, WT)
            TT(DVE, u1, u1, u2, ALU.add, RT, WT)
            TT(DVE, qre, u1, den, ALU.mult, RT, WT)
            TT(DVE, u1, abi, are, ALU.mult, RT, WT)
            TT(DVE, u2, pre, aim, ALU.mult, RT, WT)
            TT(DVE, u1, u1, u2, ALU.subtract, RT, WT)
            TT(DVE, qim, u1, den, ALU.mult, RT, WT)
            TS(DVE, qim, qim, sgn[:, 0:1], None, ALU.mult, None, RT, WT)
            qreb = qre.unsqueeze(2).broadcast_to([128, 16, 16]); qimb = qim.unsqueeze(2).broadcast_to([128, 16, 16])
            x1 = T[2][:, :, 0:16]; x2 = T[2][:, :, 16:32]
            TT(DVE, x1, qreb, B1, ALU.mult, RT, WT)
            TT(DVE, x2, qimb, B2, ALU.mult, RT, WT)
            TT(DVE, BB, x1, x2, ALU.add, RT, WT)
            TT(DVE, x1, qreb, B2, ALU.mult, RT, WT)
            TT(DVE, x2, qimb, B1, ALU.mult, RT, WT)
            TT(DVE, BBs, x1, x2, ALU.subtract, RT, WT)

            for d in range(2):
                dsl = slice(d * G8, (d + 1) * G8)
                kb = lambda tab, k0: tab[:, dsl, k0:k0 + 8].unsqueeze(3).broadcast_to([128, G8, 8, 16])
                cb = lambda tab: tab[:, dsl, :].unsqueeze(2).broadcast_to([128, G8, 8, 16])
                v = lambda x: x.rearrange("p g (s c) -> p g s c", c=16)
                RP = [r_tab, r_pq, r_par]
                WP = [r_pq]
                TT(DVE, v(t1), kb(PR, 0), cb(BB), ALU.mult, RP, WP)
                TT(DVE, v(t2), kb(PISp, 0), cb(BBs), ALU.mult, RP, WP)
                TT(DVE, Pt, t1, t2, ALU.add, RP, WP)
                TT(DVE, v(t1), kb(PRSq, 0), cb(C1), ALU.mult, RP, WP)
                TT(DVE, v(t2), kb(PI, 8), cb(C2), ALU.mult, RP, WP)
                OP(DVE, "scalar_tensor_tensor", RP, WP, out=Qt, in0=t1, scalar=-1.0, in1=t2, op0=ALU.mult, op1=ALU.subtract)
                TT(DVE, v(t1), kb(PRSq, 8), cb(C1), ALU.mult, RP, WP)
                TT(DVE, v(t2), kb(PI, 16), cb(C2), ALU.mult, RP, WP)
                OP(DVE, "scalar_tensor_tensor", RP, WP, out=Ft, in0=t1, scalar=-1.0, in1=t2, op0=ALU.mult, op1=ALU.subtract)
                TT(DVE, v(t1), kb(PRSq, 8), cb(C2), ALU.mult, RP, WP)
                TT(DVE, v(t2), kb(PI, 16), cb(C1), ALU.mult, RP, WP)
                TT(DVE, F2t, t1, t2, ALU.subtract, RP, WP)
                OP(ACT, "copy", [r_pq], [r_F], out=Fb[0:64, d], in_=Ft[0:64])
                OP(ACT, "copy", [r_pq], [r_F], out=F2b[0:64, d], in_=F2t[0:64])
                for h4 in range(2):
                    ps, rps = next_mm()
                    for i4 in range(4):
                        i = h4 * 4 + i4
                        P.op(PE, (lambda e, ps=ps, i=i, i4=i4: e.matmul(ps[:, i4 * 128:(i4 + 1) * 128], lhsT=Pt[:, i, :], rhs=Qt[:, i, :],
                                                                       start=True, stop=True)),
                             reads=[r_pq], writes=[rps], signal=(i4 == 3))
                    TT(DVE, Mtmp, ps[:, 0:512].rearrange("p (i m) -> p i m", i=4), maskM[:, d, :].unsqueeze(1).broadcast_to([128, 4, 128]),
                       ALU.mult, [rps, r_c5], [r_mtmp])
                    for i4 in range(4):
                        i = h4 * 4 + i4
                        if d == 0:
                            OP(DVE, "scalar_tensor_tensor", [r_mtmp, r_c5, r_ident_f], [r_M], out=Mb[:, d, i, :], in0=ident_f,
                               scalar=dcol[:, g0 + i:g0 + i + 1], in1=Mtmp[:, i4, :], op0=ALU.mult, op1=ALU.add)
                        elif i4 == 0:
                            OP(ACT, "copy", [r_mtmp], [r_M], out=Mb[:, d, h4 * 4:(h4 + 1) * 4, :], in_=Mtmp)
                    ps, rps = next_mm()
                    for i4 in range(4):
                        i = h4 * 4 + i4
                        P.op(PE, (lambda e, ps=ps, i=i, i4=i4: e.transpose(out=ps[:, i4 * 128:(i4 + 1) * 128], in_=Pt[:, i, :], identity=ident_f)),
                             reads=[r_pq, r_ident_f], writes=[rps], signal=(i4 == 3))
                    OP(ACT, "copy", [rps], [r_E], out=Eb[:, d, h4 * 4:(h4 + 1) * 4, :], in_=ps[:, 0:512].rearrange("p (i m) -> p i m", i=4))

            P.dma(SP, (lambda e, g0=g0: e.dma_start(out=U1, in_=Uall[0:128, g0:g0 + G8, :])), reads=[r_uall], writes=[r_U1])
            P.dma(SP, (lambda e, g0=g0: e.dma_start(out=U2[0:96], in_=Uall[128:224, g0:g0 + G8, :])), reads=[r_uall], writes=[r_U2])
            for h4 in range(2):
                pt, rpt = next_tr()
                for i4 in range(4):
                    i = h4 * 4 + i4
                    P.op(PE, (lambda e, pt=pt, i=i, i4=i4: e.transpose(out=pt[:, i4 * 224:i4 * 224 + 128], in_=U1[:, i, :], identity=ident_b)),
                         reads=[r_U1, r_ident_b], writes=[rpt], signal=False)
                    P.op(PE, (lambda e, pt=pt, i=i, i4=i4: e.transpose(out=pt[:, i4 * 224 + 128:i4 * 224 + 224], in_=U2[0:96, i, :],
                                                                      identity=ident_b[0:96, 0:96])),
                         reads=[r_U2, r_ident_b], writes=[rpt], signal=(i4 == 3))
                OP(ACT, "copy", [rpt], [r_Ug], out=Ug[:, h4 * 4:(h4 + 1) * 4, :], in_=pt[:, 0:896].rearrange("p (i j) -> p i j", i=4))

            for d in range(2):
                for i in range(G8):
                    ps, rps = next_mm()
                    P.op(PE, (lambda e, ps=ps, d=d, i=i: e.matmul(ps[0:64, 0:224], lhsT=Eb[:, d, i, 0:64], rhs=Ug[:, i, :], start=True, stop=True)),
                         reads=[r_E, r_Ug], writes=[rps], signal=False)
                    P.op(PE, (lambda e, ps=ps, d=d, i=i: e.matmul(ps[0:64, 224:448], lhsT=Eb[:, d, i, 64:128], rhs=Ug[:, i, :], start=True, stop=True)),
                         reads=[r_E, r_Ug], writes=[rps], signal=True)
                    OP(ACT, "copy", [rps], [r_S[d]], out=Sloc[d][:, :, i].rearrange("p r s j -> p r (s j)"),
                       in_=ps[0:64, 0:448].rearrange("p (r j) -> p r j", r=2))

            for d in range(2):
                E_ = SE[d]
                S = Sloc[d]
                RS = [r_S[d], r_tab, r_c5]
                WS = [r_S[d]]
                dsl = slice(d * G8, (d + 1) * G8)
                a8r = PR[0:64, dsl, 25]; a8i = PI[0:64, dsl, 25]
                a8rb = a8r.unsqueeze(1).unsqueeze(3).broadcast_to([64, 2, G8, 7])
                a8ib = a8i.unsqueeze(1).unsqueeze(3).broadcast_to([64, 2, G8, 7])
                tA, tB = stA[d], stB[d]
                OP(E_, "tensor_copy", RS, WS, out=ct2[d][:, 0], in_=a8i)
                TS(E_, ct2[d][:, 1], a8i, -1.0, None, ALU.mult, None, RS, WS)
                a8isb = ct2[d].unsqueeze(3).broadcast_to([64, 2, G8, 7])
                steps = range(1, 32) if d == 0 else range(30, -1, -1)
                for j in steps:
                    jp = j - 1 if d == 0 else j + 1
                    Xprev = S[:, :, :, :, jp]
                    Xc = S[:, :, :, :, j]
                    TT(E_, tA, a8rb, Xprev, ALU.mult, RS, WS)
                    TT(E_, tB, a8isb, Xprev, ALU.mult, RS, WS)
                    TT(E_, Xc, Xc, tA, ALU.add, RS, WS)
                    TT(E_, Xc, Xc, tB[:, ::-1], ALU.add, RS, WS)
                jl = 31 if d == 0 else 0
                a256r = PR[0:64, dsl, 26].unsqueeze(1).broadcast_to([64, 2, G8]); a256i = PI[0:64, dsl, 26].unsqueeze(1).broadcast_to([64, 2, G8])
                OP(E_, "tensor_copy", RS, WS, out=XE[d][:, 0], in_=h0_sb[:, d, :, g0:g0 + G8])
                for k in range(3):
                    qq = k if d == 0 else 3 - k
                    TT(E_, ct1[d], a256r, XE[d][:, k], ALU.mult, RS, WS)
                    TT(E_, ct2[d], a256i, XE[d][:, k], ALU.mult, RS, WS)
                    TT(E_, XE[d][:, k + 1], ct1[d], S[:, :, :, 2 + qq, jl], ALU.add, RS, WS)
                    TT(E_, XE[d][:, k + 1, 0], XE[d][:, k + 1, 0], ct2[d][:, 1], ALU.subtract, RS, WS)
                    TT(E_, XE[d][:, k + 1, 1], XE[d][:, k + 1, 1], ct2[d][:, 0], ALU.add, RS, WS)
                for q in range(4):
                    k = q if d == 0 else 3 - q
                    if q == 0:
                        TS(E_, Xin[d], XE[d][:, k], oh_sb[:, q:q + 1], None, ALU.mult, None, RS, WS)
                    else:
                        TS(E_, ct1[d], XE[d][:, k], oh_sb[:, q:q + 1], None, ALU.mult, None, RS, WS)
                        TT(E_, Xin[d], Xin[d], ct1[d], ALU.add, RS, WS)
                prj = PR[0:64, dsl, 27:59].unsqueeze(1).broadcast_to([64, 2, G8, 32]); pij = PI[0:64, dsl, 27:59].unsqueeze(1).broadcast_to([64, 2, G8, 32])
                xib = Xin[d].unsqueeze(3).broadcast_to([64, 2, G8, 32])
                So = S[:, :, :, 6, :]
                TT(E_, W1[d], prj, xib, ALU.mult, RS, WS)
                TT(E_, W2[d], pij, xib, ALU.mult, RS, WS)
                TT(E_, So, So, W1[d], ALU.add, RS, WS)
                TT(E_, So[:, 0], So[:, 0], W2[d][:, 1], ALU.subtract, RS, WS)
                TT(E_, So[:, 1], So[:, 1], W2[d][:, 0], ALU.add, RS, WS)
                RX = RS + [r_Xp[d]]
                WX = [r_Xp[d], r_S[d]]
                xp = Xp[:, d]
                P.op(E_, (lambda e, xp=xp: e.memset(xp, 0.0)), reads=RX, writes=WX)
                for r in range(2):
                    src_p = S[:, r, :, 0:2, 0:31] if d == 0 else S[:, r, :, 0:2, 1:32]
                    dst_p = xp[:, r, :, 0:64].rearrange("p g (s j) -> p g s j", s=2)
                    dst_p = dst_p[:, :, :, 1:32] if d == 0 else dst_p[:, :, :, 0:31]
                    OP(E_, "tensor_copy", RX, WX, out=dst_p, in_=src_p)
                src_o = S[:, :, :, 6, 0:31] if d == 0 else S[:, :, :, 6, 1:32]
                dst_o = xp[:, :, :, 65:96] if d == 0 else xp[:, :, :, 64:95]
                OP(E_, "tensor_copy", RX, WX, out=dst_o, in_=src_o)
                jo = 64 if d == 0 else 95
                OP(E_, "tensor_copy", RX, WX, out=xp[:, :, :, jo], in_=Xin[d])
                OP(E_, "tensor_copy", RX + [r_fin], WX + [r_fin], out=FIN[:, :, d, :, g0:g0 + G8],
                   in_=S[:, :, :, 0:2, jl].rearrange("p r g s -> p s r g"))

            for h4 in range(2):
                ps, rps = next_mm()
                for i4 in range(4):
                    i = h4 * 4 + i4
                    for (cs, ce, os_) in [(0, 64, 0), (192, 224, 64)]:
                        n = ce - cs
                        oc = i4 * 96 + os_
                        for d in range(2):
                            last = (d == 1)
                            P.op(PE, (lambda e, ps=ps, d=d, i=i, oc=oc, n=n, cs=cs, ce=ce: e.matmul(ps[:, oc:oc + n], lhsT=Mb[:, d, i, :], rhs=Ug[:, i, cs:ce],
                                                                                                     start=(d == 0), stop=False)),
                                 reads=[r_M, r_Ug], writes=[rps], signal=False)
                            P.op(PE, (lambda e, ps=ps, d=d, i=i, oc=oc, n=n, os_=os_: e.matmul(ps[:, oc:oc + n], lhsT=Fb[0:64, d, i, :],
                                                                                              rhs=Xp[0:64, d, 0, i, os_:os_ + n], start=False, stop=False)),
                                 reads=[r_F, r_Xp[d]], writes=[rps], signal=False)
                            P.op(PE, (lambda e, ps=ps, d=d, i=i, oc=oc, n=n, os_=os_, last=last: e.matmul(ps[:, oc:oc + n], lhsT=F2b[0:64, d, i, :],
                                                                                                          rhs=Xp[0:64, d, 1, i, os_:os_ + n], start=False, stop=last)),
                                 reads=[r_F, r_Xp[d]], writes=[rps], signal=(last and i4 == 3 and os_ == 64))
                OP(ACT, "copy", [rps], [r_Ysb], out=Ysb[:, h4 * 4:(h4 + 1) * 4, :], in_=ps[:, 0:384].rearrange("p (i j) -> p i j", i=4))
                ps, rps = next_mm()
                for i4 in range(4):
                    i = h4 * 4 + i4
                    P.op(PE, (lambda e, ps=ps, i=i, i4=i4: e.transpose(out=ps[0:96, i4 * 128:(i4 + 1) * 128], in_=Ysb[:, i, :], identity=ident_f)),
                         reads=[r_Ysb, r_ident_f], writes=[rps], signal=(i4 == 3))
                OP(ACT, "copy", [rps], [r_YTM], out=YTM[0:96, :, h4 * 64:(h4 + 1) * 64].rearrange("p t (i c) -> p i t c", c=16),
                   in_=ps[0:96, 0:512].rearrange("p (i t c) -> p i t c", i=4, c=16))
            for h in range(2):
                ps, rps = next_mm()
                for t4 in range(4):
                    tau = h * 4 + t4
                    P.op(PE, (lambda e, ps=ps, tau=tau, t4=t4: e.transpose(out=ps[:, t4 * 96:(t4 + 1) * 96], in_=YTM[0:96, tau, :], identity=ident_f[0:96, 0:96])),
                         reads=[r_YTM, r_ident_f], writes=[rps], signal=(t4 == 3))
                OP(ACT, "copy", [rps], [r_s5y], out=s5yT[:, gb, :].rearrange("p (r t) -> p t r", t=8)[:, h * 4:(h + 1) * 4, :],
                   in_=ps[:, 0:384].rearrange("p (t r) -> p t r", t=4))

        for sqi in range(2):
            for d in range(2):
                ps, rps = next_mm()
                for r in range(2):
                    P.op(PE, (lambda e, ps=ps, sq=sqi, d=d, r=r: e.transpose(out=ps[:, r * 64:(r + 1) * 64], in_=FIN[0:64, sq, d, r, :], identity=ident_f[0:64, 0:64])),
                         reads=[r_fin, r_ident_f], writes=[rps], signal=(r == 1))
                fo = t1[:, 0, :]
                OP(ACT, "copy", [rps, r_pq], [r_pq], out=fo, in_=ps[:, 0:128])
                P.dma(SP, (lambda e, sq=sqi, d=d, fo=fo: e.dma_start(out=o_sre[sq, d], in_=fo[:, 0:64])), reads=[r_pq], writes=[r_osre])
                P.dma(SP, (lambda e, sq=sqi, d=d, fo=fo: e.dma_start(out=o_sim[sq, d], in_=fo[:, 64:128])), reads=[r_pq], writes=[r_osim])
        P.barrier()
        A.top = SAVE_TOP


        A.top = OVERLAY0
        wbuf[:] = [A.bf16([128, 8192]) for _ in range(NWB)]
        LOW0 = A.top
        branch_b = A.hi([128, 16, TO], BF16); r_bb = Res("bb")
        branch_a = A.hi([128, 16, TO], BF16); r_ba = Res("ba")
        A.mark()
        gA = A.f32([128, TO]); gB = A.f32([128, TO]); r_g = Res()
        bglu = A.f32([128, 16]); r_bglu = Res()
        P.dma(SP, lambda e: e.dma_start(out=bglu, in_=bglu_fm), writes=[r_bglu])
        for ch in range(16):
            y = s5yT[:, ch, :]
            R_ = [r_s5y, r_g]
            W_ = [r_g]
            TT(DVE, gA, y, y, ALU.mult, R_, W_)
            TS(DVE, gA, gA, 0.044715, 1.0, ALU.mult, ALU.add, R_, W_)
            TT(DVE, gA, gA, y, ALU.mult, R_, W_)
            OP(ACT, "activation", R_, W_, out=gB, in_=gA, func=AF.Sigmoid, scale=1.5957691216057308)
            TT(DVE, y, y, gB, ALU.mult, R_ + [r_s5y], [r_s5y])
        zt = [A.bf16([128, TO]) for _ in range(2)]; r_zt = [Res(), Res()]
        sgt = [A.bf16([128, 512]) for _ in range(2)]; r_sgt = [Res(), Res()]
        zc = {"i": 0, "ch": -1}

        def evac_glu(ch, n0, n, ps, rps):
            if ch != zc["ch"]:
                zc["ch"] = ch
                zc["i"] += 1
                i = zc["i"] % 2
                P.dma(SP, (lambda e, i=i, ch=ch: e.dma_start(out=zt[i], in_=ZB[ch])), reads=[r_zb], writes=[r_zt[i]])
            i = zc["i"] % 2
            k = (0 if n0 == 0 else 1)
            OP(ACT, "activation", [rps, r_bglu], [r_sgt[k]], out=sgt[k][:, 0:n], in_=ps[:, 0:n], func=AF.Sigmoid, bias=bglu[:, ch:ch + 1])
            TT(DVE, sgt[k][:, 0:n], sgt[k][:, 0:n], s5yT[:, ch, n0:n0 + n], ALU.mult, [r_sgt[k], r_s5y], [r_sgt[k]])
            TT(DVE, branch_b[:, ch, n0:n0 + n], sgt[k][:, 0:n], zt[i][:, n0:n0 + n], ALU.mult, [r_sgt[k], r_zt[i]], [r_bb])

        linear_fm(w_glu, 0, 16, s5yT, [r_s5y], 16, evac_glu)
        P.barrier()
        A.release()

        q_peT = s5yT
        r_qpe = Res("qpe")
        A.mark()
        ropeq = A.f32([128, 6, 64]); r_ropeq = Res()
        P.dma(SP, lambda e: e.dma_start(out=ropeq, in_=rope_q.rearrange("(a p) d -> p a d", p=128)), writes=[r_ropeq])
        qpf = A.f32([128, 16, 64]); r_qpf = Res()
        qpb = A.bf16([128, 16, 64]); r_qpb = Res()
        rt = [A.f32([128, 16, 32]) for _ in range(4)]; r_rt = Res()
        wi = ctr["w"] % NWB
        ctr["w"] += 1
        wpe = wbuf[wi].rearrange("p (hh k c) -> p hh k c", hh=2, k=8)
        for hh in range(2):
            for kc in range(8):
                P.dma(POOL, (lambda e, hh=hh, kc=kc: e.dma_start(
                    out=wpe[:, hh, kc, :].rearrange("p (h d) -> p h d", d=64),
                    in_=w_uq[kc * 128:(kc + 1) * 128, :].rearrange("p (h d) -> p h d", d=192)[:, hh * 8:(hh + 1) * 8, 128:192])), writes=[r_wbuf[wi]])
        for tt_ in range(6):
            for hh in range(2):
                ps, rps = next_mm()
                for kc in range(8):
                    P.op(PE, (lambda e, ps=ps, kc=kc, hh=hh, tt_=tt_: e.matmul(ps[:, 0:512], lhsT=cqnT[:, kc, tt_ * 128:(tt_ + 1) * 128], rhs=wpe[:, hh, kc, :],
                                                                              start=(kc == 0), stop=(kc == 7))),
                         reads=[r_cqn, r_wbuf[wi]], writes=[rps], signal=(kc == 7))
                OP(ACT, "copy", [rps], [r_qpf], out=qpf[:, hh * 8:(hh + 1) * 8, :].rearrange("p h d -> p (h d)"), in_=ps[:, 0:512])
            sv = qpf.rearrange("p h (x s d) -> p h x s d", x=2, s=2)
            dv = qpb.rearrange("p h (x s d) -> p h x s d", x=2, s=2)
            cosb = ropeq[:, tt_, 0:32].rearrange("p (x d) -> p x d", x=2).unsqueeze(1).broadcast_to([128, 16, 2, 16])
            sinb = ropeq[:, tt_, 32:64].rearrange("p (x d) -> p x d", x=2).unsqueeze(1).broadcast_to([128, 16, 2, 16])
            x1, x2 = sv[:, :, :, 0, :], sv[:, :, :, 1, :]
            tv = [t_.rearrange("p h (x d) -> p h x d", x=2) for t_ in rt]
            RR = [r_qpf, r_ropeq, r_rt]
            TT(DVE, tv[0], x1, cosb, ALU.mult, RR, [r_rt])
            TT(DVE, tv[1], x2, sinb, ALU.mult, RR, [r_rt])
            TT(DVE, tv[2], x2, cosb, ALU.mult, RR, [r_rt])
            TT(DVE, tv[3], x1, sinb, ALU.mult, RR, [r_rt])
            TT(DVE, dv[:, :, :, 0, :], tv[0], tv[1], ALU.subtract, [r_rt, r_qpb], [r_qpb])
            TT(DVE, dv[:, :, :, 1, :], tv[2], tv[3], ALU.add, [r_rt, r_qpb], [r_qpb])
            for hh in range(2):
                pt, rpt = next_tr()
                for h8 in range(8):
                    h = hh * 8 + h8
                    P.op(PE, (lambda e, pt=pt, h=h, h8=h8: e.transpose(out=pt[0:64, h8 * 128:(h8 + 1) * 128], in_=qpb[:, h, :], identity=ident_b)),
                         reads=[r_qpb, r_ident_b], writes=[rpt], signal=(h8 == 7))
                OP(ACT, "copy", [rpt], [r_qpe], out=q_peT[0:64, hh * 8:(hh + 1) * 8, tt_ * 128:(tt_ + 1) * 128],
                   in_=pt[0:64, :].rearrange("p (h t) -> p h t", h=8))
        P.barrier()
        A.release()

        A.mark()
        qT = A.bf16([128, TO]); r_qT = Res()
        kT = A.bf16([128, NKEY]); r_kT = Res()
        Vh = A.bf16([128, 16, 128]); r_Vh = Res()
        Pe = A.bf16([128, 1536]); r_Pe = Res()
        PT = A.bf16([128, 12, 128]); r_PT = Res()
        stt_ = A.f32([128, 16]); r_stt = Res()
        zat = [A.bf16([128, TO]) for _ in range(2)]; r_zat = [Res(), Res()]
        SCALE = 192.0 ** -0.5
        KR = [(0, 256), (0, 256), (256, 256), (256, 256), (512, 1536), (512, 1536)]
        for h in range(16):
            zi = h % 2
            P.dma(SP, (lambda e, zi=zi, h=h: e.dma_start(out=zat[zi], in_=ZA[h])), reads=[r_za], writes=[r_zat[zi]])
            wq, rwq = load_w(w_uq, 0, 8, h * 192, 128)
            wkv, rwkv = load_w(w_ukv, 0, 4, h * 256, 256)
            for (n0, n) in SEGS:
                ps, rps = next_mm()
                for kc in range(8):
                    P.op(PE, (lambda e, ps=ps, kc=kc, n0=n0, n=n, wq=wq: e.matmul(ps[:, 0:n], lhsT=wq[:, kc, :], rhs=cqnT[:, kc, n0:n0 + n],
                                                                                 start=(kc == 0), stop=(kc == 7))),
                         reads=[rwq, r_cqn], writes=[rps], signal=(kc == 7))
                OP(ACT, "copy", [rps], [r_qT], out=qT[:, n0:n0 + n], in_=ps[:, 0:n])
            for s4 in range(4):
                ps, rps = next_mm()
                for kc in range(4):
                    P.op(PE, (lambda e, ps=ps, kc=kc, s4=s4, wkv=wkv: e.matmul(ps[:, 0:512], lhsT=wkv[:, kc, 0:128], rhs=ckvT_keys[:, kc, s4 * 512:(s4 + 1) * 512],
                                                                              start=(kc == 0), stop=(kc == 3))),
                         reads=[rwkv, r_ckvT], writes=[rps], signal=(kc == 3))
                OP(ACT, "copy", [rps], [r_kT], out=kT[:, s4 * 512:(s4 + 1) * 512], in_=ps[:, 0:512])
            for b4 in range(4):
                ps, rps = next_mm()
                for bi in range(4):
                    blk = b4 * 4 + bi
                    for kc in range(4):
                        P.op(PE, (lambda e, ps=ps, kc=kc, bi=bi, blk=blk, wkv=wkv: e.matmul(ps[:, bi * 128:(bi + 1) * 128], lhsT=ckvT_keys[:, kc, blk * 128:(blk + 1) * 128],
                                                                                           rhs=wkv[:, kc, 128:256], start=(kc == 0), stop=(kc == 3))),
                             reads=[rwkv, r_ckvT], writes=[rps], signal=(kc == 3 and bi == 3))
                OP(DVE, "tensor_copy", [rps], [r_Vh], out=Vh[:, b4 * 4:(b4 + 1) * 4, :], in_=ps[:, 0:512].rearrange("p (b d) -> p b d", b=4))
            for qt in range(6):
                k0, nk = KR[qt]
                nb = (nk + 511) // 512
                qs = slice(qt * 128, (qt + 1) * 128)
                banks = [(psf[3 + b], r_psf[3 + b]) for b in range(nb)]
                for b, (ps, rps) in enumerate(banks):
                    w_ = min(512, nk - b * 512)
                    ks = slice(k0 + b * 512, k0 + b * 512 + w_)
                    P.op(PE, (lambda e, ps=ps, w_=w_, ks=ks, qs=qs: e.matmul(ps[:, 0:w_], lhsT=qT[:, qs], rhs=kT[:, ks], start=True, stop=False)),
                         reads=[r_qT, r_kT], writes=[rps], signal=False)
                    P.op(PE, (lambda e, ps=ps, w_=w_, ks=ks, qs=qs, h=h: e.matmul(ps[:, 0:w_], lhsT=q_peT[0:64, h, qs], rhs=kpeT_keys[0:64, ks], start=False, stop=True)),
                         reads=[r_qpe, r_kpeT], writes=[rps], signal=True)
                RSt = [r_stt]
                for b, (ps, rps) in enumerate(banks):
                    w_ = min(512, nk - b * 512)
                    OP(DVE, "tensor_reduce", [rps] + RSt, [r_stt], out=stt_[:, b:b + 1], in_=ps[:, 0:w_], axis=AX.X, op=ALU.max)
                if nb > 1:
                    OP(DVE, "tensor_reduce", RSt, [r_stt], out=stt_[:, 4:5], in_=stt_[:, 0:nb], axis=AX.X, op=ALU.max)
                    mcol = stt_[:, 4:5]
                else:
                    mcol = stt_[:, 0:1]
                TS(DVE, stt_[:, 5:6], mcol, -SCALE, None, ALU.mult, None, RSt, [r_stt])
                for b, (ps, rps) in enumerate(banks):
                    w_ = min(512, nk - b * 512)
                    OP(ACT, "activation", [rps, r_stt, r_Pe], [r_Pe, r_stt], out=Pe[:, b * 512:b * 512 + w_], in_=ps[:, 0:w_], func=AF.Exp, scale=SCALE,
                       bias=stt_[:, 5:6], accum_out=stt_[:, 8 + b:9 + b])
                if nb > 1:
                    OP(DVE, "tensor_reduce", RSt, [r_stt], out=stt_[:, 6:7], in_=stt_[:, 8:8 + nb], axis=AX.X, op=ALU.add)
                    scol = stt_[:, 6:7]
                else:
                    scol = stt_[:, 8:9]
                OP(DVE, "reciprocal", RSt, [r_stt], out=stt_[:, 7:8], in_=scol)
                TS(DVE, Pe[:, 0:nk], Pe[:, 0:nk], stt_[:, 7:8], None, ALU.mult, None, [r_Pe, r_stt], [r_Pe])
                nblk = nk // 128
                for g8 in range((nblk + 7) // 8):
                    pt, rpt = next_tr()
                    nn = min(8, nblk - g8 * 8)
                    for bi in range(nn):
                        blk = g8 * 8 + bi
                        P.op(PE, (lambda e, pt=pt, bi=bi, blk=blk: e.transpose(out=pt[:, bi * 128:(bi + 1) * 128], in_=Pe[:, blk * 128:(blk + 1) * 128], identity=ident_b)),
                             reads=[r_Pe, r_ident_b], writes=[rpt], signal=(bi == nn - 1))
                    OP(ACT, "copy", [rpt], [r_PT], out=PT[:, g8 * 8:g8 * 8 + nn, :], in_=pt[:, 0:nn * 128].rearrange("p (b t) -> p b t", b=nn))
                ps, rps = next_mm()
                kb0 = k0 // 128
                for blk in range(nblk):
                    P.op(PE, (lambda e, ps=ps, blk=blk, kb0=kb0, nblk=nblk: e.matmul(ps[:, 0:128], lhsT=Vh[:, kb0 + blk, :], rhs=PT[:, blk, :],
                                                                        start=(blk == 0), stop=(blk == nblk - 1))),
                         reads=[r_Vh, r_PT], writes=[rps], signal=(blk == nblk - 1))
                TT(DVE, branch_a[:, h, qs], ps[:, 0:128], zat[zi][:, qs], ALU.mult, [rps, r_zat[zi]], [r_ba])
        P.barrier()
        A.release()

        ctr["mm_n"] = 6
        A.top = CONST_TOP
        wbuf[:] = [A.bf16([128, 8192]) for _ in range(NWB)]
        LOW1 = A.top
        merged = A.bf16([128, KC, TO]); r_mg = Res("merged")
        A.mark()
        gat = [A.bf16([128, TO]) for _ in range(2)]; r_gat = [Res(), Res()]
        gbt = [A.bf16([128, TO]) for _ in range(2)]; r_gbt = [Res(), Res()]
        mt = [A.f32([128, 512]) for _ in range(2)]; r_mt = [Res(), Res()]
        for j2 in range(16):
            wa, rwa = load_w(w_pa, 0, 16, j2 * 256, 256)
            wb, rwb = load_w(w_pb, 0, 16, j2 * 256, 256)
            for c2 in range(2):
                j = j2 * 2 + c2
                gi = j % 2
                P.dma(SP, (lambda e, gi=gi, j=j: e.dma_start(out=gat[gi], in_=GA[j])), reads=[r_ga], writes=[r_gat[gi]])
                P.dma(SP, (lambda e, gi=gi, j=j: e.dma_start(out=gbt[gi], in_=GB[j])), reads=[r_gb], writes=[r_gbt[gi]])
                for si, (n0, n) in enumerate(SEGS):
                    psa, rpsa = next_mm()
                    for kc in range(16):
                        P.op(PE, (lambda e, ps=psa, kc=kc, c2=c2, n0=n0, n=n, wa=wa: e.matmul(ps[:, 0:n], lhsT=wa[:, kc, c2 * 128:(c2 + 1) * 128], rhs=branch_a[:, kc, n0:n0 + n],
                                                                                             start=(kc == 0), stop=(kc == 15))),
                             reads=[rwa, r_ba], writes=[rpsa], signal=(kc == 15))
                    psb_, rpsb = next_mm()
                    for kc in range(16):
                        P.op(PE, (lambda e, ps=psb_, kc=kc, c2=c2, n0=n0, n=n, wb=wb: e.matmul(ps[:, 0:n], lhsT=wb[:, kc, c2 * 128:(c2 + 1) * 128], rhs=branch_b[:, kc, n0:n0 + n],
                                                                                              start=(kc == 0), stop=(kc == 15))),
                             reads=[rwb, r_bb], writes=[rpsb], signal=(kc == 15))
                    TT(DVE, mt[0][:, 0:n], psa[:, 0:n], gat[gi][:, n0:n0 + n], ALU.mult, [rpsa, r_gat[gi], r_mt[0]], [r_mt[0]])
                    TT(DVE, mt[1][:, 0:n], psb_[:, 0:n], gbt[gi][:, n0:n0 + n], ALU.mult, [rpsb, r_gbt[gi], r_mt[1]], [r_mt[1]])
                    TT(DVE, merged[:, j, n0:n0 + n], mt[0][:, 0:n], mt[1][:, 0:n], ALU.add, [r_mt[0], r_mt[1]], [r_mg])
        P.barrier()
        A.release()

        A.n = NW
        yacc = A.f32([128, 6, D]); r_yacc = Res("yacc")
        A.mark()
        gtl = [A.f32([128, 2, 256]) for _ in range(2)]; r_gtl = [Res(), Res()]
        for cb in range(16):
            wv, rw = load_w(w_o, 0, KC, cb * 256, 256)
            gi = cb % 2
            for m in range(2):
                P.dma(SP, (lambda e, gi=gi, m=m, cb=cb: e.dma_start(out=gtl[gi][:, m, :], in_=gate_d[m:m + 1, cb * 256:(cb + 1) * 256].partition_broadcast(128))),
                      reads=[r_gate_d], writes=[r_gtl[gi]])
            for tt_ in range(6):
                m = 0 if tt_ < 4 else 1
                ps, rps = next_mm()
                for kc in range(KC):
                    P.op(PE, (lambda e, ps=ps, kc=kc, tt_=tt_, wv=wv: e.matmul(ps[:, 0:256], lhsT=merged[:, kc, tt_ * 128:(tt_ + 1) * 128], rhs=wv[:, kc, :],
                                                                              start=(kc == 0), stop=(kc == KC - 1))),
                         reads=[rw, r_mg], writes=[rps], signal=(kc == KC - 1))
                TT(DVE, yacc[:, tt_, cb * 256:(cb + 1) * 256], ps[:, 0:256], gtl[gi][:, m, :], ALU.mult, [rps, r_gtl[gi]], [r_yacc])
        P.barrier()
        A.release()
        A.top = CONST_TOP
        lng = A.f32([128, D]); lnb = A.f32([128, D]); r_ln = Res()
        P.dma(SP, lambda e: e.dma_start(out=lng, in_=ln_g.partition_broadcast(128)), writes=[r_ln])
        P.dma(SP, lambda e: e.dma_start(out=lnb, in_=ln_b.partition_broadcast(128)), writes=[r_ln])
        xr = A.f32([128, D]); r_xr = Res()
        fs = A.f32([128, 64]); r_fs = Res()
        r_yout = Res("yout")
        for tt_ in range(6):
            z = yacc[:, tt_, :]
            P.dma(SP, (lambda e, tt_=tt_: e.dma_start(out=xr, in_=x_own[tt_ * 128:(tt_ + 1) * 128, :])), writes=[r_xr])
            OP(DVE, "scalar_tensor_tensor", [r_xr, r_yacc], [r_yacc], out=z, in0=xr, scalar=ALPHA, in1=z, op0=ALU.mult, op1=ALU.add)
            for c in range(8):
                OP(DVE, "bn_stats", [r_yacc, r_fs], [r_fs], out=fs[:, c * 6:(c + 1) * 6], in_=z[:, c * 512:(c + 1) * 512])
            OP(DVE, "bn_aggr", [r_fs], [r_fs], out=fs[:, 48:50], in_=fs[:, 0:48])
            TS(DVE, fs[:, 50:51], fs[:, 49:50], EPS, None, ALU.add, None, [r_fs], [r_fs])
            OP(ACT, "activation", [r_fs], [r_fs], out=fs[:, 51:52], in_=fs[:, 50:51], func=AF.Sqrt)
            OP(DVE, "reciprocal", [r_fs], [r_fs], out=fs[:, 52:53], in_=fs[:, 51:52])
            TS(DVE, fs[:, 53:54], fs[:, 48:49], fs[:, 52:53], -1.0, ALU.mult, ALU.mult, [r_fs], [r_fs])
            OP(ACT, "activation", [r_fs, r_yacc], [r_yacc], out=z, in_=z, func=AF.Identity, scale=fs[:, 52:53], bias=fs[:, 53:54])
            TT(DVE, z, z, lng, ALU.mult, [r_yacc, r_ln], [r_yacc])
            TT(DVE, z, z, lnb, ALU.add, [r_yacc, r_ln], [r_yacc])
            P.dma(SP, (lambda e, tt_=tt_, z=z: e.dma_start(out=y_out[tt_ * 128:(tt_ + 1) * 128, :], in_=z)), reads=[r_yacc], writes=[r_yout])

        P.wait_on(SP, [r_ockv, r_okpe, r_gate_d, r_uall, r_za, r_zb, r_ga, r_gb, r_osre, r_osim, r_yout])
        P.wait_on(ACT, [r_ckvT, r_kpeT])
        P.barrier()
        P.emit()
    return nc


def kernel(x_prompt, x_sample, cache_ckv, cache_kpe, state_s5_re, state_s5_im, c, c_ctx,
           w_ada, b_ada, w_in, g_qn, w_uq, g_kvn, w_ukv,
           s5_a_re, s5_a_im, s5_log_dt, s5_b_re, s5_b_im, s5_c_re, s5_c_im, s5_d,
           w_glu, b_glu, w_pa, w_pb, w_o, ln_g, ln_b):
    f = lambda a: np.ascontiguousarray(np.asarray(a, dtype=np.float32))
    x_prompt, x_sample = f(x_prompt), f(x_sample)
    if "nc" not in _NC_CACHE:
        try:
            build_program()
        except _Done:
            pass
    nc = _NC_CACHE["nc"]
    w_ada0, w_in0 = f(w_ada)[0], f(w_in)[0]
    b0 = f(b_ada)[0]
    b_fm = np.ascontiguousarray(b0[:2 * D].reshape(64, 128).T)
    b_gate = np.ascontiguousarray(np.broadcast_to(b0[2 * D:][None, :], (2, D)))
    g_qn_fm = np.ascontiguousarray(f(g_qn)[0].reshape(8, 128).T)
    tt = np.arange(TA)
    inv = np.power(10000.0, -np.arange(16, dtype=np.float32) / 16).astype(np.float32)
    ang_r = (tt // 64).astype(np.float32)[:, None] * inv[None, :]
    ang_c = (tt % 64).astype(np.float32)[:, None] * inv[None, :]
    rope_k = np.ascontiguousarray(np.concatenate([np.cos(ang_r), np.cos(ang_c), np.sin(ang_r), np.sin(ang_c)], axis=1).astype(np.float32))
    are_, aim_, ldt_ = f(s5_a_re)[0], f(s5_a_im)[0], f(s5_log_dt)[0]
    a_t = lambda a: a.transpose(2, 0, 1).reshape(64, 256)
    s5A = np.zeros((128, 3, 256), np.float32)
    s5A[:, 0] = np.concatenate([a_t(are_), a_t(are_)], 0)
    s5A[:, 1] = np.concatenate([a_t(aim_), a_t(aim_)], 0)
    s5A[:, 2] = np.broadcast_to(ldt_.reshape(1, 256), (128, 256))
    bre, bim = f(s5_b_re)[0], f(s5_b_im)[0]
    b_t = lambda a: a.transpose(2, 0, 1, 3).reshape(64, 256, 16)
    s5B = np.stack([np.concatenate([b_t(bre), b_t(bim)], 0), np.concatenate([b_t(bim), b_t(bre)], 0)], axis=1)
    cre, cim = f(s5_c_re)[0], f(s5_c_im)[0]
    c_t = lambda a: a.transpose(3, 0, 1, 2).reshape(64, 256, 16)
    s5C = np.stack([np.concatenate([c_t(cre), c_t(cim)], 0), np.concatenate([c_t(cim), c_t(cre)], 0)], axis=1)
    s5A, s5B, s5C = map(np.ascontiguousarray, (s5A, s5B.astype(np.float32), s5C.astype(np.float32)))
    ar8 = np.arange(8, dtype=np.float32); ar32 = np.arange(32, dtype=np.float32)
    kt = np.zeros((2, 59), np.float32)
    kt[0] = np.concatenate([7 - ar8, ar8 - 7, ar8 + 1, [1, 8, 256], 8 * (ar32 + 1)])
    kt[1] = np.concatenate([ar8, -ar8, 8 - ar8, [1, 8, 256], 8 * (32 - ar32)])
    kt_in = np.ascontiguousarray(np.broadcast_to(kt[None], (128, 2, 59)))
    sgn_in = np.concatenate([-np.ones((64, 1), np.float32), np.ones((64, 1), np.float32)], 0)
    ss = np.repeat(np.arange(8), 16)
    mask_in = np.ascontiguousarray(np.stack([(ss[:, None] <= ss[None, :]), (ss[:, None] >= ss[None, :])], axis=1).astype(np.float32))
    dcol_in = np.ascontiguousarray(np.tile(f(s5_d)[0].reshape(128, 16).T, (8, 1)))
    W_UQ, W_UKV, W_GLU, W_PA, W_PB, W_O = f(w_uq)[0], f(w_ukv)[0], f(w_glu)[0], f(w_pa)[0], f(w_pb)[0], f(w_o)[0]
    bglu_fm = np.ascontiguousarray(f(b_glu)[0].reshape(16, 128).T)
    rope_id = np.concatenate([np.ones((512, 32), np.float32), np.zeros((512, 32), np.float32)], axis=1)
    in_maps = []
    for i in range(8):
        b, q = i // 4, i % 4
        x_own = np.concatenate([x_prompt[2 * i], x_prompt[2 * i + 1], x_sample[b, q * 256:(q + 1) * 256]], axis=0)
        cond = np.stack([f(c_ctx), f(c)[b]], axis=0)
        condT = np.ascontiguousarray(cond.reshape(2, KC, 128).transpose(2, 1, 0))
        in_maps.append({
            "x_own": np.ascontiguousarray(x_own), "x_aux": np.ascontiguousarray(x_sample[b]),
            "condT": condT, "b_fm": b_fm, "b_gate": b_gate, "w_ada": w_ada0, "w_in": w_in0,
            "g_kvn": f(g_kvn), "g_qn_fm": g_qn_fm,
            "s5A": s5A, "s5B": s5B, "s5C": s5C, "kt_in": kt_in, "sgn_in": sgn_in, "mask_in": mask_in, "dcol_in": dcol_in,
            "h0_in": np.ascontiguousarray(np.stack([f(state_s5_re)[b, 0], f(state_s5_im)[b, 0]], axis=1).transpose(3, 0, 1, 2)),
            "oh_in": np.ascontiguousarray(np.broadcast_to(np.eye(4, dtype=np.float32)[q][None, :], (64, 4))),
            "w_uq": W_UQ, "w_ukv": W_UKV, "w_glu": W_GLU, "w_pa": W_PA, "w_pb": W_PB, "w_o": W_O, "bglu_fm": bglu_fm,
            "rope_q": np.ascontiguousarray(np.concatenate([rope_id, rope_k[q * 256:(q + 1) * 256]], axis=0)),
            "ln_g": f(ln_g), "ln_b": f(ln_b),
            "cache_ckv": f(cache_ckv)[b, 0], "cache_kpe": f(cache_kpe)[b, 0], "rope_k": rope_k,
        })
    ncores = int(os.environ.get("KCORES", "8"))
    res = run_bass_kernel_spmd(nc, in_maps[:ncores], core_ids=list(range(ncores)))
    R = list(res.results) + [res.results[0]] * (8 - ncores)
    y_prompt = np.zeros((16, 256, D), np.float32)
    y_sample = np.zeros((2, 1024, D), np.float32)
    n_ckv = np.zeros((16, 1, 256, 512), np.float32)
    n_kpe = np.zeros((16, 1, 256, 64), np.float32)
    n_sre = np.zeros((16, 1, 2, 128, 64), np.float32)
    n_sim = np.zeros((16, 1, 2, 128, 64), np.float32)
    for i in range(8):
        n_ckv[2 * i:2 * i + 2, 0] = R[i]["o_ckv"].reshape(2, 256, 512)
        n_kpe[2 * i:2 * i + 2, 0] = R[i]["o_kpe"].reshape(2, 256, 64)
        n_sre[2 * i:2 * i + 2, 0] = R[i]["o_sre"]
        n_sim[2 * i:2 * i + 2, 0] = R[i]["o_sim"]
        yo = R[i]["y_out"]
        y_prompt[2 * i] = yo[0:256]
        y_prompt[2 * i + 1] = yo[256:512]
        y_sample[i // 4, (i % 4) * 256:(i % 4 + 1) * 256] = yo[512:768]
    global DBG
    DBG = R
    return (y_prompt, y_sample, n_ckv, n_kpe, n_sre, n_sim)
```

```python
import contextlib
import numpy as np
import concourse.bass as bass
import concourse.mybir as mybir
from concourse.bass_utils import run_bass_kernel_spmd

F32 = mybir.dt.float32
BF16 = mybir.dt.bfloat16
AF = mybir.ActivationFunctionType
ALU = mybir.AluOpType
AX = mybir.AxisListType

PE, ACT, DVE, POOL, SP = "tensor", "scalar", "vector", "gpsimd", "sync"
ENGS = [PE, ACT, DVE, POOL, SP]
SEM_ROLL = 12000

D = 4096
KC = 32
TO = 768
TA = 1024
NKEY = 2048
IN_COLS = 15936
C_CQ, C_CKV, C_KPE, C_ZA, C_UB, C_ZB, C_GA, C_GB = 0, 1024, 1536, 1600, 3648, 5696, 7744, 11840
ALPHA = 2.0 ** 0.25
EPS = 1e-6
DEBUG = False
import os
STAGE = int(os.environ.get('KSTAGE', '99'))


_NC_CACHE = {}


class _Done(Exception):
    pass


class Res:
    __slots__ = ("name", "w", "r", "wl")

    def __init__(self, name=""):
        self.name = name
        self.w = None
        self.r = {}
        self.wl = {}


class Prog:
    def __init__(self, nc, n_dma_sems=24):
        self.nc = nc
        self.ops = {e: [] for e in ENGS}
        self.sem_handles = {}
        self.cur = {}
        self.prev = {}
        self.sem_n = {e: 0 for e in ENGS}
        self.seen = {e: {} for e in ENGS}
        for e in [PE, ACT, DVE, POOL]:
            self._new_counter(e)
        self.n_dma_sems = n_dma_sems
        self.dma_sems = []
        self.dma_pool = {}
        for q in [SP, POOL, ACT]:
            self.dma_pool[q] = []
            for i in range(n_dma_sems if q == SP else 8):
                k = ("dma", q, i)
                self.sem_handles[k] = None
                ent = [k, 0]
                self.dma_sems.append(ent)
                self.dma_pool[q].append(ent)
        self.dma_q = {SP: 0, POOL: 0, ACT: 0}
        self.dma_i = 0
        self.slot_epoch = {}

    def _new_counter(self, e):
        k = ("cnt", e, self.sem_n[e])
        self.sem_n[e] += 1
        self.sem_handles[k] = None
        if e in self.cur:
            self.prev[e] = tuple(self.cur[e])
        self.cur[e] = [k, 0]

    def _need(self, eng, dep, waits, allow_same=False):
        if dep is None:
            return
        k, v, src = dep
        if src == eng and not (allow_same and eng != PE):
            return
        if self.seen[eng].get(k, 0) >= v:
            return
        waits[k] = max(waits.get(k, 0), v)

    def _collect(self, eng, reads, writes, relax=False):
        waits = {}
        for r in reads:
            self._need(eng, r.w, waits, allow_same=True)
            for k_, v_ in r.wl.items():
                self._need(eng, (k_, v_, "dma"), waits)
        for w in writes:
            self._need(eng, w.w, waits, allow_same=not relax)
            for k_, v_ in w.wl.items():
                self._need(eng, (k_, v_, "dma"), waits)
            for re_, d in w.r.items():
                self._need(eng, (d[0], d[1], re_), waits, allow_same=True)
        for k, v in waits.items():
            self.seen[eng][k] = v
        return list(waits.items())

    def op(self, eng, fn, reads=(), writes=(), signal=True, relax=False):
        if eng != PE:
            signal = True
        waits = self._collect(eng, reads, writes, relax=relax)
        k, v = self.cur[eng]
        nv = v + 1
        if signal:
            self.cur[eng][1] = nv
        for r in reads:
            r.r[eng] = (k, nv)
        for w in writes:
            w.w = (k, nv, eng)
            w.r = {}
            w.wl = {}
        self.ops[eng].append((waits, fn, (k, 1) if signal else None))
        if signal and nv >= SEM_ROLL:
            self._new_counter(eng)

    def dma_slot(self, eng, fn, slot_id, reads=(), writes=()):
        waits = self._collect(eng, reads, writes)
        ep = self.slot_epoch.get(slot_id, 0)
        self.slot_epoch[slot_id] = ep + 1
        k = ("slot", slot_id, ep)
        self.sem_handles.setdefault(("slot", slot_id), None)
        if ep > 0:
            waits = list(waits) + [(("slot", slot_id, ep - 1), 16)]
            self.ops[eng].append((waits, ("clear", ("slot", slot_id)), None))
            waits = []
        self.dma_i += 1
        for r in reads:
            r.r["dma%d" % self.dma_i] = (k, 16)
        for w in writes:
            w.w = (k, 16, "dma")
            w.r = {}
        self.ops[eng].append((waits, fn, (k, 16)))

    def dma(self, eng, fn, reads=(), writes=()):
        waits = self._collect(eng, reads, writes)
        pool = self.dma_pool[eng]
        slot = pool[self.dma_q[eng] % len(pool)]
        self.dma_q[eng] += 1
        self.dma_i += 1
        k, v = slot
        if v > 0 and self.seen[eng].get(k, 0) < v:
            waits.append((k, v))
            self.seen[eng][k] = v
        nv = v + 16
        slot[1] = nv
        for r in reads:
            r.r["dma%d" % self.dma_i] = (k, nv)
        for w in writes:
            if w.w is not None and w.w[2] == "dma":
                w.wl[w.w[0]] = max(w.wl.get(w.w[0], 0), w.w[1])
            else:
                w.wl = {}
            w.w = (k, nv, "dma")
            w.r = {}
        self.ops[eng].append((waits, fn, (k, 16)))

    def wait_on(self, eng, res_list):
        waits = self._collect(eng, res_list, ())
        self.ops[eng].append((waits, None, None))

    def barrier(self):
        targets = []
        for e in [PE, ACT, DVE, POOL]:
            k, v = self.cur[e]
            if v > 0:
                targets.append((k, v))
            elif e in self.prev:
                targets.append(self.prev[e])
        for k, v in self.dma_sems:
            if v > 0:
                targets.append((k, v))
        for sid, ep in self.slot_epoch.items():
            targets.append((("slot", sid, ep - 1), 16))
        for e in ENGS:
            waits = []
            for k, v in targets:
                if k[0] == "cnt" and k[1] == e:
                    continue
                if self.seen[e].get(k, 0) < v:
                    waits.append((k, v))
                    self.seen[e][k] = v
            if waits:
                self.ops[e].append((waits, None, None))

    def emit(self):
        nc = self.nc
        with contextlib.ExitStack() as st:
            for k in self.sem_handles:
                self.sem_handles[k] = st.enter_context(nc.semaphore("s_" + "_".join(str(x) for x in k)))
            block = st.enter_context(nc.Block())
            HH = self.sem_handles

            def H(k):
                return HH[k[:2]] if k[0] == "slot" else HH[k]

            def run(engine_name):
                def body(e):
                    for waits, fn, inc in self.ops[engine_name]:
                        for k, v in waits:
                            e.wait_ge(H(k), v)
                        if fn is None:
                            continue
                        if isinstance(fn, tuple):
                            e.sem_clear(H(fn[1]))
                            continue
                        ins = fn(e)
                        if inc is not None:
                            ins.then_inc(H(inc[0]), inc[1])
                return body

            block.sync(run(SP))
            block.scalar(run(ACT))
            block.vector(run(DVE))
            block.gpsimd(run(POOL))
            block.tensor(run(PE))


class Arena:
    def __init__(self, t, nwords):
        self.t = t
        self.n = nwords
        self.top = 0
        self.marks = []

    def f32(self, shape):
        n = int(np.prod(shape[1:]))
        assert self.top + n <= self.n, ("arena overflow", self.top, n, self.n)
        v = self.t[0:shape[0], self.top:self.top + n]
        self.top += n
        return self._shape(v, shape)

    def bf16(self, shape):
        n = int(np.prod(shape[1:]))
        nw = (n + 1) // 2
        assert self.top + nw <= self.n, ("arena overflow", self.top, nw, self.n)
        v = self.t[0:shape[0], self.top:self.top + nw].bitcast(BF16)[:, 0:n]
        self.top += nw
        return self._shape(v, shape)

    @staticmethod
    def _shape(v, shape):
        if len(shape) == 2:
            return v
        if len(shape) == 3:
            return v.rearrange("p (a b) -> p a b", a=shape[1])
        if len(shape) == 4:
            return v.rearrange("p (a b c) -> p a b c", a=shape[1], b=shape[2])
        if len(shape) == 5:
            return v.rearrange("p (a b c d) -> p a b c d", a=shape[1], b=shape[2], c=shape[3])
        raise ValueError(shape)

    def hi(self, shape, dt):
        n = int(np.prod(shape[1:]))
        nw = n if dt == F32 else (n + 1) // 2
        assert self.top + nw <= self.n, ("arena overflow(hi)", self.top, nw, self.n)
        self.n -= nw
        v = self.t[0:shape[0], self.n:self.n + nw]
        if dt != F32:
            v = v.bitcast(BF16)[:, 0:n]
        return self._shape(v, shape)

    def mark(self):
        self.marks.append(self.top)

    def release(self):
        self.top = self.marks.pop()


def build_program(stage=99):
    nc = bass.Bass("TRN2", target_bir_lowering=False)
    _NC_CACHE["nc"] = nc

    def din(name, shape, dt=F32):
        return nc.dram_tensor(name, list(shape), dt, kind="ExternalInput").ap()

    def dout(name, shape, dt=F32):
        return nc.dram_tensor(name, list(shape), dt, kind="ExternalOutput").ap()

    def dscr(name, shape, dt=F32):
        return nc.dram_tensor(name, list(shape), dt, kind="Internal").ap()

    x_own = din("x_own", [TO, D])
    x_aux = din("x_aux", [TA, D])
    condT = din("condT", [128, KC, 2])
    b_fm = din("b_fm", [128, 64])
    b_gate = din("b_gate", [2, D])
    w_ada = din("w_ada", [D, 3 * D])
    w_in = din("w_in", [D, IN_COLS])
    g_kvn = din("g_kvn", [1, 512])
    g_qn_fm = din("g_qn_fm", [128, 8])
    cache_ckv = din("cache_ckv", [512, 512])
    cache_kpe = din("cache_kpe", [512, 64])

    o_ckv = dout("o_ckv", [512, 512])
    o_kpe = dout("o_kpe", [512, 64])

    rope_k = din("rope_k", [TA, 64])
    s5A = din("s5A", [128, 3, 256])
    s5B = din("s5B", [128, 2, 256, 16])
    s5C = din("s5C", [128, 2, 256, 16])
    kt_in = din("kt_in", [128, 2, 59])
    sgn_in = din("sgn_in", [128, 1])
    mask_in = din("mask_in", [128, 2, 128])
    dcol_in = din("dcol_in", [128, 128])
    h0_in = din("h0_in", [64, 2, 2, 128])
    oh_in = din("oh_in", [64, 4])
    w_uq = din("w_uq", [1024, 3072])
    w_ukv = din("w_ukv", [512, 4096])
    w_glu = din("w_glu", [2048, 2048])
    w_pa = din("w_pa", [2048, D])
    w_pb = din("w_pb", [2048, D])
    w_o = din("w_o", [D, D])
    bglu_fm = din("bglu_fm", [128, 16])
    rope_q = din("rope_q", [TO, 64])
    ln_g = din("ln_g", [1, D])
    ln_b = din("ln_b", [1, D])
    y_out = dout("y_out", [TO, D])
    o_sre = dout("o_sre", [2, 2, 128, 64])
    o_sim = dout("o_sim", [2, 2, 128, 64])
    gate_d = dscr("gate_d", [2, D])
    Uall = dscr("Uall", [224, 128, 128], BF16)
    ZA = dscr("ZA", [16, 128, TO], BF16)
    ZB = dscr("ZB", [16, 128, TO], BF16)
    GA = dscr("GA", [32, 128, TO], BF16)
    GB = dscr("GB", [32, 128, TO], BF16)

    P = Prog(nc)
    with contextlib.ExitStack() as st:
        NW = 53000
        arena_t = st.enter_context(nc.sbuf_tensor("arena", [128, NW], F32))
        A = Arena(arena_t, NW)
        psf = [st.enter_context(nc.psum_tensor("psf%d" % i, [128, 512], F32)) for i in range(6)]
        psb = [st.enter_context(nc.psum_tensor("psb%d" % i, [128, 1024], BF16)) for i in range(2)]
        r_psf = [Res("psf%d" % i) for i in range(6)]
        r_psb = [Res("psb%d" % i) for i in range(2)]
        ctr = {"mm": 0, "tr": 0, "w": 0, "ev": 0}

        def next_mm():
            i = ctr["mm"] % ctr.get("mm_n", 3)
            ctr["mm"] += 1
            return psf[i], r_psf[i]

        def next_tr():
            i = ctr["tr"] % 2
            ctr["tr"] += 1
            return psb[i], r_psb[i]

        ident_f = A.f32([128, 128]); r_ident_f = Res()
        ident_b = A.bf16([128, 128]); r_ident_b = Res()
        ones_b = A.bf16([128, 128]); r_ones = Res()
        P.op(POOL, lambda e: e.memset(ident_f, 0.0), writes=[r_ident_f], signal=False)
        P.op(POOL, lambda e: e.affine_select(out=ident_f, in_=ident_f, pattern=[[-1, 128]], compare_op=ALU.not_equal,
                                             fill=1.0, base=0, channel_multiplier=1), writes=[r_ident_f])
        P.op(DVE, lambda e: e.tensor_copy(out=ident_b, in_=ident_f), reads=[r_ident_f], writes=[r_ident_b])
        P.op(DVE, lambda e: e.memset(ones_b, 1.0), writes=[r_ones])

        NWB = 3
        wbuf = []
        r_wbuf = [Res("wb%d" % i) for i in range(NWB)]

        def load_w(W, r0, nk, c0, ncols):
            assert nk * ncols <= 8192
            i = ctr["w"] % NWB
            ctr["w"] += 1
            view = wbuf[i][:, 0:nk * ncols].rearrange("p (k c) -> p k c", k=nk)
            src = W[r0:r0 + nk * 128, c0:c0 + ncols].rearrange("(k p) c -> p k c", p=128)
            P.dma(POOL, lambda e: e.dma_start(out=view, in_=src), writes=[r_wbuf[i]])
            return view, r_wbuf[i]

        modT = A.f32([128, 64, 2]); r_mod = Res("mod")
        bfm = A.f32([128, 64]); r_bfm = Res()
        scT = A.bf16([128, KC, 2]); r_scT = Res()
        CONST_TOP = A.top
        ckvT_keys = A.bf16([128, 4, NKEY]); r_ckvT = Res("ckvT")
        kpeT_keys = A.bf16([128, NKEY]); r_kpeT = Res("kpeT")
        gkb = A.f32([128, 512]); r_gkb = Res()
        OVERLAY0 = A.top
        wck = A.bf16([128, KC, 576]); r_wck = Res()
        wbuf.extend([A.bf16([128, 8192]) for _ in range(NWB)])
        AFTER_WBUF = A.top
        A.mark()
        cT = A.f32([128, KC, 2]); r_cT = Res()
        r_gate_d = Res("gate_d")
        P.dma(SP, lambda e: e.dma_start(out=cT, in_=condT), writes=[r_cT])
        P.dma(SP, lambda e: e.dma_start(out=bfm, in_=b_fm), writes=[r_bfm])
        P.op(ACT, lambda e: e.activation(out=scT, in_=cT, func=AF.Silu), reads=[r_cT], writes=[r_scT])
        P.op(DVE, lambda e: e.tensor_scalar_add(out=bfm[:, 32:64], in0=bfm[:, 32:64], scalar1=1.0), reads=[r_bfm], writes=[r_bfm])
        for blk in range(16):
            for kh in range(2):
                wv, rw = load_w(w_ada, kh * 2048, 16, blk * 512, 512)
                for c4 in range(4):
                    ps, rps = psf[c4], r_psf[c4]
                    for k16 in range(16):
                        kc = kh * 16 + k16
                        P.op(PE, (lambda e, ps=ps, wv=wv, k16=k16, kc=kc, c4=c4: e.matmul(ps[:, 0:2], lhsT=wv[:, k16, c4 * 128:(c4 + 1) * 128],
                                                                                         rhs=scT[:, kc, :], start=(kc == 0), stop=(kc == KC - 1))),
                             reads=[rw, r_scT], writes=[rps], signal=(kc == KC - 1))
            for c4 in range(4):
                ch = blk * 4 + c4
                ps, rps = psf[c4], r_psf[c4]
                P.op(DVE, (lambda e, ps=ps, ch=ch: e.tensor_scalar(out=modT[:, ch, :], in0=ps[:, 0:2], scalar1=bfm[:, ch:ch + 1],
                                                                  scalar2=None, op0=ALU.add)),
                     reads=[rps, r_bfm], writes=[r_mod])
        P.barrier()
        A.release()

        P.dma(SP, lambda e: e.dma_start(out=gkb, in_=g_kvn.partition_broadcast(128)), writes=[r_gkb])
        for part in range(3):
            c0 = [0, 256, 512][part]
            n = [256, 256, 64][part]
            wv, rw = load_w(w_in, 0, KC, C_CKV + c0, n)
            P.op(DVE, (lambda e, wv=wv, c0=c0, n=n: e.tensor_copy(out=wck[:, :, c0:c0 + n], in_=wv)), reads=[rw], writes=[r_wck])

        evc = {"i": 0}

        def evac_copy(out, in_, reads, writes):
            evc["i"] += 1
            if evc["i"] % 2:
                P.op(ACT, lambda e: e.copy(out=out, in_=in_), reads=reads, writes=writes)
            else:
                P.op(DVE, lambda e: e.tensor_copy(out=out, in_=in_), reads=reads, writes=writes)

        def make_ln_bufs(nxn=2):
            xns = [A.bf16([128, D]) for _ in range(nxn)]
            rxn = [Res() for _ in range(nxn)]
            return dict(xt=A.f32([128, D]), r_xt=Res(), xn=[xns[k % nxn] for k in range(2)], r_xn=[rxn[k % nxn] for k in range(2)],
                        st=[A.f32([128, 64]) for _ in range(2)], r_st=[Res(), Res()], i=0)

        def ln_tile(B, xsrc, row0, hT, r_hT, t0, m):
            i = B["i"] % 2
            B["i"] += 1
            xt, r_xt, xn, r_xn, stt, r_st = B["xt"], B["r_xt"], B["xn"][i], B["r_xn"][i], B["st"][i], B["r_st"][i]
            P.dma(SP, lambda e: e.dma_start(out=xt, in_=xsrc[row0:row0 + 128, :]), writes=[r_xt])
            for c in range(8):
                P.op(DVE, (lambda e, c=c: e.bn_stats(out=stt[:, c * 6:(c + 1) * 6], in_=xt[:, c * 512:(c + 1) * 512])),
                     reads=[r_xt], writes=[r_st], relax=(c > 0))
            P.op(DVE, lambda e: e.bn_aggr(out=stt[:, 48:50], in_=stt[:, 0:48]), reads=[r_st], writes=[r_st])
            P.op(DVE, lambda e: e.tensor_scalar_add(out=stt[:, 50:51], in0=stt[:, 49:50], scalar1=EPS), reads=[r_st], writes=[r_st])
            P.op(ACT, lambda e: e.activation(out=stt[:, 51:52], in_=stt[:, 50:51], func=AF.Sqrt), reads=[r_st], writes=[r_st])
            P.op(DVE, lambda e: e.reciprocal(out=stt[:, 52:53], in_=stt[:, 51:52]), reads=[r_st], writes=[r_st])
            P.op(DVE, lambda e: e.tensor_scalar(out=stt[:, 53:54], in0=stt[:, 48:49], scalar1=stt[:, 52:53], scalar2=-1.0,
                                                op0=ALU.mult, op1=ALU.mult), reads=[r_st], writes=[r_st])
            P.op(DVE, lambda e: e.tensor_scalar(out=xn, in0=xt, scalar1=stt[:, 52:53], scalar2=stt[:, 53:54],
                                                op0=ALU.mult, op1=ALU.add), reads=[r_xt, r_st], writes=[r_xn])
            for g4 in range(4):
                pt, rpt = next_tr()
                for q in range(8):
                    kc = g4 * 8 + q
                    P.op(PE, (lambda e, pt=pt, q=q, kc=kc: e.transpose(out=pt[:, q * 128:(q + 1) * 128], in_=xn[:, kc * 128:(kc + 1) * 128],
                                                                       identity=ident_b)),
                         reads=[r_xn, r_ident_b], writes=[rpt], signal=(q == 7))
                for q in range(8):
                    kc = g4 * 8 + q
                    if True:
                        P.op(ACT, (lambda e, pt=pt, q=q, kc=kc: e.activation(out=hT[:, kc, t0:t0 + 128], in_=pt[:, q * 128:(q + 1) * 128],
                                                                             func=AF.Identity, scale=modT[:, 32 + kc, m:m + 1],
                                                                             bias=modT[:, kc, m:m + 1])),
                             reads=[rpt, r_mod], writes=[r_hT[0]], relax=(q > 0))
                    else:
                        P.op(DVE, (lambda e, pt=pt, q=q, kc=kc: e.tensor_scalar(out=hT[:, kc, t0:t0 + 128], in0=pt[:, q * 128:(q + 1) * 128],
                                                                               scalar1=modT[:, 32 + kc, m:m + 1], scalar2=modT[:, kc, m:m + 1],
                                                                               op0=ALU.mult, op1=ALU.add)),
                             reads=[rpt, r_mod], writes=[r_hT[1]])


        def make_ck_bufs():
            return dict(ck_f=[A.f32([128, 512]) for _ in range(2)], r_ckf=[Res(), Res()],
                        ck_b=[A.bf16([128, 512]) for _ in range(2)], r_ckb=[Res(), Res()],
                        kp_f=[A.f32([128, 64]) for _ in range(2)], r_kpf=[Res(), Res()],
                        kp_b=[A.bf16([128, 64]) for _ in range(2)], r_kpb=[Res(), Res()],
                        sm=[A.f32([128, 8]) for _ in range(2)], r_sm=[Res(), Res()], i=0)

        r_ockv = Res(); r_okpe = Res(); r_osre = Res(); r_osim = Res()

        def ckv_kpe_tile(B, hT, r_hT, t0, key0, out_row0, rope):
            i = B["i"] % 2
            B["i"] += 1
            ck_f, r_ckf, ck_b, r_ckb = B["ck_f"][i], B["r_ckf"][i], B["ck_b"][i], B["r_ckb"][i]
            kp_f, r_kpf, kp_b, r_kpb, s, r_s = B["kp_f"][i], B["r_kpf"][i], B["kp_b"][i], B["r_kpb"][i], B["sm"][i], B["r_sm"][i]
            ps1, r1 = (psf[3], r_psf[3]) if i == 0 else (psf[5], r_psf[5])
            ps2, r2 = psf[4], r_psf[4]
            for kc in range(KC):
                P.op(PE, (lambda e, kc=kc: e.matmul(ps1[:, 0:512], lhsT=hT[:, kc, t0:t0 + 128], rhs=wck[:, kc, 0:512],
                                                    start=(kc == 0), stop=(kc == KC - 1))),
                     reads=[*r_hT, r_wck], writes=[r1], signal=False)
                P.op(PE, (lambda e, kc=kc: e.matmul(ps2[:, 0:64], lhsT=hT[:, kc, t0:t0 + 128], rhs=wck[:, kc, 512:576],
                                                    start=(kc == 0), stop=(kc == KC - 1))),
                     reads=[*r_hT, r_wck], writes=[r2], signal=(kc == KC - 1))
            P.op(ACT, lambda e: e.activation(out=ck_b, in_=ps1[:, 0:512], func=AF.Square, accum_out=s[:, 0:1]),
                 reads=[r1], writes=[r_ckb, r_s])
            P.op(DVE, lambda e: e.tensor_scalar(out=s[:, 1:2], in0=s[:, 0:1], scalar1=1.0 / 512, scalar2=EPS, op0=ALU.mult, op1=ALU.add),
                 reads=[r_s], writes=[r_s])
            P.op(ACT, lambda e: e.activation(out=s[:, 2:3], in_=s[:, 1:2], func=AF.Sqrt), reads=[r_s], writes=[r_s])
            P.op(DVE, lambda e: e.reciprocal(out=s[:, 3:4], in_=s[:, 2:3]), reads=[r_s], writes=[r_s])
            P.op(DVE, lambda e: e.scalar_tensor_tensor(out=ck_f, in0=ps1[:, 0:512], scalar=s[:, 3:4], in1=gkb, op0=ALU.mult, op1=ALU.mult),
                 reads=[r1, r_gkb, r_s], writes=[r_ckf])
            P.op(DVE, lambda e: e.tensor_copy(out=ck_b, in_=ck_f), reads=[r_ckf], writes=[r_ckb])
            P.op(DVE, lambda e: e.tensor_copy(out=kp_f, in_=ps2[:, 0:64]), reads=[r2], writes=[r_kpf])
            if rope is None:
                P.op(DVE, lambda e: e.tensor_copy(out=kp_b, in_=kp_f), reads=[r_kpf], writes=[r_kpb])
            else:
                rope(kp_f, kp_b, r_kpf, r_kpb)
            if out_row0 is not None:
                P.dma(SP, lambda e: e.dma_start(out=o_ckv[out_row0:out_row0 + 128, :], in_=ck_f), reads=[r_ckf], writes=[r_ockv])
                P.dma(SP, lambda e: e.dma_start(out=o_kpe[out_row0:out_row0 + 128, :], in_=kp_f), reads=[r_kpf], writes=[r_okpe])
            pt, rpt = next_tr()
            for kc in range(4):
                P.op(PE, (lambda e, kc=kc: e.transpose(out=pt[:, kc * 128:(kc + 1) * 128], in_=ck_b[:, kc * 128:(kc + 1) * 128], identity=ident_b)),
                     reads=[r_ckb, r_ident_b], writes=[rpt], signal=False)
            P.op(PE, lambda e: e.transpose(out=pt[0:64, 512:640], in_=kp_b, identity=ident_b),
                 reads=[r_kpb, r_ident_b], writes=[rpt])
            P.op(ACT, lambda e: e.copy(out=ckvT_keys[:, :, key0:key0 + 128], in_=pt[:, 0:512].rearrange("p (k t) -> p k t", k=4)),
                 reads=[rpt], writes=[r_ckvT])
            P.op(ACT, lambda e: e.copy(out=kpeT_keys[0:64, key0:key0 + 128], in_=pt[0:64, 512:640]), reads=[rpt], writes=[r_kpeT])

        def make_rope(tab, r_tab, tmp, r_tmp, a):
            def rope(src, dst, r_src, r_dst):
                sv = src.rearrange("p (x h d) -> p x h d", x=2, h=2)
                dv = dst.rearrange("p (x h d) -> p x h d", x=2, h=2)
                cos = tab[:, a, 0:32].rearrange("p (x d) -> p x d", x=2)
                sin = tab[:, a, 32:64].rearrange("p (x d) -> p x d", x=2)
                x1, x2 = sv[:, :, 0, :], sv[:, :, 1, :]
                t = [tmp[:, k, :].rearrange("p (x d) -> p x d", x=2) for k in range(4)]
                P.op(DVE, lambda e: e.tensor_tensor(out=t[0], in0=x1, in1=cos, op=ALU.mult), reads=[r_src, r_tab], writes=[r_tmp])
                P.op(DVE, lambda e: e.tensor_tensor(out=t[1], in0=x2, in1=sin, op=ALU.mult), reads=[r_src, r_tab], writes=[r_tmp])
                P.op(DVE, lambda e: e.tensor_tensor(out=t[2], in0=x2, in1=cos, op=ALU.mult), reads=[r_src, r_tab], writes=[r_tmp])
                P.op(DVE, lambda e: e.tensor_tensor(out=t[3], in0=x1, in1=sin, op=ALU.mult), reads=[r_src, r_tab], writes=[r_tmp])
                P.op(DVE, lambda e: e.tensor_tensor(out=dv[:, :, 0, :], in0=t[0], in1=t[1], op=ALU.subtract), reads=[r_tmp], writes=[r_dst])
                P.op(DVE, lambda e: e.tensor_tensor(out=dv[:, :, 1, :], in0=t[2], in1=t[3], op=ALU.add), reads=[r_tmp], writes=[r_dst])
            return rope

        r_uall = Res("Uall")

        def ub_proj(hTx, r_hTx, ntok, ust, r_ust, dests):
            nj = ntok // 8
            for blk in range(8):
                wv, rw = load_w(w_in, 0, KC, C_UB + blk * 256, 256)
                u = ust[blk % 2]
                ru = r_ust[blk % 2]
                for tau in range(8):
                    ps, rps = next_mm()
                    for kc in range(KC):
                        P.op(PE, (lambda e, ps=ps, wv=wv, kc=kc, tau=tau: e.matmul(ps[0:nj, 0:256], lhsT=hTx[:, kc, tau:ntok:8], rhs=wv[:, kc, :],
                                                                                 start=(kc == 0), stop=(kc == KC - 1))),
                             reads=[rw, *r_hTx], writes=[rps], signal=(kc == KC - 1))
                    evac_copy(u[0:nj, :, tau * 16:(tau + 1) * 16], ps[0:nj, 0:256].rearrange("p (g c) -> p g c", c=16), [rps], [ru])
                for (r0, r1, d0) in dests:
                    P.dma(SP, (lambda e, u=u, r0=r0, r1=r1, d0=d0, blk=blk: e.dma_start(
                        out=Uall[d0:d0 + (r1 - r0), blk * 16:(blk + 1) * 16, :], in_=u[r0:r1, :, :])), reads=[ru], writes=[r_uall])

        def finish(extra=()):
            P.barrier()
            P.wait_on(SP, [r_ockv, r_okpe, r_gate_d, r_uall] + list(extra))
            P.wait_on(ACT, [r_ckvT, r_kpeT])
            P.emit()
            raise _Done()

        if STAGE == 0:
            finish()
        A.mark()
        lnB = make_ln_bufs()
        ckB = make_ck_bufs()
        ropek = A.f32([128, 8, 64]); r_ropek = Res()
        rtmp = A.f32([128, 4, 32]); r_rtmp = Res()
        P.dma(SP, lambda e: e.dma_start(out=ropek, in_=rope_k.rearrange("(a p) d -> p a d", p=128)), writes=[r_ropek])
        hTa = A.bf16([128, KC, 512]); r_hTa = [Res("hTa0"), Res("hTa1")]
        ustA = [A.bf16([64, 16, 128]) for _ in range(2)]; r_ustA = [Res(), Res()]
        bgb = [A.f32([2, 256]) for _ in range(2)]; r_bgb = [Res(), Res()]
        gsb = [A.f32([2, 256]) for _ in range(2)]; r_gsb = [Res(), Res()]
        for half in range(2):
            for t in range(4):
                ln_tile(lnB, x_aux, half * 512 + t * 128, hTa, r_hTa, t * 128, 1)
            if STAGE == 11:
                finish()
            for t in range(4):
                a = half * 4 + t
                ckv_kpe_tile(ckB, hTa, r_hTa, t * 128, 512 + a * 128, None, make_rope(ropek, r_ropek, rtmp, r_rtmp, a))
            if STAGE == 12:
                finish()
            ub_proj(hTa, r_hTa, 512, ustA, r_ustA, [(0, 64, 64 + half * 64)])
            if half == 0:
                for blk in range(16):
                    gi_ = blk % 2
                    cs_ = slice(blk * 256, (blk + 1) * 256)
                    P.dma(SP, (lambda e, gi_=gi_, cs_=cs_: e.dma_start(out=bgb[gi_], in_=b_gate[:, cs_])), writes=[r_bgb[gi_]])
                    wv, rw = load_w(w_ada, 0, KC, 2 * D + blk * 256, 256)
                    ps, rps = next_mm()
                    for kc in range(KC):
                        P.op(PE, (lambda e, ps=ps, wv=wv, kc=kc: e.matmul(ps[0:2, 0:256], lhsT=scT[:, kc, :], rhs=wv[:, kc, :],
                                                                         start=(kc == 0), stop=(kc == KC - 1))),
                             reads=[rw, r_scT], writes=[rps], signal=(kc == KC - 1))
                    P.op(DVE, (lambda e, ps=ps, gi_=gi_: e.tensor_tensor(out=gsb[gi_], in0=ps[0:2, 0:256], in1=bgb[gi_], op=ALU.add)),
                         reads=[rps, r_bgb[gi_]], writes=[r_gsb[gi_]])
                    P.dma(SP, (lambda e, gi_=gi_, cs_=cs_: e.dma_start(out=gate_d[:, cs_], in_=gsb[gi_])), reads=[r_gsb[gi_]], writes=[r_gate_d])
            if STAGE == 13:
                finish()
        for t in range(4):
            i = t % 2
            ck_f, r_ckf, ck_b, r_ckb = ckB["ck_f"][i], ckB["r_ckf"][i], ckB["ck_b"][i], ckB["r_ckb"][i]
            kp_f, r_kpf, kp_b, r_kpb = ckB["kp_f"][i], ckB["r_kpf"][i], ckB["kp_b"][i], ckB["r_kpb"][i]
            P.dma(SP, (lambda e, t=t, ck_f=ck_f: e.dma_start(out=ck_f, in_=cache_ckv[t * 128:(t + 1) * 128, :])), writes=[r_ckf])
            P.dma(SP, (lambda e, t=t, kp_f=kp_f: e.dma_start(out=kp_f, in_=cache_kpe[t * 128:(t + 1) * 128, :])), writes=[r_kpf])
            P.op(DVE, (lambda e, ck_f=ck_f, ck_b=ck_b: e.tensor_copy(out=ck_b, in_=ck_f)), reads=[r_ckf], writes=[r_ckb])
            P.op(DVE, (lambda e, kp_f=kp_f, kp_b=kp_b: e.tensor_copy(out=kp_b, in_=kp_f)), reads=[r_kpf], writes=[r_kpb])
            pt, rpt = next_tr()
            for kc in range(4):
                P.op(PE, (lambda e, kc=kc, pt=pt, ck_b=ck_b: e.transpose(out=pt[:, kc * 128:(kc + 1) * 128], in_=ck_b[:, kc * 128:(kc + 1) * 128],
                                                                        identity=ident_b)),
                     reads=[r_ckb, r_ident_b], writes=[rpt], signal=False)
            P.op(PE, (lambda e, pt=pt, kp_b=kp_b: e.transpose(out=pt[0:64, 512:640], in_=kp_b, identity=ident_b)),
                 reads=[r_kpb, r_ident_b], writes=[rpt])
            key0 = 1536 + t * 128
            P.op(ACT, (lambda e, pt=pt, key0=key0: e.copy(out=ckvT_keys[:, :, key0:key0 + 128],
                                                         in_=pt[:, 0:512].rearrange("p (k t) -> p k t", k=4))), reads=[rpt], writes=[r_ckvT])
            P.op(ACT, (lambda e, pt=pt, key0=key0: e.copy(out=kpeT_keys[0:64, key0:key0 + 128], in_=pt[0:64, 512:640])), reads=[rpt], writes=[r_kpeT])
        if STAGE == 1:
            finish()
        P.barrier()
        A.release()

        cqnT = A.hi([128, 8, TO], BF16); r_cqn = Res("cqn")
        A.mark()
        hT = A.bf16([128, KC, TO]); r_hT = [Res("hT0"), Res("hT1")]
        A.mark()
        lnB = make_ln_bufs(1)
        ckB = make_ck_bufs()
        for t in range(6):
            ln_tile(lnB, x_own, t * 128, hT, r_hT, t * 128, 0 if t < 4 else 1)
        for t in range(4):
            ckv_kpe_tile(ckB, hT, r_hT, t * 128, t * 128, t * 128, None)
        P.barrier()
        A.release()

        if STAGE == 2:
            finish()
        SEGS = [(0, 512), (512, 256)]

        def linear_fm(W, c0, nchunks, xT, r_xT, nk, evac):
            per = max(1, min(2, 8192 // (nk * 128) // 1))
            per = 2 if nk * 256 <= 8192 else 1
            ch = 0
            while ch < nchunks:
                nb = min(per, nchunks - ch)
                wv, rw = load_w(W, 0, nk, c0 + ch * 128, nb * 128)
                for c2 in range(nb):
                    for (n0, n) in SEGS:
                        ps, rps = next_mm()
                        for kc in range(nk):
                            P.op(PE, (lambda e, ps=ps, wv=wv, kc=kc, c2=c2, n0=n0, n=n: e.matmul(
                                ps[:, 0:n], lhsT=wv[:, kc, c2 * 128:(c2 + 1) * 128], rhs=xT[:, kc, n0:n0 + n],
                                start=(kc == 0), stop=(kc == nk - 1))),
                                reads=[rw, *r_xT], writes=[rps], signal=(kc == nk - 1))
                        evac(ch + c2, n0, n, ps, rps)
                ch += nb

        A.mark()
        sq = A.bf16([128, 8, TO]); r_sq = Res()
        rstd_b = A.f32([128, TO]); r_rstd = Res()
        gq = A.f32([128, 8]); r_gq = Res()
        stage = [A.bf16([128, TO]) for _ in range(3)]; r_stage = [Res() for _ in range(3)]
        ustO = [A.bf16([96, 16, 128]) for _ in range(2)]; r_ustO = [Res(), Res()]
        P.dma(SP, lambda e: e.dma_start(out=gq, in_=g_qn_fm), writes=[r_gq])

        def evac_cq(ch, n0, n, ps, rps):
            P.op(ACT, lambda e: e.copy(out=cqnT[:, ch, n0:n0 + n], in_=ps[:, 0:n]), reads=[rps], writes=[r_cqn])
            P.op(ACT, lambda e: e.activation(out=sq[:, ch, n0:n0 + n], in_=ps[:, 0:n], func=AF.Square), reads=[rps], writes=[r_sq])

        linear_fm(w_in, C_CQ, 8, hT, r_hT, KC, evac_cq)
        for (n0, n) in SEGS:
            ps, rps = next_mm()
            for ch in range(8):
                P.op(PE, (lambda e, ps=ps, ch=ch, n0=n0, n=n: e.matmul(ps[:, 0:n], lhsT=ones_b, rhs=sq[:, ch, n0:n0 + n],
                                                                      start=(ch == 0), stop=(ch == 7))),
                     reads=[r_ones, r_sq], writes=[rps], signal=(ch == 7))
            P.op(DVE, (lambda e, ps=ps, n0=n0, n=n: e.tensor_scalar(out=rstd_b[:, n0:n0 + n], in0=ps[:, 0:n], scalar1=1.0 / 1024, scalar2=EPS,
                                                                   op0=ALU.mult, op1=ALU.add)), reads=[rps], writes=[r_rstd])
        P.op(ACT, lambda e: e.activation(out=rstd_b, in_=rstd_b, func=AF.Sqrt), reads=[r_rstd], writes=[r_rstd])
        P.op(DVE, lambda e: e.reciprocal(out=rstd_b, in_=rstd_b), reads=[r_rstd], writes=[r_rstd])
        for ch in range(8):
            P.op(DVE, (lambda e, ch=ch: e.scalar_tensor_tensor(out=cqnT[:, ch, :], in0=cqnT[:, ch, :], scalar=gq[:, ch:ch + 1], in1=rstd_b,
                                                              op0=ALU.mult, op1=ALU.mult)), reads=[r_cqn, r_gq, r_rstd], writes=[r_cqn])

        if STAGE == 3:
            finish()
        stc = {"i": 0}

        def make_evac_act(func, dst, r_dst):
            def evac(ch, n0, n, ps, rps):
                i = stc["i"] % 3
                sg, rsg = stage[i], r_stage[i]
                P.op(ACT, lambda e: e.activation(out=sg[:, n0:n0 + n], in_=ps[:, 0:n], func=func), reads=[rps], writes=[rsg])
                if n0 + n == TO:
                    P.dma(SP, lambda e: e.dma_start(out=dst[ch], in_=sg), reads=[rsg], writes=[r_dst])
                    stc["i"] += 1
            return evac

        r_za, r_zb, r_ga, r_gb = Res("ZA"), Res("ZB"), Res("GA"), Res("GB")
        linear_fm(w_in, C_ZA, 16, hT, r_hT, KC, make_evac_act(AF.Silu, ZA, r_za))
        if STAGE == 4:
            finish([r_za])
        ub_proj(hT, r_hT, TO, ustO, r_ustO, [(0, 64, 0), (64, 96, 192)])
        if STAGE == 5:
            finish([r_za])
        linear_fm(w_in, C_ZB, 16, hT, r_hT, KC, make_evac_act(AF.Silu, ZB, r_zb))
        linear_fm(w_in, C_GA, 32, hT, r_hT, KC, make_evac_act(AF.Sigmoid, GA, r_ga))
        linear_fm(w_in, C_GB, 32, hT, r_hT, KC, make_evac_act(AF.Sigmoid, GB, r_gb))
        P.barrier()
        A.release()
        A.release()

        P.barrier()
        SAVE_TOP = A.top
        A.top = OVERLAY0
        s5yT = A.hi([128, 16, TO], BF16); r_s5y = Res("s5y")
        I32 = mybir.dt.int32
        NK = 59
        G8 = 8
        TWO_PI = 6.283185307179586

        def OP(eng, meth, reads, writes, relax=False, **kw):
            P.op(eng, (lambda e, meth=meth, kw=kw: getattr(e, meth)(**kw)), reads=reads, writes=writes, relax=relax)

        def TT(eng, out, in0, in1, op, reads, writes):
            OP(eng, "tensor_tensor", reads, writes, out=out, in0=in0, in1=in1, op=op)

        def TS(eng, out, in0, s1, s2, op0, op1, reads, writes):
            if s2 is None:
                OP(eng, "tensor_scalar", reads, writes, out=out, in0=in0, scalar1=s1, scalar2=None, op0=op0)
            else:
                OP(eng, "tensor_scalar", reads, writes, out=out, in0=in0, scalar1=s1, scalar2=s2, op0=op0, op1=op1)

        def bc(ap, axis, shape):
            return ap.unsqueeze(axis).broadcast_to(shape)

        kt_sb = A.f32([128, 2, NK]); sgn = A.f32([128, 1]); maskM = A.f32([128, 2, 128]); dcol = A.f32([128, 128])
        h0_sb = A.f32([64, 2, 2, 128]); oh_sb = A.f32([64, 4]); FIN = A.f32([64, 2, 2, 2, 128])
        r_c5 = Res("c5"); r_fin = Res("fin")
        for dst, src in [(kt_sb, kt_in), (sgn, sgn_in), (maskM, mask_in), (dcol, dcol_in), (h0_sb, h0_in), (oh_sb, oh_in)]:
            P.dma(SP, (lambda e, dst=dst, src=src: e.dma_start(out=dst, in_=src)), writes=[r_c5])
        A3 = A.f32([128, 3, 2, G8]); B12 = A.f32([128, 2, 2, G8 * 16]); C12 = A.f32([128, 2, 2, G8 * 16]); r_par = Res("par")
        T = [A.f32([128, 16, NK]) for _ in range(5)]
        TI = A.f32([128, 16, NK]).bitcast(I32)
        PR = A.f32([128, 16, NK]); PI = A.f32([128, 16, NK])
        PRSq = A.f32([128, 16, 16]); PISp = A.f32([128, 16, 8])
        sm5 = A.f32([128, 12, 16])
        BB = A.f32([128, 16, 16]); BBs = A.f32([128, 16, 16])
        r_tab = Res("tab")
        Pt = A.f32([128, G8, 128]); Qt = A.f32([128, G8, 128]); Ft = A.f32([128, G8, 128]); F2t = A.f32([128, G8, 128])
        t1 = A.f32([128, G8, 128]); t2 = A.f32([128, G8, 128]); r_pq = Res("pq")
        Mb = A.bf16([128, 2, G8, 128]); Eb = A.bf16([128, 2, G8, 128]); Fb = A.bf16([128, 2, G8, 128]); F2b = A.bf16([128, 2, G8, 128])
        r_M = Res("M"); r_E = Res("E"); r_F = Res("F")
        Mtmp = A.f32([128, 4, 128]); r_mtmp = Res()
        U1 = A.bf16([128, 8, 128]); U2 = A.bf16([128, 8, 128]); r_U1 = Res(); r_U2 = Res()
        Ug = A.bf16([128, G8, 224]); r_Ug = Res("Ug")
        Sloc = [A.f32([64, 2, G8, 7, 32]) for _ in range(2)]; r_S = [Res("S0"), Res("S1")]
        stA = [A.f32([64, 2, G8, 7]) for _ in range(2)]; stB = [A.f32([64, 2, G8, 7]) for _ in range(2)]
        XE = [A.f32([64, 4, 2, G8]) for _ in range(2)]; Xin = [A.f32([64, 2, G8]) for _ in range(2)]
        ct1 = [A.f32([64, 2, G8]) for _ in range(2)]; ct2 = [A.f32([64, 2, G8]) for _ in range(2)]
        W1 = [A.f32([64, 2, G8, 32]) for _ in range(2)]; W2 = [A.f32([64, 2, G8, 32]) for _ in range(2)]
        Xp = A.bf16([64, 2, 2, G8, 96]); r_Xp = [Res("Xp0"), Res("Xp1")]
        Ysb = A.f32([128, G8, 96]); r_Ysb = Res()
        YTM = A.f32([96, 8, 128]); r_YTM = Res()
        SE = [DVE, POOL]

        for gb in range(16):
            g0 = gb * G8
            for d in range(2):
                c0 = d * 128 + g0
                P.dma(SP, (lambda e, d=d, c0=c0: e.dma_start(out=A3[:, :, d, :], in_=s5A[:, :, c0:c0 + G8])), writes=[r_par])
                for w in range(2):
                    P.dma(SP, (lambda e, d=d, w=w, c0=c0: e.dma_start(out=B12[:, w, d, :].rearrange("p (g c) -> p g c", c=16),
                                                                     in_=s5B[:, w, c0:c0 + G8, :])), writes=[r_par])
                    P.dma(SP, (lambda e, d=d, w=w, c0=c0: e.dma_start(out=C12[:, w, d, :].rearrange("p (g c) -> p g c", c=16),
                                                                     in_=s5C[:, w, c0:c0 + G8, :])), writes=[r_par])
            RT = [r_par, r_c5, r_tab]
            WT = [r_tab]
            are = A3[:, 0].rearrange("p d g -> p (d g)"); aim = A3[:, 1].rearrange("p d g -> p (d g)"); ldt = A3[:, 2].rearrange("p d g -> p (d g)")
            B1 = B12[:, 0].rearrange("p d (g c) -> p (d g) c", c=16); B2 = B12[:, 1].rearrange("p d (g c) -> p (d g) c", c=16)
            C1 = C12[:, 0].rearrange("p d (g c) -> p (d g) c", c=16); C2 = C12[:, 1].rearrange("p d (g c) -> p (d g) c", c=16)
            dt_, dar, dai = sm5[:, 0], sm5[:, 1], sm5[:, 2]
            OP(ACT, "activation", RT, WT, out=dt_, in_=ldt, func=AF.Exp)
            TT(DVE, dar, dt_, are, ALU.mult, RT, WT)
            TT(DVE, dai, dt_, aim, ALU.mult, RT, WT)
            ktb = kt_sb.unsqueeze(2).broadcast_to([128, 2, G8, NK])
            v4 = lambda x: x.rearrange("p (d g) k -> p d g k", d=2)
            c4 = lambda x: x.rearrange("p (d g) -> p d g", d=2).unsqueeze(3).broadcast_to([128, 2, G8, NK])
            TT(DVE, v4(T[0]), c4(dai), ktb, ALU.mult, RT, WT)
            TT(DVE, v4(T[1]), c4(dar), ktb, ALU.mult, RT, WT)
            OP(ACT, "activation", RT, WT, out=T[1], in_=T[1], func=AF.Exp)
            for (dstT, off) in [(PI, 64.0), (PR, 64.25)]:
                TS(DVE, T[2], T[0], 1.0 / TWO_PI, off, ALU.mult, ALU.add, RT, WT)
                OP(DVE, "tensor_copy", RT, WT, out=TI, in_=T[2])
                OP(DVE, "tensor_copy", RT, WT, out=T[3], in_=TI)
                TT(DVE, T[2], T[2], T[3], ALU.subtract, RT, WT)
                TS(DVE, T[3], T[2], 0.5, None, ALU.is_gt, None, RT, WT)
                TT(DVE, T[2], T[2], T[3], ALU.subtract, RT, WT)
                OP(ACT, "activation", RT, WT, out=T[4], in_=T[2], func=AF.Sin, scale=6.28318)
                TT(DVE, dstT, T[1], T[4], ALU.mult, RT, WT)
            TS(DVE, PRSq, PR[:, :, 8:24], sgn[:, 0:1], None, ALU.mult, None, RT, WT)
            TS(DVE, PISp, PI[:, :, 0:8], sgn[:, 0:1], None, ALU.mult, None, RT, WT)
            abr, abi = PR[:, :, 24], PI[:, :, 24]
            den, pre, qre, qim, u1, u2 = sm5[:, 3], sm5[:, 4], sm5[:, 5], sm5[:, 6], sm5[:, 7], sm5[:, 8]
            TT(DVE, den, are, are, ALU.mult, RT, WT)
            TT(DVE, u1, aim, aim, ALU.mult, RT, WT)
            TT(DVE, den, den, u1, ALU.add, RT, WT)
            OP(DVE, "reciprocal", RT, WT, out=den, in_=den)
            TS(DVE, pre, abr, -1.0, None, ALU.add, None, RT, WT)
            TT(DVE, u1, pre, are, ALU.mult, RT, WT)
            TT(DVE, u2, abi, aim, ALU.mult, RT, WT)
            TT(DVE, u1, u1, u2, ALU.add, RT, WT)
            TT(DVE, qre, u1, den, ALU.mult, RT, WT)
            TT(DVE, u1, abi, are, ALU.mult, RT, WT)
            TT(DVE, u2, pre, aim, ALU.mult, RT, WT)
            TT(DVE, u1, u1, u2, ALU.subtract, RT, WT)
            TT(DVE, qim, u1, den, ALU.mult, RT, WT)
            TS(DVE, qim, qim, sgn[:, 0:1], None, ALU.mult, None, RT, WT)
            qreb = qre.unsqueeze(2).broadcast_to([128, 16, 16]); qimb = qim.unsqueeze(2).broadcast_to([128, 16, 16])
            x1 = T[2][:, :, 0:16]; x2 = T[2][:, :, 16:32]
            TT(DVE, x1, qreb, B1, ALU.mult, RT, WT)
            TT(DVE, x2, qimb, B2, ALU.mult, RT, WT)
            TT(DVE, BB, x1, x2, ALU.add, RT, WT)
            TT(DVE, x1, qreb, B2, ALU.mult, RT, WT)
            TT(DVE, x2, qimb, B1, ALU.mult, RT, WT)
            TT(DVE, BBs, x1, x2, ALU.subtract, RT, WT)

            for d in range(2):
                dsl = slice(d * G8, (d + 1) * G8)
                kb = lambda tab, k0: tab[:, dsl, k0:k0 + 8].unsqueeze(3).broadcast_to([128, G8, 8, 16])
                cb = lambda tab: tab[:, dsl, :].unsqueeze(2).broadcast_to([128, G8, 8, 16])
                v = lambda x: x.rearrange("p g (s c) -> p g s c", c=16)
                RP = [r_tab, r_pq, r_par]
                WP = [r_pq]
                TT(DVE, v(t1), kb(PR, 0), cb(BB), ALU.mult, RP, WP)
                TT(DVE, v(t2), kb(PISp, 0), cb(BBs), ALU.mult, RP, WP)
                TT(DVE, Pt, t1, t2, ALU.add, RP, WP)
                TT(DVE, v(t1), kb(PRSq, 0), cb(C1), ALU.mult, RP, WP)
                TT(DVE, v(t2), kb(PI, 8), cb(C2), ALU.mult, RP, WP)
                OP(DVE, "scalar_tensor_tensor", RP, WP, out=Qt, in0=t1, scalar=-1.0, in1=t2, op0=ALU.mult, op1=ALU.subtract)
                TT(DVE, v(t1), kb(PRSq, 8), cb(C1), ALU.mult, RP, WP)
                TT(DVE, v(t2), kb(PI, 16), cb(C2), ALU.mult, RP, WP)
                OP(DVE, "scalar_tensor_tensor", RP, WP, out=Ft, in0=t1, scalar=-1.0, in1=t2, op0=ALU.mult, op1=ALU.subtract)
                TT(DVE, v(t1), kb(PRSq, 8), cb(C2), ALU.mult, RP, WP)
                TT(DVE, v(t2), kb(PI, 16), cb(C1), ALU.mult, RP, WP)
                TT(DVE, F2t, t1, t2, ALU.subtract, RP, WP)
                OP(ACT, "copy", [r_pq], [r_F], out=Fb[0:64, d], in_=Ft[0:64])
                OP(ACT, "copy", [r_pq], [r_F], out=F2b[0:64, d], in_=F2t[0:64])
                for h4 in range(2):
                    ps, rps = next_mm()
                    for i4 in range(4):
                        i = h4 * 4 + i4
                        P.op(PE, (lambda e, ps=ps, i=i, i4=i4: e.matmul(ps[:, i4 * 128:(i4 + 1) * 128], lhsT=Pt[:, i, :], rhs=Qt[:, i, :],
                                                                       start=True, stop=True)),
                             reads=[r_pq], writes=[rps], signal=(i4 == 3))
                    TT(DVE, Mtmp, ps[:, 0:512].rearrange("p (i m) -> p i m", i=4), maskM[:, d, :].unsqueeze(1).broadcast_to([128, 4, 128]),
                       ALU.mult, [rps, r_c5], [r_mtmp])
                    for i4 in range(4):
                        i = h4 * 4 + i4
                        if d == 0:
                            OP(DVE, "scalar_tensor_tensor", [r_mtmp, r_c5, r_ident_f], [r_M], out=Mb[:, d, i, :], in0=ident_f,
                               scalar=dcol[:, g0 + i:g0 + i + 1], in1=Mtmp[:, i4, :], op0=ALU.mult, op1=ALU.add)
                        elif i4 == 0:
                            OP(ACT, "copy", [r_mtmp], [r_M], out=Mb[:, d, h4 * 4:(h4 + 1) * 4, :], in_=Mtmp)
                    ps, rps = next_mm()
                    for i4 in range(4):
                        i = h4 * 4 + i4
                        P.op(PE, (lambda e, ps=ps, i=i, i4=i4: e.transpose(out=ps[:, i4 * 128:(i4 + 1) * 128], in_=Pt[:, i, :], identity=ident_f)),
                             reads=[r_pq, r_ident_f], writes=[rps], signal=(i4 == 3))
                    OP(ACT, "copy", [rps], [r_E], out=Eb[:, d, h4 * 4:(h4 + 1) * 4, :], in_=ps[:, 0:512].rearrange("p (i m) -> p i m", i=4))

            P.dma(SP, (lambda e, g0=g0: e.dma_start(out=U1, in_=Uall[0:128, g0:g0 + G8, :])), reads=[r_uall], writes=[r_U1])
            P.dma(SP, (lambda e, g0=g0: e.dma_start(out=U2[0:96], in_=Uall[128:224, g0:g0 + G8, :])), reads=[r_uall], writes=[r_U2])
            for h4 in range(2):
                pt, rpt = next_tr()
                for i4 in range(4):
                    i = h4 * 4 + i4
                    P.op(PE, (lambda e, pt=pt, i=i, i4=i4: e.transpose(out=pt[:, i4 * 224:i4 * 224 + 128], in_=U1[:, i, :], identity=ident_b)),
                         reads=[r_U1, r_ident_b], writes=[rpt], signal=False)
                    P.op(PE, (lambda e, pt=pt, i=i, i4=i4: e.transpose(out=pt[:, i4 * 224 + 128:i4 * 224 + 224], in_=U2[0:96, i, :],
                                                                      identity=ident_b[0:96, 0:96])),
                         reads=[r_U2, r_ident_b], writes=[rpt], signal=(i4 == 3))
                OP(ACT, "copy", [rpt], [r_Ug], out=Ug[:, h4 * 4:(h4 + 1) * 4, :], in_=pt[:, 0:896].rearrange("p (i j) -> p i j", i=4))

            for d in range(2):
                for i in range(G8):
                    ps, rps = next_mm()
                    P.op(PE, (lambda e, ps=ps, d=d, i=i: e.matmul(ps[0:64, 0:224], lhsT=Eb[:, d, i, 0:64], rhs=Ug[:, i, :], start=True, stop=True)),
                         reads=[r_E, r_Ug], writes=[rps], signal=False)
                    P.op(PE, (lambda e, ps=ps, d=d, i=i: e.matmul(ps[0:64, 224:448], lhsT=Eb[:, d, i, 64:128], rhs=Ug[:, i, :], start=True, stop=True)),
                         reads=[r_E, r_Ug], writes=[rps], signal=True)
                    OP(ACT, "copy", [rps], [r_S[d]], relax=(i > 0), out=Sloc[d][:, :, i].rearrange("p r s j -> p r (s j)"),
                       in_=ps[0:64, 0:448].rearrange("p (r j) -> p r j", r=2))

            for d in range(2):
                E_ = SE[d]
                S = Sloc[d]
                RS = [r_S[d], r_tab, r_c5]
                WS = [r_S[d]]
                dsl = slice(d * G8, (d + 1) * G8)
                a8r = PR[0:64, dsl, 25]; a8i = PI[0:64, dsl, 25]
                a8rb = a8r.unsqueeze(1).unsqueeze(3).broadcast_to([64, 2, G8, 7])
                a8ib = a8i.unsqueeze(1).unsqueeze(3).broadcast_to([64, 2, G8, 7])
                tA, tB = stA[d], stB[d]
                OP(E_, "tensor_copy", RS, WS, out=ct2[d][:, 0], in_=a8i)
                TS(E_, ct2[d][:, 1], a8i, -1.0, None, ALU.mult, None, RS, WS)
                a8isb = ct2[d].unsqueeze(3).broadcast_to([64, 2, G8, 7])
                steps = range(1, 32) if d == 0 else range(30, -1, -1)
                for j in steps:
                    jp = j - 1 if d == 0 else j + 1
                    Xprev = S[:, :, :, :, jp]
                    Xc = S[:, :, :, :, j]
                    TT(E_, tA, a8rb, Xprev, ALU.mult, RS, WS)
                    TT(E_, tB, a8isb, Xprev, ALU.mult, RS, WS)
                    TT(E_, Xc, Xc, tA, ALU.add, RS, WS)
                    TT(E_, Xc, Xc, tB[:, ::-1], ALU.add, RS, WS)
                jl = 31 if d == 0 else 0
                a256r = PR[0:64, dsl, 26].unsqueeze(1).broadcast_to([64, 2, G8]); a256i = PI[0:64, dsl, 26].unsqueeze(1).broadcast_to([64, 2, G8])
                OP(E_, "tensor_copy", RS, WS, out=XE[d][:, 0], in_=h0_sb[:, d, :, g0:g0 + G8])
                for k in range(3):
                    qq = k if d == 0 else 3 - k
                    TT(E_, ct1[d], a256r, XE[d][:, k], ALU.mult, RS, WS)
                    TT(E_, ct2[d], a256i, XE[d][:, k], ALU.mult, RS, WS)
                    TT(E_, XE[d][:, k + 1], ct1[d], S[:, :, :, 2 + qq, jl], ALU.add, RS, WS)
                    TT(E_, XE[d][:, k + 1, 0], XE[d][:, k + 1, 0], ct2[d][:, 1], ALU.subtract, RS, WS)
                    TT(E_, XE[d][:, k + 1, 1], XE[d][:, k + 1, 1], ct2[d][:, 0], ALU.add, RS, WS)
                for q in range(4):
                    k = q if d == 0 else 3 - q
                    if q == 0:
                        TS(E_, Xin[d], XE[d][:, k], oh_sb[:, q:q + 1], None, ALU.mult, None, RS, WS)
                    else:
                        TS(E_, ct1[d], XE[d][:, k], oh_sb[:, q:q + 1], None, ALU.mult, None, RS, WS)
                        TT(E_, Xin[d], Xin[d], ct1[d], ALU.add, RS, WS)
                prj = PR[0:64, dsl, 27:59].unsqueeze(1).broadcast_to([64, 2, G8, 32]); pij = PI[0:64, dsl, 27:59].unsqueeze(1).broadcast_to([64, 2, G8, 32])
                xib = Xin[d].unsqueeze(3).broadcast_to([64, 2, G8, 32])
                So = S[:, :, :, 6, :]
                TT(E_, W1[d], prj, xib, ALU.mult, RS, WS)
                TT(E_, W2[d], pij, xib, ALU.mult, RS, WS)
                TT(E_, So, So, W1[d], ALU.add, RS, WS)
                TT(E_, So[:, 0], So[:, 0], W2[d][:, 1], ALU.subtract, RS, WS)
                TT(E_, So[:, 1], So[:, 1], W2[d][:, 0], ALU.add, RS, WS)
                RX = RS + [r_Xp[d]]
                WX = [r_Xp[d], r_S[d]]
                xp = Xp[:, d]
                P.op(E_, (lambda e, xp=xp: e.memset(xp, 0.0)), reads=RX, writes=WX)
                for r in range(2):
                    src_p = S[:, r, :, 0:2, 0:31] if d == 0 else S[:, r, :, 0:2, 1:32]
                    dst_p = xp[:, r, :, 0:64].rearrange("p g (s j) -> p g s j", s=2)
                    dst_p = dst_p[:, :, :, 1:32] if d == 0 else dst_p[:, :, :, 0:31]
                    OP(E_, "tensor_copy", RX, WX, out=dst_p, in_=src_p)
                src_o = S[:, :, :, 6, 0:31] if d == 0 else S[:, :, :, 6, 1:32]
                dst_o = xp[:, :, :, 65:96] if d == 0 else xp[:, :, :, 64:95]
                OP(E_, "tensor_copy", RX, WX, out=dst_o, in_=src_o)
                jo = 64 if d == 0 else 95
                OP(E_, "tensor_copy", RX, WX, out=xp[:, :, :, jo], in_=Xin[d])
                OP(E_, "tensor_copy", RX + [r_fin], WX + [r_fin], out=FIN[:, :, d, :, g0:g0 + G8],
                   in_=S[:, :, :, 0:2, jl].rearrange("p r g s -> p s r g"))

            for h4 in range(2):
                ps, rps = next_mm()
                for i4 in range(4):
                    i = h4 * 4 + i4
                    for (cs, ce, os_) in [(0, 64, 0), (192, 224, 64)]:
                        n = ce - cs
                        oc = i4 * 96 + os_
                        for d in range(2):
                            last = (d == 1)
                            P.op(PE, (lambda e, ps=ps, d=d, i=i, oc=oc, n=n, cs=cs, ce=ce: e.matmul(ps[:, oc:oc + n], lhsT=Mb[:, d, i, :], rhs=Ug[:, i, cs:ce],
                                                                                                     start=(d == 0), stop=False)),
                                 reads=[r_M, r_Ug], writes=[rps], signal=False)
                            P.op(PE, (lambda e, ps=ps, d=d, i=i, oc=oc, n=n, os_=os_: e.matmul(ps[:, oc:oc + n], lhsT=Fb[0:64, d, i, :],
                                                                                              rhs=Xp[0:64, d, 0, i, os_:os_ + n], start=False, stop=False)),
                                 reads=[r_F, r_Xp[d]], writes=[rps], signal=False)
                            P.op(PE, (lambda e, ps=ps, d=d, i=i, oc=oc, n=n, os_=os_, last=last: e.matmul(ps[:, oc:oc + n], lhsT=F2b[0:64, d, i, :],
                                                                                                          rhs=Xp[0:64, d, 1, i, os_:os_ + n], start=False, stop=last)),
                                 reads=[r_F, r_Xp[d]], writes=[rps], signal=(last and i4 == 3 and os_ == 64))
                OP(ACT, "copy", [rps], [r_Ysb], out=Ysb[:, h4 * 4:(h4 + 1) * 4, :], in_=ps[:, 0:384].rearrange("p (i j) -> p i j", i=4))
                ps, rps = next_mm()
                for i4 in range(4):
                    i = h4 * 4 + i4
                    P.op(PE, (lambda e, ps=ps, i=i, i4=i4: e.transpose(out=ps[0:96, i4 * 128:(i4 + 1) * 128], in_=Ysb[:, i, :], identity=ident_f)),
                         reads=[r_Ysb, r_ident_f], writes=[rps], signal=(i4 == 3))
                OP(ACT, "copy", [rps], [r_YTM], out=YTM[0:96, :, h4 * 64:(h4 + 1) * 64].rearrange("p t (i c) -> p i t c", c=16),
                   in_=ps[0:96, 0:512].rearrange("p (i t c) -> p i t c", i=4, c=16))
            for h in range(2):
                ps, rps = next_mm()
                for t4 in range(4):
                    tau = h * 4 + t4
                    P.op(PE, (lambda e, ps=ps, tau=tau, t4=t4: e.transpose(out=ps[:, t4 * 96:(t4 + 1) * 96], in_=YTM[0:96, tau, :], identity=ident_f[0:96, 0:96])),
                         reads=[r_YTM, r_ident_f], writes=[rps], signal=(t4 == 3))
                OP(ACT, "copy", [rps], [r_s5y], out=s5yT[:, gb, :].rearrange("p (r t) -> p t r", t=8)[:, h * 4:(h + 1) * 4, :],
                   in_=ps[:, 0:384].rearrange("p (t r) -> p t r", t=4))

        for sqi in range(2):
            for d in range(2):
                ps, rps = next_mm()
                for r in range(2):
                    P.op(PE, (lambda e, ps=ps, sq=sqi, d=d, r=r: e.transpose(out=ps[:, r * 64:(r + 1) * 64], in_=FIN[0:64, sq, d, r, :], identity=ident_f[0:64, 0:64])),
                         reads=[r_fin, r_ident_f], writes=[rps], signal=(r == 1))
                fo = t1[:, 0, :]
                OP(ACT, "copy", [rps, r_pq], [r_pq], out=fo, in_=ps[:, 0:128])
                P.dma(SP, (lambda e, sq=sqi, d=d, fo=fo: e.dma_start(out=o_sre[sq, d], in_=fo[:, 0:64])), reads=[r_pq], writes=[r_osre])
                P.dma(SP, (lambda e, sq=sqi, d=d, fo=fo: e.dma_start(out=o_sim[sq, d], in_=fo[:, 64:128])), reads=[r_pq], writes=[r_osim])
        P.barrier()
        A.top = SAVE_TOP


        A.top = OVERLAY0
        wbuf[:] = [A.bf16([128, 8192]) for _ in range(NWB)]
        LOW0 = A.top
        branch_b = A.hi([128, 16, TO], BF16); r_bb = Res("bb")
        branch_a = A.hi([128, 16, TO], BF16); r_ba = Res("ba")
        A.mark()
        gAs = [A.f32([128, TO]) for _ in range(2)]; gBs = [A.f32([128, TO]) for _ in range(2)]; r_gs = [Res(), Res()]
        bglu = A.f32([128, 16]); r_bglu = Res()
        P.dma(SP, lambda e: e.dma_start(out=bglu, in_=bglu_fm), writes=[r_bglu])
        for ch in range(16):
            y = s5yT[:, ch, :]
            gA, gB, r_g = gAs[ch % 2], gBs[ch % 2], r_gs[ch % 2]
            R_ = [r_s5y, r_g]
            W_ = [r_g]
            TT(DVE, gA, y, y, ALU.mult, R_, W_)
            TS(DVE, gA, gA, 0.044715, 1.0, ALU.mult, ALU.add, R_, W_)
            TT(DVE, gA, gA, y, ALU.mult, R_, W_)
            OP(ACT, "activation", R_, W_, out=gB, in_=gA, func=AF.Sigmoid, scale=1.5957691216057308)
            OP(DVE, "tensor_tensor", R_, [r_s5y], relax=True, out=y, in0=y, in1=gB, op=ALU.mult)
        zt = [A.bf16([128, TO]) for _ in range(2)]; r_zt = [Res(), Res()]
        sgt = [A.bf16([128, 512]) for _ in range(2)]; r_sgt = [Res(), Res()]
        zc = {"i": 0, "ch": -1}

        def evac_glu(ch, n0, n, ps, rps):
            if ch != zc["ch"]:
                zc["ch"] = ch
                zc["i"] += 1
                i = zc["i"] % 2
                P.dma(SP, (lambda e, i=i, ch=ch: e.dma_start(out=zt[i], in_=ZB[ch])), reads=[r_zb], writes=[r_zt[i]])
            i = zc["i"] % 2
            k = (0 if n0 == 0 else 1)
            OP(ACT, "activation", [rps, r_bglu], [r_sgt[k]], out=sgt[k][:, 0:n], in_=ps[:, 0:n], func=AF.Sigmoid, bias=bglu[:, ch:ch + 1])
            TT(DVE, sgt[k][:, 0:n], sgt[k][:, 0:n], s5yT[:, ch, n0:n0 + n], ALU.mult, [r_sgt[k], r_s5y], [r_sgt[k]])
            TT(DVE, branch_b[:, ch, n0:n0 + n], sgt[k][:, 0:n], zt[i][:, n0:n0 + n], ALU.mult, [r_sgt[k], r_zt[i]], [r_bb])

        linear_fm(w_glu, 0, 16, s5yT, [r_s5y], 16, evac_glu)
        P.barrier()
        A.release()

        q_peT = s5yT
        r_qpe = Res("qpe")
        A.mark()
        ropeq = A.f32([128, 6, 64]); r_ropeq = Res()
        P.dma(SP, lambda e: e.dma_start(out=ropeq, in_=rope_q.rearrange("(a p) d -> p a d", p=128)), writes=[r_ropeq])
        qpf = A.f32([128, 16, 64]); r_qpf = Res()
        qpb = A.bf16([128, 16, 64]); r_qpb = Res()
        rt = [A.f32([128, 16, 32]) for _ in range(4)]; r_rt = Res()
        wi = ctr["w"] % NWB
        ctr["w"] += 1
        wpe = wbuf[wi].rearrange("p (hh k c) -> p hh k c", hh=2, k=8)
        for hh in range(2):
            for kc in range(8):
                P.dma(POOL, (lambda e, hh=hh, kc=kc: e.dma_start(
                    out=wpe[:, hh, kc, :].rearrange("p (h d) -> p h d", d=64),
                    in_=w_uq[kc * 128:(kc + 1) * 128, :].rearrange("p (h d) -> p h d", d=192)[:, hh * 8:(hh + 1) * 8, 128:192])), writes=[r_wbuf[wi]])
        for tt_ in range(6):
            for hh in range(2):
                ps, rps = next_mm()
                for kc in range(8):
                    P.op(PE, (lambda e, ps=ps, kc=kc, hh=hh, tt_=tt_: e.matmul(ps[:, 0:512], lhsT=cqnT[:, kc, tt_ * 128:(tt_ + 1) * 128], rhs=wpe[:, hh, kc, :],
                                                                              start=(kc == 0), stop=(kc == 7))),
                         reads=[r_cqn, r_wbuf[wi]], writes=[rps], signal=(kc == 7))
                OP(ACT, "copy", [rps], [r_qpf], out=qpf[:, hh * 8:(hh + 1) * 8, :].rearrange("p h d -> p (h d)"), in_=ps[:, 0:512])
            sv = qpf.rearrange("p h (x s d) -> p h x s d", x=2, s=2)
            dv = qpb.rearrange("p h (x s d) -> p h x s d", x=2, s=2)
            cosb = ropeq[:, tt_, 0:32].rearrange("p (x d) -> p x d", x=2).unsqueeze(1).broadcast_to([128, 16, 2, 16])
            sinb = ropeq[:, tt_, 32:64].rearrange("p (x d) -> p x d", x=2).unsqueeze(1).broadcast_to([128, 16, 2, 16])
            x1, x2 = sv[:, :, :, 0, :], sv[:, :, :, 1, :]
            tv = [t_.rearrange("p h (x d) -> p h x d", x=2) for t_ in rt]
            RR = [r_qpf, r_ropeq, r_rt]
            TT(DVE, tv[0], x1, cosb, ALU.mult, RR, [r_rt])
            TT(DVE, tv[1], x2, sinb, ALU.mult, RR, [r_rt])
            TT(DVE, tv[2], x2, cosb, ALU.mult, RR, [r_rt])
            TT(DVE, tv[3], x1, sinb, ALU.mult, RR, [r_rt])
            TT(DVE, dv[:, :, :, 0, :], tv[0], tv[1], ALU.subtract, [r_rt, r_qpb], [r_qpb])
            TT(DVE, dv[:, :, :, 1, :], tv[2], tv[3], ALU.add, [r_rt, r_qpb], [r_qpb])
            for hh in range(2):
                pt, rpt = next_tr()
                for h8 in range(8):
                    h = hh * 8 + h8
                    P.op(PE, (lambda e, pt=pt, h=h, h8=h8: e.transpose(out=pt[0:64, h8 * 128:(h8 + 1) * 128], in_=qpb[:, h, :], identity=ident_b)),
                         reads=[r_qpb, r_ident_b], writes=[rpt], signal=(h8 == 7))
                OP(ACT, "copy", [rpt], [r_qpe], out=q_peT[0:64, hh * 8:(hh + 1) * 8, tt_ * 128:(tt_ + 1) * 128],
                   in_=pt[0:64, :].rearrange("p (h t) -> p h t", h=8))
        P.barrier()
        A.release()

        A.mark()
        qT = A.bf16([128, TO]); r_qT = Res()
        kT = A.bf16([128, NKEY]); r_kT = Res()
        Vh = A.bf16([128, 16, 128]); r_Vh = Res()
        Pe = A.bf16([128, 1536]); r_Pe = Res()
        PT = A.bf16([128, 12, 128]); r_PT = Res()
        stt_ = A.f32([128, 16]); r_stt = Res()
        zat = [A.bf16([128, TO]) for _ in range(2)]; r_zat = [Res(), Res()]
        SCALE = 192.0 ** -0.5
        KR = [(0, 256), (0, 256), (256, 256), (256, 256), (512, 1536), (512, 1536)]
        for h in range(16):
            zi = h % 2
            P.dma(SP, (lambda e, zi=zi, h=h: e.dma_start(out=zat[zi], in_=ZA[h])), reads=[r_za], writes=[r_zat[zi]])
            wq, rwq = load_w(w_uq, 0, 8, h * 192, 128)
            wkv, rwkv = load_w(w_ukv, 0, 4, h * 256, 256)
            for (n0, n) in SEGS:
                ps, rps = next_mm()
                for kc in range(8):
                    P.op(PE, (lambda e, ps=ps, kc=kc, n0=n0, n=n, wq=wq: e.matmul(ps[:, 0:n], lhsT=wq[:, kc, :], rhs=cqnT[:, kc, n0:n0 + n],
                                                                                 start=(kc == 0), stop=(kc == 7))),
                         reads=[rwq, r_cqn], writes=[rps], signal=(kc == 7))
                OP(ACT, "copy", [rps], [r_qT], relax=(n0 > 0), out=qT[:, n0:n0 + n], in_=ps[:, 0:n])
            for s4 in range(4):
                ps, rps = next_mm()
                for kc in range(4):
                    P.op(PE, (lambda e, ps=ps, kc=kc, s4=s4, wkv=wkv: e.matmul(ps[:, 0:512], lhsT=wkv[:, kc, 0:128], rhs=ckvT_keys[:, kc, s4 * 512:(s4 + 1) * 512],
                                                                              start=(kc == 0), stop=(kc == 3))),
                         reads=[rwkv, r_ckvT], writes=[rps], signal=(kc == 3))
                OP(ACT, "copy", [rps], [r_kT], relax=(s4 > 0), out=kT[:, s4 * 512:(s4 + 1) * 512], in_=ps[:, 0:512])
            for b4 in range(4):
                ps, rps = next_mm()
                for bi in range(4):
                    blk = b4 * 4 + bi
                    for kc in range(4):
                        P.op(PE, (lambda e, ps=ps, kc=kc, bi=bi, blk=blk, wkv=wkv: e.matmul(ps[:, bi * 128:(bi + 1) * 128], lhsT=ckvT_keys[:, kc, blk * 128:(blk + 1) * 128],
                                                                                           rhs=wkv[:, kc, 128:256], start=(kc == 0), stop=(kc == 3))),
                             reads=[rwkv, r_ckvT], writes=[rps], signal=(kc == 3 and bi == 3))
                OP(DVE, "tensor_copy", [rps], [r_Vh], relax=(b4 > 0), out=Vh[:, b4 * 4:(b4 + 1) * 4, :], in_=ps[:, 0:512].rearrange("p (b d) -> p b d", b=4))
            for qt in range(6):
                k0, nk = KR[qt]
                nb = (nk + 511) // 512
                qs = slice(qt * 128, (qt + 1) * 128)
                banks = [(psf[3 + b], r_psf[3 + b]) for b in range(nb)]
                for b, (ps, rps) in enumerate(banks):
                    w_ = min(512, nk - b * 512)
                    ks = slice(k0 + b * 512, k0 + b * 512 + w_)
                    P.op(PE, (lambda e, ps=ps, w_=w_, ks=ks, qs=qs: e.matmul(ps[:, 0:w_], lhsT=qT[:, qs], rhs=kT[:, ks], start=True, stop=False)),
                         reads=[r_qT, r_kT], writes=[rps], signal=False)
                    P.op(PE, (lambda e, ps=ps, w_=w_, ks=ks, qs=qs, h=h: e.matmul(ps[:, 0:w_], lhsT=q_peT[0:64, h, qs], rhs=kpeT_keys[0:64, ks], start=False, stop=True)),
                         reads=[r_qpe, r_kpeT], writes=[rps], signal=True)
                RSt = [r_stt]
                for b, (ps, rps) in enumerate(banks):
                    w_ = min(512, nk - b * 512)
                    OP(DVE, "tensor_reduce", [rps] + RSt, [r_stt], out=stt_[:, b:b + 1], in_=ps[:, 0:w_], axis=AX.X, op=ALU.max)
                if nb > 1:
                    OP(DVE, "tensor_reduce", RSt, [r_stt], out=stt_[:, 4:5], in_=stt_[:, 0:nb], axis=AX.X, op=ALU.max)
                    mcol = stt_[:, 4:5]
                else:
                    mcol = stt_[:, 0:1]
                TS(DVE, stt_[:, 5:6], mcol, -SCALE, None, ALU.mult, None, RSt, [r_stt])
                for b, (ps, rps) in enumerate(banks):
                    w_ = min(512, nk - b * 512)
                    OP(ACT, "activation", [rps, r_stt, r_Pe], [r_Pe, r_stt], out=Pe[:, b * 512:b * 512 + w_], in_=ps[:, 0:w_], func=AF.Exp, scale=SCALE,
                       bias=stt_[:, 5:6], accum_out=stt_[:, 8 + b:9 + b])
                if nb > 1:
                    OP(DVE, "tensor_reduce", RSt, [r_stt], out=stt_[:, 6:7], in_=stt_[:, 8:8 + nb], axis=AX.X, op=ALU.add)
                    scol = stt_[:, 6:7]
                else:
                    scol = stt_[:, 8:9]
                OP(DVE, "reciprocal", RSt, [r_stt], out=stt_[:, 7:8], in_=scol)
                TS(DVE, Pe[:, 0:nk], Pe[:, 0:nk], stt_[:, 7:8], None, ALU.mult, None, [r_Pe, r_stt], [r_Pe])
                nblk = nk // 128
                for g8 in range((nblk + 7) // 8):
                    pt, rpt = next_tr()
                    nn = min(8, nblk - g8 * 8)
                    for bi in range(nn):
                        blk = g8 * 8 + bi
                        P.op(PE, (lambda e, pt=pt, bi=bi, blk=blk: e.transpose(out=pt[:, bi * 128:(bi + 1) * 128], in_=Pe[:, blk * 128:(blk + 1) * 128], identity=ident_b)),
                             reads=[r_Pe, r_ident_b], writes=[rpt], signal=(bi == nn - 1))
                    OP(ACT, "copy", [rpt], [r_PT], out=PT[:, g8 * 8:g8 * 8 + nn, :], in_=pt[:, 0:nn * 128].rearrange("p (b t) -> p b t", b=nn))
                ps, rps = next_mm()
                kb0 = k0 // 128
                for blk in range(nblk):
                    P.op(PE, (lambda e, ps=ps, blk=blk, kb0=kb0, nblk=nblk: e.matmul(ps[:, 0:128], lhsT=Vh[:, kb0 + blk, :], rhs=PT[:, blk, :],
                                                                        start=(blk == 0), stop=(blk == nblk - 1))),
                         reads=[r_Vh, r_PT], writes=[rps], signal=(blk == nblk - 1))
                TT(DVE, branch_a[:, h, qs], ps[:, 0:128], zat[zi][:, qs], ALU.mult, [rps, r_zat[zi]], [r_ba])
        P.barrier()
        A.release()

        ctr["mm_n"] = 6
        A.top = CONST_TOP
        wbuf[:] = [A.bf16([128, 8192]) for _ in range(NWB)]
        LOW1 = A.top
        merged = A.bf16([128, KC, TO]); r_mg = Res("merged")
        A.mark()
        gat = [A.bf16([128, TO]) for _ in range(2)]; r_gat = [Res(), Res()]
        gbt = [A.bf16([128, TO]) for _ in range(2)]; r_gbt = [Res(), Res()]
        mt = [A.f32([128, 512]) for _ in range(2)]; r_mt = [Res(), Res()]
        for j2 in range(16):
            wa, rwa = load_w(w_pa, 0, 16, j2 * 256, 256)
            wb, rwb = load_w(w_pb, 0, 16, j2 * 256, 256)
            for c2 in range(2):
                j = j2 * 2 + c2
                gi = j % 2
                P.dma(SP, (lambda e, gi=gi, j=j: e.dma_start(out=gat[gi], in_=GA[j])), reads=[r_ga], writes=[r_gat[gi]])
                P.dma(SP, (lambda e, gi=gi, j=j: e.dma_start(out=gbt[gi], in_=GB[j])), reads=[r_gb], writes=[r_gbt[gi]])
                for si, (n0, n) in enumerate(SEGS):
                    psa, rpsa = next_mm()
                    for kc in range(16):
                        P.op(PE, (lambda e, ps=psa, kc=kc, c2=c2, n0=n0, n=n, wa=wa: e.matmul(ps[:, 0:n], lhsT=wa[:, kc, c2 * 128:(c2 + 1) * 128], rhs=branch_a[:, kc, n0:n0 + n],
                                                                                             start=(kc == 0), stop=(kc == 15))),
                             reads=[rwa, r_ba], writes=[rpsa], signal=(kc == 15))
                    psb_, rpsb = next_mm()
                    for kc in range(16):
                        P.op(PE, (lambda e, ps=psb_, kc=kc, c2=c2, n0=n0, n=n, wb=wb: e.matmul(ps[:, 0:n], lhsT=wb[:, kc, c2 * 128:(c2 + 1) * 128], rhs=branch_b[:, kc, n0:n0 + n],
                                                                                              start=(kc == 0), stop=(kc == 15))),
                             reads=[rwb, r_bb], writes=[rpsb], signal=(kc == 15))
                    TT(DVE, mt[0][:, 0:n], psa[:, 0:n], gat[gi][:, n0:n0 + n], ALU.mult, [rpsa, r_gat[gi], r_mt[0]], [r_mt[0]])
                    TT(DVE, mt[1][:, 0:n], psb_[:, 0:n], gbt[gi][:, n0:n0 + n], ALU.mult, [rpsb, r_gbt[gi], r_mt[1]], [r_mt[1]])
                    TT(DVE, merged[:, j, n0:n0 + n], mt[0][:, 0:n], mt[1][:, 0:n], ALU.add, [r_mt[0], r_mt[1]], [r_mg])
        P.barrier()
        A.release()

        A.n = NW
        yacc = A.f32([128, 6, D]); r_yacc = Res("yacc")
        A.mark()
        gtl = [A.f32([128, 2, 256]) for _ in range(2)]; r_gtl = [Res(), Res()]
        for cb in range(16):
            wv, rw = load_w(w_o, 0, KC, cb * 256, 256)
            gi = cb % 2
            for m in range(2):
                P.dma(SP, (lambda e, gi=gi, m=m, cb=cb: e.dma_start(out=gtl[gi][:, m, :], in_=gate_d[m:m + 1, cb * 256:(cb + 1) * 256].partition_broadcast(128))),
                      reads=[r_gate_d], writes=[r_gtl[gi]])
            for tt_ in range(6):
                m = 0 if tt_ < 4 else 1
                ps, rps = next_mm()
                for kc in range(KC):
                    P.op(PE, (lambda e, ps=ps, kc=kc, tt_=tt_, wv=wv: e.matmul(ps[:, 0:256], lhsT=merged[:, kc, tt_ * 128:(tt_ + 1) * 128], rhs=wv[:, kc, :],
                                                                              start=(kc == 0), stop=(kc == KC - 1))),
                         reads=[rw, r_mg], writes=[rps], signal=(kc == KC - 1))
                TT(DVE, yacc[:, tt_, cb * 256:(cb + 1) * 256], ps[:, 0:256], gtl[gi][:, m, :], ALU.mult, [rps, r_gtl[gi]], [r_yacc])
        P.barrier()
        A.release()
        A.top = CONST_TOP
        lng = A.f32([128, D]); lnb = A.f32([128, D]); r_ln = Res()
        P.dma(SP, lambda e: e.dma_start(out=lng, in_=ln_g.partition_broadcast(128)), writes=[r_ln])
        P.dma(SP, lambda e: e.dma_start(out=lnb, in_=ln_b.partition_broadcast(128)), writes=[r_ln])
        xr = A.f32([128, D]); r_xr = Res()
        fs = A.f32([128, 64]); r_fs = Res()
        r_yout = Res("yout")
        for tt_ in range(6):
            z = yacc[:, tt_, :]
            P.dma(SP, (lambda e, tt_=tt_: e.dma_start(out=xr, in_=x_own[tt_ * 128:(tt_ + 1) * 128, :])), writes=[r_xr])
            OP(DVE, "scalar_tensor_tensor", [r_xr, r_yacc], [r_yacc], out=z, in0=xr, scalar=ALPHA, in1=z, op0=ALU.mult, op1=ALU.add)
            for c in range(8):
                OP(DVE, "bn_stats", [r_yacc, r_fs], [r_fs], out=fs[:, c * 6:(c + 1) * 6], in_=z[:, c * 512:(c + 1) * 512])
            OP(DVE, "bn_aggr", [r_fs], [r_fs], out=fs[:, 48:50], in_=fs[:, 0:48])
            TS(DVE, fs[:, 50:51], fs[:, 49:50], EPS, None, ALU.add, None, [r_fs], [r_fs])
            OP(ACT, "activation", [r_fs], [r_fs], out=fs[:, 51:52], in_=fs[:, 50:51], func=AF.Sqrt)
            OP(DVE, "reciprocal", [r_fs], [r_fs], out=fs[:, 52:53], in_=fs[:, 51:52])
            TS(DVE, fs[:, 53:54], fs[:, 48:49], fs[:, 52:53], -1.0, ALU.mult, ALU.mult, [r_fs], [r_fs])
            OP(ACT, "activation", [r_fs, r_yacc], [r_yacc], out=z, in_=z, func=AF.Identity, scale=fs[:, 52:53], bias=fs[:, 53:54])
            TT(DVE, z, z, lng, ALU.mult, [r_yacc, r_ln], [r_yacc])
            TT(DVE, z, z, lnb, ALU.add, [r_yacc, r_ln], [r_yacc])
            P.dma(SP, (lambda e, tt_=tt_, z=z: e.dma_start(out=y_out[tt_ * 128:(tt_ + 1) * 128, :], in_=z)), reads=[r_yacc], writes=[r_yout])

        P.wait_on(SP, [r_ockv, r_okpe, r_gate_d, r_uall, r_za, r_zb, r_ga, r_gb, r_osre, r_osim, r_yout])
        P.wait_on(ACT, [r_ckvT, r_kpeT])
        P.barrier()
        P.emit()
    return nc


def kernel(x_prompt, x_sample, cache_ckv, cache_kpe, state_s5_re, state_s5_im, c, c_ctx,
           w_ada, b_ada, w_in, g_qn, w_uq, g_kvn, w_ukv,
           s5_a_re, s5_a_im, s5_log_dt, s5_b_re, s5_b_im, s5_c_re, s5_c_im, s5_d,
           w_glu, b_glu, w_pa, w_pb, w_o, ln_g, ln_b):
    f = lambda a: np.ascontiguousarray(np.asarray(a, dtype=np.float32))
    x_prompt, x_sample = f(x_prompt), f(x_sample)
    if "nc" not in _NC_CACHE:
        try:
            build_program()
        except _Done:
            pass
    nc = _NC_CACHE["nc"]
    w_ada0, w_in0 = f(w_ada)[0], f(w_in)[0]
    b0 = f(b_ada)[0]
    b_fm = np.ascontiguousarray(b0[:2 * D].reshape(64, 128).T)
    b_gate = np.ascontiguousarray(np.broadcast_to(b0[2 * D:][None, :], (2, D)))
    g_qn_fm = np.ascontiguousarray(f(g_qn)[0].reshape(8, 128).T)
    tt = np.arange(TA)
    inv = np.power(10000.0, -np.arange(16, dtype=np.float32) / 16).astype(np.float32)
    ang_r = (tt // 64).astype(np.float32)[:, None] * inv[None, :]
    ang_c = (tt % 64).astype(np.float32)[:, None] * inv[None, :]
    rope_k = np.ascontiguousarray(np.concatenate([np.cos(ang_r), np.cos(ang_c), np.sin(ang_r), np.sin(ang_c)], axis=1).astype(np.float32))
    are_, aim_, ldt_ = f(s5_a_re)[0], f(s5_a_im)[0], f(s5_log_dt)[0]
    a_t = lambda a: a.transpose(2, 0, 1).reshape(64, 256)
    s5A = np.zeros((128, 3, 256), np.float32)
    s5A[:, 0] = np.concatenate([a_t(are_), a_t(are_)], 0)
    s5A[:, 1] = np.concatenate([a_t(aim_), a_t(aim_)], 0)
    s5A[:, 2] = np.broadcast_to(ldt_.reshape(1, 256), (128, 256))
    bre, bim = f(s5_b_re)[0], f(s5_b_im)[0]
    b_t = lambda a: a.transpose(2, 0, 1, 3).reshape(64, 256, 16)
    s5B = np.stack([np.concatenate([b_t(bre), b_t(bim)], 0), np.concatenate([b_t(bim), b_t(bre)], 0)], axis=1)
    cre, cim = f(s5_c_re)[0], f(s5_c_im)[0]
    c_t = lambda a: a.transpose(3, 0, 1, 2).reshape(64, 256, 16)
    s5C = np.stack([np.concatenate([c_t(cre), c_t(cim)], 0), np.concatenate([c_t(cim), c_t(cre)], 0)], axis=1)
    s5A, s5B, s5C = map(np.ascontiguousarray, (s5A, s5B.astype(np.float32), s5C.astype(np.float32)))
    ar8 = np.arange(8, dtype=np.float32); ar32 = np.arange(32, dtype=np.float32)
    kt = np.zeros((2, 59), np.float32)
    kt[0] = np.concatenate([7 - ar8, ar8 - 7, ar8 + 1, [1, 8, 256], 8 * (ar32 + 1)])
    kt[1] = np.concatenate([ar8, -ar8, 8 - ar8, [1, 8, 256], 8 * (32 - ar32)])
    kt_in = np.ascontiguousarray(np.broadcast_to(kt[None], (128, 2, 59)))
    sgn_in = np.concatenate([-np.ones((64, 1), np.float32), np.ones((64, 1), np.float32)], 0)
    ss = np.repeat(np.arange(8), 16)
    mask_in = np.ascontiguousarray(np.stack([(ss[:, None] <= ss[None, :]), (ss[:, None] >= ss[None, :])], axis=1).astype(np.float32))
    dcol_in = np.ascontiguousarray(np.tile(f(s5_d)[0].reshape(128, 16).T, (8, 1)))
    W_UQ, W_UKV, W_GLU, W_PA, W_PB, W_O = f(w_uq)[0], f(w_ukv)[0], f(w_glu)[0], f(w_pa)[0], f(w_pb)[0], f(w_o)[0]
    bglu_fm = np.ascontiguousarray(f(b_glu)[0].reshape(16, 128).T)
    rope_id = np.concatenate([np.ones((512, 32), np.float32), np.zeros((512, 32), np.float32)], axis=1)
    in_maps = []
    for i in range(8):
        b, q = i // 4, i % 4
        x_own = np.concatenate([x_prompt[2 * i], x_prompt[2 * i + 1], x_sample[b, q * 256:(q + 1) * 256]], axis=0)
        cond = np.stack([f(c_ctx), f(c)[b]], axis=0)
        condT = np.ascontiguousarray(cond.reshape(2, KC, 128).transpose(2, 1, 0))
        in_maps.append({
            "x_own": np.ascontiguousarray(x_own), "x_aux": np.ascontiguousarray(x_sample[b]),
            "condT": condT, "b_fm": b_fm, "b_gate": b_gate, "w_ada": w_ada0, "w_in": w_in0,
            "g_kvn": f(g_kvn), "g_qn_fm": g_qn_fm,
            "s5A": s5A, "s5B": s5B, "s5C": s5C, "kt_in": kt_in, "sgn_in": sgn_in, "mask_in": mask_in, "dcol_in": dcol_in,
            "h0_in": np.ascontiguousarray(np.stack([f(state_s5_re)[b, 0], f(state_s5_im)[b, 0]], axis=1).transpose(3, 0, 1, 2)),
            "oh_in": np.ascontiguousarray(np.broadcast_to(np.eye(4, dtype=np.float32)[q][None, :], (64, 4))),
            "w_uq": W_UQ, "w_ukv": W_UKV, "w_glu": W_GLU, "w_pa": W_PA, "w_pb": W_PB, "w_o": W_O, "bglu_fm": bglu_fm,
            "rope_q": np.ascontiguousarray(np.concatenate([rope_id, rope_k[q * 256:(q + 1) * 256]], axis=0)),
            "ln_g": f(ln_g), "ln_b": f(ln_b),
            "cache_ckv": f(cache_ckv)[b, 0], "cache_kpe": f(cache_kpe)[b, 0], "rope_k": rope_k,
        })
    ncores = int(os.environ.get("KCORES", "8"))
    res = run_bass_kernel_spmd(nc, in_maps[:ncores], core_ids=list(range(ncores)))
    R = list(res.results) + [res.results[0]] * (8 - ncores)
    y_prompt = np.zeros((16, 256, D), np.float32)
    y_sample = np.zeros((2, 1024, D), np.float32)
    n_ckv = np.zeros((16, 1, 256, 512), np.float32)
    n_kpe = np.zeros((16, 1, 256, 64), np.float32)
    n_sre = np.zeros((16, 1, 2, 128, 64), np.float32)
    n_sim = np.zeros((16, 1, 2, 128, 64), np.float32)
    for i in range(8):
        n_ckv[2 * i:2 * i + 2, 0] = R[i]["o_ckv"].reshape(2, 256, 512)
        n_kpe[2 * i:2 * i + 2, 0] = R[i]["o_kpe"].reshape(2, 256, 64)
        n_sre[2 * i:2 * i + 2, 0] = R[i]["o_sre"]
        n_sim[2 * i:2 * i + 2, 0] = R[i]["o_sim"]
        yo = R[i]["y_out"]
        y_prompt[2 * i] = yo[0:256]
        y_prompt[2 * i + 1] = yo[256:512]
        y_sample[i // 4, (i % 4) * 256:(i % 4 + 1) * 256] = yo[512:768]
    global DBG
    DBG = R
    return (y_prompt, y_sample, n_ckv, n_kpe, n_sre, n_sim)
```

```python
import contextlib
import numpy as np
import concourse.bass as bass
import concourse.mybir as mybir
from concourse.bass_utils import run_bass_kernel_spmd

F32 = mybir.dt.float32
BF16 = mybir.dt.bfloat16
AF = mybir.ActivationFunctionType
ALU = mybir.AluOpType
AX = mybir.AxisListType

PE, ACT, DVE, POOL, SP = "tensor", "scalar", "vector", "gpsimd", "sync"
ENGS = [PE, ACT, DVE, POOL, SP]
SEM_ROLL = 12000

D = 4096
KC = 32
TO = 768
TA = 1024
NKEY = 2048
IN_COLS = 15936
C_CQ, C_CKV, C_KPE, C_ZA, C_UB, C_ZB, C_GA, C_GB = 0, 1024, 1536, 1600, 3648, 5696, 7744, 11840
ALPHA = 2.0 ** 0.25
EPS = 1e-6
DEBUG = False
import os
STAGE = int(os.environ.get('KSTAGE', '99'))


_NC_CACHE = {}


class _Done(Exception):
    pass


class Res:
    __slots__ = ("name", "w", "r", "wl")

    def __init__(self, name=""):
        self.name = name
        self.w = None
        self.r = {}
        self.wl = {}


class Prog:
    def __init__(self, nc, n_dma_sems=24):
        self.nc = nc
        self.ops = {e: [] for e in ENGS}
        self.sem_handles = {}
        self.cur = {}
        self.prev = {}
        self.sem_n = {e: 0 for e in ENGS}
        self.seen = {e: {} for e in ENGS}
        for e in [PE, ACT, DVE, POOL]:
            self._new_counter(e)
        self.n_dma_sems = n_dma_sems
        self.dma_sems = []
        self.dma_pool = {}
        for q in [SP, POOL, ACT]:
            self.dma_pool[q] = []
            for i in range(n_dma_sems if q == SP else 8):
                k = ("dma", q, i)
                self.sem_handles[k] = None
                ent = [k, 0]
                self.dma_sems.append(ent)
                self.dma_pool[q].append(ent)
        self.dma_q = {SP: 0, POOL: 0, ACT: 0}
        self.dma_i = 0
        self.slot_epoch = {}

    def _new_counter(self, e):
        k = ("cnt", e, self.sem_n[e])
        self.sem_n[e] += 1
        self.sem_handles[k] = None
        if e in self.cur:
            self.prev[e] = tuple(self.cur[e])
        self.cur[e] = [k, 0]

    def _need(self, eng, dep, waits, allow_same=False):
        if dep is None:
            return
        k, v, src = dep
        if src == eng and not (allow_same and eng != PE):
            return
        if self.seen[eng].get(k, 0) >= v:
            return
        waits[k] = max(waits.get(k, 0), v)

    def _collect(self, eng, reads, writes, relax=False):
        waits = {}
        for r in reads:
            self._need(eng, r.w, waits, allow_same=True)
            for k_, v_ in r.wl.items():
                self._need(eng, (k_, v_, "dma"), waits)
        for w in writes:
            self._need(eng, w.w, waits, allow_same=not relax)
            for k_, v_ in w.wl.items():
                self._need(eng, (k_, v_, "dma"), waits)
            for re_, d in w.r.items():
                self._need(eng, (d[0], d[1], re_), waits, allow_same=True)
        for k, v in waits.items():
            self.seen[eng][k] = v
        return list(waits.items())

    def op(self, eng, fn, reads=(), writes=(), signal=True, relax=False):
        if eng != PE:
            signal = True
        waits = self._collect(eng, reads, writes, relax=relax)
        k, v = self.cur[eng]
        nv = v + 1
        if signal:
            self.cur[eng][1] = nv
        for r in reads:
            r.r[eng] = (k, nv)
        for w in writes:
            w.w = (k, nv, eng)
            w.r = {}
            w.wl = {}
        self.ops[eng].append((waits, fn, (k, 1) if signal else None))
        if signal and nv >= SEM_ROLL:
            self._new_counter(eng)

    def dma_slot(self, eng, fn, slot_id, reads=(), writes=()):
        waits = self._collect(eng, reads, writes)
        ep = self.slot_epoch.get(slot_id, 0)
        self.slot_epoch[slot_id] = ep + 1
        k = ("slot", slot_id, ep)
        self.sem_handles.setdefault(("slot", slot_id), None)
        if ep > 0:
            waits = list(waits) + [(("slot", slot_id, ep - 1), 16)]
            self.ops[eng].append((waits, ("clear", ("slot", slot_id)), None))
            waits = []
        self.dma_i += 1
        for r in reads:
            r.r["dma%d" % self.dma_i] = (k, 16)
        for w in writes:
            w.w = (k, 16, "dma")
            w.r = {}
        self.ops[eng].append((waits, fn, (k, 16)))

    def dma(self, eng, fn, reads=(), writes=()):
        waits = self._collect(eng, reads, writes)
        pool = self.dma_pool[eng]
        slot = pool[self.dma_q[eng] % len(pool)]
        self.dma_q[eng] += 1
        self.dma_i += 1
        k, v = slot
        if v > 0 and self.seen[eng].get(k, 0) < v:
            waits.append((k, v))
            self.seen[eng][k] = v
        nv = v + 16
        slot[1] = nv
        for r in reads:
            r.r["dma%d" % self.dma_i] = (k, nv)
        for w in writes:
            if w.w is not None and w.w[2] == "dma":
                w.wl[w.w[0]] = max(w.wl.get(w.w[0], 0), w.w[1])
            else:
                w.wl = {}
            w.w = (k, nv, "dma")
            w.r = {}
        self.ops[eng].append((waits, fn, (k, 16)))

    def wait_on(self, eng, res_list):
        waits = self._collect(eng, res_list, ())
        self.ops[eng].append((waits, None, None))

    def barrier(self):
        targets = []
        for e in [PE, ACT, DVE, POOL]:
            k, v = self.cur[e]
            if v > 0:
                targets.append((k, v))
            elif e in self.prev:
                targets.append(self.prev[e])
        for k, v in self.dma_sems:
            if v > 0:
                targets.append((k, v))
        for sid, ep in self.slot_epoch.items():
            targets.append((("slot", sid, ep - 1), 16))
        for e in ENGS:
            waits = []
            for k, v in targets:
                if k[0] == "cnt" and k[1] == e:
                    continue
                if self.seen[e].get(k, 0) < v:
                    waits.append((k, v))
                    self.seen[e][k] = v
            if waits:
                self.ops[e].append((waits, None, None))

    def emit(self):
        nc = self.nc
        with contextlib.ExitStack() as st:
            for k in self.sem_handles:
                self.sem_handles[k] = st.enter_context(nc.semaphore("s_" + "_".join(str(x) for x in k)))
            block = st.enter_context(nc.Block())
            HH = self.sem_handles

            def H(k):
                return HH[k[:2]] if k[0] == "slot" else HH[k]

            def run(engine_name):
                def body(e):
                    for waits, fn, inc in self.ops[engine_name]:
                        for k, v in waits:
                            e.wait_ge(H(k), v)
                        if fn is None:
                            continue
                        if isinstance(fn, tuple):
                            e.sem_clear(H(fn[1]))
                            continue
                        ins = fn(e)
                        if inc is not None:
                            ins.then_inc(H(inc[0]), inc[1])
                return body

            block.sync(run(SP))
            block.scalar(run(ACT))
            block.vector(run(DVE))
            block.gpsimd(run(POOL))
            block.tensor(run(PE))


class Arena:
    def __init__(self, t, nwords):
        self.t = t
        self.n = nwords
        self.top = 0
        self.marks = []

    def f32(self, shape):
        n = int(np.prod(shape[1:]))
        assert self.top + n <= self.n, ("arena overflow", self.top, n, self.n)
        v = self.t[0:shape[0], self.top:self.top + n]
        self.top += n
        return self._shape(v, shape)

    def bf16(self, shape):
        n = int(np.prod(shape[1:]))
        nw = (n + 1) // 2
        assert self.top + nw <= self.n, ("arena overflow", self.top, nw, self.n)
        v = self.t[0:shape[0], self.top:self.top + nw].bitcast(BF16)[:, 0:n]
        self.top += nw
        return self._shape(v, shape)

    @staticmethod
    def _shape(v, shape):
        if len(shape) == 2:
            return v
        if len(shape) == 3:
            return v.rearrange("p (a b) -> p a b", a=shape[1])
        if len(shape) == 4:
            return v.rearrange("p (a b c) -> p a b c", a=shape[1], b=shape[2])
        if len(shape) == 5:
            return v.rearrange("p (a b c d) -> p a b c d", a=shape[1], b=shape[2], c=shape[3])
        raise ValueError(shape)

    def hi(self, shape, dt):
        n = int(np.prod(shape[1:]))
        nw = n if dt == F32 else (n + 1) // 2
        assert self.top + nw <= self.n, ("arena overflow(hi)", self.top, nw, self.n)
        self.n -= nw
        v = self.t[0:shape[0], self.n:self.n + nw]
        if dt != F32:
            v = v.bitcast(BF16)[:, 0:n]
        return self._shape(v, shape)

    def mark(self):
        self.marks.append(self.top)

    def release(self):
        self.top = self.marks.pop()


def build_program(stage=99):
    nc = bass.Bass("TRN2", target_bir_lowering=False)
    _NC_CACHE["nc"] = nc

    def din(name, shape, dt=F32):
        return nc.dram_tensor(name, list(shape), dt, kind="ExternalInput").ap()

    def dout(name, shape, dt=F32):
        return nc.dram_tensor(name, list(shape), dt, kind="ExternalOutput").ap()

    def dscr(name, shape, dt=F32):
        return nc.dram_tensor(name, list(shape), dt, kind="Internal").ap()

    x_own = din("x_own", [TO, D])
    x_aux = din("x_aux", [TA, D])
    condT = din("condT", [128, KC, 2])
    b_fm = din("b_fm", [128, 64])
    b_gate = din("b_gate", [2, D])
    w_ada = din("w_ada", [D, 3 * D])
    w_in = din("w_in", [D, IN_COLS])
    g_kvn = din("g_kvn", [1, 512])
    g_qn_fm = din("g_qn_fm", [128, 8])
    cache_ckv = din("cache_ckv", [512, 512])
    cache_kpe = din("cache_kpe", [512, 64])

    o_ckv = dout("o_ckv", [512, 512])
    o_kpe = dout("o_kpe", [512, 64])

    rope_k = din("rope_k", [TA, 64])
    s5A = din("s5A", [128, 3, 256])
    s5B = din("s5B", [128, 2, 256, 16])
    s5C = din("s5C", [128, 2, 256, 16])
    kt_in = din("kt_in", [128, 2, 59])
    sgn_in = din("sgn_in", [128, 1])
    mask_in = din("mask_in", [128, 2, 128])
    dcol_in = din("dcol_in", [128, 128])
    h0_in = din("h0_in", [64, 2, 2, 128])
    oh_in = din("oh_in", [64, 4])
    w_uq = din("w_uq", [1024, 3072])
    w_ukv = din("w_ukv", [512, 4096])
    w_glu = din("w_glu", [2048, 2048])
    w_pa = din("w_pa", [2048, D])
    w_pb = din("w_pb", [2048, D])
    w_o = din("w_o", [D, D])
    bglu_fm = din("bglu_fm", [128, 16])
    rope_q = din("rope_q", [TO, 64])
    ln_g = din("ln_g", [1, D])
    ln_b = din("ln_b", [1, D])
    y_out = dout("y_out", [TO, D])
    o_sre = dout("o_sre", [2, 2, 128, 64])
    o_sim = dout("o_sim", [2, 2, 128, 64])
    gate_d = dscr("gate_d", [2, D])
    Uall = dscr("Uall", [224, 128, 128], BF16)
    ZA = dscr("ZA", [16, 128, TO], BF16)
    ZB = dscr("ZB", [16, 128, TO], BF16)
    GA = dscr("GA", [32, 128, TO], BF16)
    GB = dscr("GB", [32, 128, TO], BF16)

    P = Prog(nc)
    with contextlib.ExitStack() as st:
        NW = 53000
        arena_t = st.enter_context(nc.sbuf_tensor("arena", [128, NW], F32))
        A = Arena(arena_t, NW)
        psf = [st.enter_context(nc.psum_tensor("psf%d" % i, [128, 512], F32)) for i in range(6)]
        psb = [st.enter_context(nc.psum_tensor("psb%d" % i, [128, 1024], BF16)) for i in range(2)]
        r_psf = [Res("psf%d" % i) for i in range(6)]
        r_psb = [Res("psb%d" % i) for i in range(2)]
        ctr = {"mm": 0, "tr": 0, "w": 0, "ev": 0}

        def next_mm():
            i = ctr["mm"] % ctr.get("mm_n", 3)
            ctr["mm"] += 1
            return psf[i], r_psf[i]

        def next_tr():
            i = ctr["tr"] % 2
            ctr["tr"] += 1
            return psb[i], r_psb[i]

        ident_f = A.f32([128, 128]); r_ident_f = Res()
        ident_b = A.bf16([128, 128]); r_ident_b = Res()
        ones_b = A.bf16([128, 128]); r_ones = Res()
        P.op(POOL, lambda e: e.memset(ident_f, 0.0), writes=[r_ident_f], signal=False)
        P.op(POOL, lambda e: e.affine_select(out=ident_f, in_=ident_f, pattern=[[-1, 128]], compare_op=ALU.not_equal,
                                             fill=1.0, base=0, channel_multiplier=1), writes=[r_ident_f])
        P.op(DVE, lambda e: e.tensor_copy(out=ident_b, in_=ident_f), reads=[r_ident_f], writes=[r_ident_b])
        P.op(DVE, lambda e: e.memset(ones_b, 1.0), writes=[r_ones])

        NWB = 3
        wbuf = []
        r_wbuf = [Res("wb%d" % i) for i in range(NWB)]

        def load_w(W, r0, nk, c0, ncols):
            assert nk * ncols <= 8192
            i = ctr["w"] % NWB
            ctr["w"] += 1
            view = wbuf[i][:, 0:nk * ncols].rearrange("p (k c) -> p k c", k=nk)
            src = W[r0:r0 + nk * 128, c0:c0 + ncols].rearrange("(k p) c -> p k c", p=128)
            P.dma(POOL, lambda e: e.dma_start(out=view, in_=src), writes=[r_wbuf[i]])
            return view, r_wbuf[i]

        modT = A.f32([128, 64, 2]); r_mod = Res("mod")
        bfm = A.f32([128, 64]); r_bfm = Res()
        scT = A.bf16([128, KC, 2]); r_scT = Res()
        CONST_TOP = A.top
        ckvT_keys = A.bf16([128, 4, NKEY]); r_ckvT = Res("ckvT")
        kpeT_keys = A.bf16([128, NKEY]); r_kpeT = Res("kpeT")
        gkb = A.f32([128, 512]); r_gkb = Res()
        OVERLAY0 = A.top
        wck = A.bf16([128, KC, 576]); r_wck = Res()
        wbuf.extend([A.bf16([128, 8192]) for _ in range(NWB)])
        AFTER_WBUF = A.top
        A.mark()
        cT = A.f32([128, KC, 2]); r_cT = Res()
        r_gate_d = Res("gate_d")
        P.dma(SP, lambda e: e.dma_start(out=cT, in_=condT), writes=[r_cT])
        P.dma(SP, lambda e: e.dma_start(out=bfm, in_=b_fm), writes=[r_bfm])
        P.op(ACT, lambda e: e.activation(out=scT, in_=cT, func=AF.Silu), reads=[r_cT], writes=[r_scT])
        P.op(DVE, lambda e: e.tensor_scalar_add(out=bfm[:, 32:64], in0=bfm[:, 32:64], scalar1=1.0), reads=[r_bfm], writes=[r_bfm])
        for blk in range(16):
            for kh in range(2):
                wv, rw = load_w(w_ada, kh * 2048, 16, blk * 512, 512)
                for c4 in range(4):
                    ps, rps = psf[c4], r_psf[c4]
                    for k16 in range(16):
                        kc = kh * 16 + k16
                        P.op(PE, (lambda e, ps=ps, wv=wv, k16=k16, kc=kc, c4=c4: e.matmul(ps[:, 0:2], lhsT=wv[:, k16, c4 * 128:(c4 + 1) * 128],
                                                                                         rhs=scT[:, kc, :], start=(kc == 0), stop=(kc == KC - 1))),
                             reads=[rw, r_scT], writes=[rps], signal=(kc == KC - 1))
            for c4 in range(4):
                ch = blk * 4 + c4
                ps, rps = psf[c4], r_psf[c4]
                P.op(DVE, (lambda e, ps=ps, ch=ch: e.tensor_scalar(out=modT[:, ch, :], in0=ps[:, 0:2], scalar1=bfm[:, ch:ch + 1],
                                                                  scalar2=None, op0=ALU.add)),
                     reads=[rps, r_bfm], writes=[r_mod])
        P.barrier()
        A.release()

        P.dma(SP, lambda e: e.dma_start(out=gkb, in_=g_kvn.partition_broadcast(128)), writes=[r_gkb])
        for part in range(3):
            c0 = [0, 256, 512][part]
            n = [256, 256, 64][part]
            wv, rw = load_w(w_in, 0, KC, C_CKV + c0, n)
            P.op(DVE, (lambda e, wv=wv, c0=c0, n=n: e.tensor_copy(out=wck[:, :, c0:c0 + n], in_=wv)), reads=[rw], writes=[r_wck])

        evc = {"i": 0}

        def evac_copy(out, in_, reads, writes):
            evc["i"] += 1
            if evc["i"] % 2:
                P.op(ACT, lambda e: e.copy(out=out, in_=in_), reads=reads, writes=writes)
            else:
                P.op(DVE, lambda e: e.tensor_copy(out=out, in_=in_), reads=reads, writes=writes)

        def make_ln_bufs(nxn=2):
            xns = [A.bf16([128, D]) for _ in range(nxn)]
            rxn = [Res() for _ in range(nxn)]
            return dict(xt=A.f32([128, D]), r_xt=Res(), xn=[xns[k % nxn] for k in range(2)], r_xn=[rxn[k % nxn] for k in range(2)],
                        st=[A.f32([128, 64]) for _ in range(2)], r_st=[Res(), Res()], i=0)

        def ln_tile(B, xsrc, row0, hT, r_hT, t0, m):
            i = B["i"] % 2
            B["i"] += 1
            xt, r_xt, xn, r_xn, stt, r_st = B["xt"], B["r_xt"], B["xn"][i], B["r_xn"][i], B["st"][i], B["r_st"][i]
            P.dma(SP, lambda e: e.dma_start(out=xt, in_=xsrc[row0:row0 + 128, :]), writes=[r_xt])
            for c in range(8):
                P.op(DVE, (lambda e, c=c: e.bn_stats(out=stt[:, c * 6:(c + 1) * 6], in_=xt[:, c * 512:(c + 1) * 512])),
                     reads=[r_xt], writes=[r_st], relax=(c > 0))
            P.op(DVE, lambda e: e.bn_aggr(out=stt[:, 48:50], in_=stt[:, 0:48]), reads=[r_st], writes=[r_st])
            P.op(DVE, lambda e: e.tensor_scalar_add(out=stt[:, 50:51], in0=stt[:, 49:50], scalar1=EPS), reads=[r_st], writes=[r_st])
            P.op(ACT, lambda e: e.activation(out=stt[:, 51:52], in_=stt[:, 50:51], func=AF.Sqrt), reads=[r_st], writes=[r_st])
            P.op(DVE, lambda e: e.reciprocal(out=stt[:, 52:53], in_=stt[:, 51:52]), reads=[r_st], writes=[r_st])
            P.op(DVE, lambda e: e.tensor_scalar(out=stt[:, 53:54], in0=stt[:, 48:49], scalar1=stt[:, 52:53], scalar2=-1.0,
                                                op0=ALU.mult, op1=ALU.mult), reads=[r_st], writes=[r_st])
            P.op(DVE, lambda e: e.tensor_scalar(out=xn, in0=xt, scalar1=stt[:, 52:53], scalar2=stt[:, 53:54],
                                                op0=ALU.mult, op1=ALU.add), reads=[r_xt, r_st], writes=[r_xn])
            for g4 in range(4):
                pt, rpt = next_tr()
                for q in range(8):
                    kc = g4 * 8 + q
                    P.op(PE, (lambda e, pt=pt, q=q, kc=kc: e.transpose(out=pt[:, q * 128:(q + 1) * 128], in_=xn[:, kc * 128:(kc + 1) * 128],
                                                                       identity=ident_b)),
                         reads=[r_xn, r_ident_b], writes=[rpt], signal=(q == 7))
                for q in range(8):
                    kc = g4 * 8 + q
                    if True:
                        P.op(ACT, (lambda e, pt=pt, q=q, kc=kc: e.activation(out=hT[:, kc, t0:t0 + 128], in_=pt[:, q * 128:(q + 1) * 128],
                                                                             func=AF.Identity, scale=modT[:, 32 + kc, m:m + 1],
                                                                             bias=modT[:, kc, m:m + 1])),
                             reads=[rpt, r_mod], writes=[r_hT[0]], relax=(q > 0))
                    else:
                        P.op(DVE, (lambda e, pt=pt, q=q, kc=kc: e.tensor_scalar(out=hT[:, kc, t0:t0 + 128], in0=pt[:, q * 128:(q + 1) * 128],
                                                                               scalar1=modT[:, 32 + kc, m:m + 1], scalar2=modT[:, kc, m:m + 1],
                                                                               op0=ALU.mult, op1=ALU.add)),
                             reads=[rpt, r_mod], writes=[r_hT[1]])


        def make_ck_bufs():
            return dict(ck_f=[A.f32([128, 512]) for _ in range(2)], r_ckf=[Res(), Res()],
                        ck_b=[A.bf16([128, 512]) for _ in range(2)], r_ckb=[Res(), Res()],
                        kp_f=[A.f32([128, 64]) for _ in range(2)], r_kpf=[Res(), Res()],
                        kp_b=[A.bf16([128, 64]) for _ in range(2)], r_kpb=[Res(), Res()],
                        sm=[A.f32([128, 8]) for _ in range(2)], r_sm=[Res(), Res()], i=0)

        r_ockv = Res(); r_okpe = Res(); r_osre = Res(); r_osim = Res()

        def ckv_kpe_tile(B, hT, r_hT, t0, key0, out_row0, rope):
            i = B["i"] % 2
            B["i"] += 1
            ck_f, r_ckf, ck_b, r_ckb = B["ck_f"][i], B["r_ckf"][i], B["ck_b"][i], B["r_ckb"][i]
            kp_f, r_kpf, kp_b, r_kpb, s, r_s = B["kp_f"][i], B["r_kpf"][i], B["kp_b"][i], B["r_kpb"][i], B["sm"][i], B["r_sm"][i]
            ps1, r1 = (psf[3], r_psf[3]) if i == 0 else (psf[5], r_psf[5])
            ps2, r2 = psf[4], r_psf[4]
            for kc in range(KC):
                P.op(PE, (lambda e, kc=kc: e.matmul(ps1[:, 0:512], lhsT=hT[:, kc, t0:t0 + 128], rhs=wck[:, kc, 0:512],
                                                    start=(kc == 0), stop=(kc == KC - 1))),
                     reads=[*r_hT, r_wck], writes=[r1], signal=False)
                P.op(PE, (lambda e, kc=kc: e.matmul(ps2[:, 0:64], lhsT=hT[:, kc, t0:t0 + 128], rhs=wck[:, kc, 512:576],
                                                    start=(kc == 0), stop=(kc == KC - 1))),
                     reads=[*r_hT, r_wck], writes=[r2], signal=(kc == KC - 1))
            P.op(ACT, lambda e: e.activation(out=ck_b, in_=ps1[:, 0:512], func=AF.Square, accum_out=s[:, 0:1]),
                 reads=[r1], writes=[r_ckb, r_s])
            P.op(DVE, lambda e: e.tensor_scalar(out=s[:, 1:2], in0=s[:, 0:1], scalar1=1.0 / 512, scalar2=EPS, op0=ALU.mult, op1=ALU.add),
                 reads=[r_s], writes=[r_s])
            P.op(ACT, lambda e: e.activation(out=s[:, 2:3], in_=s[:, 1:2], func=AF.Sqrt), reads=[r_s], writes=[r_s])
            P.op(DVE, lambda e: e.reciprocal(out=s[:, 3:4], in_=s[:, 2:3]), reads=[r_s], writes=[r_s])
            P.op(DVE, lambda e: e.scalar_tensor_tensor(out=ck_f, in0=ps1[:, 0:512], scalar=s[:, 3:4], in1=gkb, op0=ALU.mult, op1=ALU.mult),
                 reads=[r1, r_gkb, r_s], writes=[r_ckf])
            P.op(DVE, lambda e: e.tensor_copy(out=ck_b, in_=ck_f), reads=[r_ckf], writes=[r_ckb])
            P.op(DVE, lambda e: e.tensor_copy(out=kp_f, in_=ps2[:, 0:64]), reads=[r2], writes=[r_kpf])
            if rope is None:
                P.op(DVE, lambda e: e.tensor_copy(out=kp_b, in_=kp_f), reads=[r_kpf], writes=[r_kpb])
            else:
                rope(kp_f, kp_b, r_kpf, r_kpb)
            if out_row0 is not None:
                P.dma(SP, lambda e: e.dma_start(out=o_ckv[out_row0:out_row0 + 128, :], in_=ck_f), reads=[r_ckf], writes=[r_ockv])
                P.dma(SP, lambda e: e.dma_start(out=o_kpe[out_row0:out_row0 + 128, :], in_=kp_f), reads=[r_kpf], writes=[r_okpe])
            pt, rpt = next_tr()
            for kc in range(4):
                P.op(PE, (lambda e, kc=kc: e.transpose(out=pt[:, kc * 128:(kc + 1) * 128], in_=ck_b[:, kc * 128:(kc + 1) * 128], identity=ident_b)),
                     reads=[r_ckb, r_ident_b], writes=[rpt], signal=False)
            P.op(PE, lambda e: e.transpose(out=pt[0:64, 512:640], in_=kp_b, identity=ident_b),
                 reads=[r_kpb, r_ident_b], writes=[rpt])
            P.op(ACT, lambda e: e.copy(out=ckvT_keys[:, :, key0:key0 + 128], in_=pt[:, 0:512].rearrange("p (k t) -> p k t", k=4)),
                 reads=[rpt], writes=[r_ckvT])
            P.op(ACT, lambda e: e.copy(out=kpeT_keys[0:64, key0:key0 + 128], in_=pt[0:64, 512:640]), reads=[rpt], writes=[r_kpeT])

        def make_rope(tab, r_tab, tmp, r_tmp, a):
            def rope(src, dst, r_src, r_dst):
                sv = src.rearrange("p (x h d) -> p x h d", x=2, h=2)
                dv = dst.rearrange("p (x h d) -> p x h d", x=2, h=2)
                cos = tab[:, a, 0:32].rearrange("p (x d) -> p x d", x=2)
                sin = tab[:, a, 32:64].rearrange("p (x d) -> p x d", x=2)
                x1, x2 = sv[:, :, 0, :], sv[:, :, 1, :]
                t = [tmp[:, k, :].rearrange("p (x d) -> p x d", x=2) for k in range(4)]
                P.op(DVE, lambda e: e.tensor_tensor(out=t[0], in0=x1, in1=cos, op=ALU.mult), reads=[r_src, r_tab], writes=[r_tmp])
                P.op(DVE, lambda e: e.tensor_tensor(out=t[1], in0=x2, in1=sin, op=ALU.mult), reads=[r_src, r_tab], writes=[r_tmp])
                P.op(DVE, lambda e: e.tensor_tensor(out=t[2], in0=x2, in1=cos, op=ALU.mult), reads=[r_src, r_tab], writes=[r_tmp])
                P.op(DVE, lambda e: e.tensor_tensor(out=t[3], in0=x1, in1=sin, op=ALU.mult), reads=[r_src, r_tab], writes=[r_tmp])
                P.op(DVE, lambda e: e.tensor_tensor(out=dv[:, :, 0, :], in0=t[0], in1=t[1], op=ALU.subtract), reads=[r_tmp], writes=[r_dst])
                P.op(DVE, lambda e: e.tensor_tensor(out=dv[:, :, 1, :], in0=t[2], in1=t[3], op=ALU.add), reads=[r_tmp], writes=[r_dst])
            return rope

        r_uall = Res("Uall")

        def ub_proj(hTx, r_hTx, ntok, ust, r_ust, dests):
            nj = ntok // 8
            for blk in range(8):
                wv, rw = load_w(w_in, 0, KC, C_UB + blk * 256, 256)
                u = ust[blk % 2]
                ru = r_ust[blk % 2]
                for tau in range(8):
                    ps, rps = next_mm()
                    for kc in range(KC):
                        P.op(PE, (lambda e, ps=ps, wv=wv, kc=kc, tau=tau: e.matmul(ps[0:nj, 0:256], lhsT=hTx[:, kc, tau:ntok:8], rhs=wv[:, kc, :],
                                                                                 start=(kc == 0), stop=(kc == KC - 1))),
                             reads=[rw, *r_hTx], writes=[rps], signal=(kc == KC - 1))
                    evac_copy(u[0:nj, :, tau * 16:(tau + 1) * 16], ps[0:nj, 0:256].rearrange("p (g c) -> p g c", c=16), [rps], [ru])
                for (r0, r1, d0) in dests:
                    P.dma(SP, (lambda e, u=u, r0=r0, r1=r1, d0=d0, blk=blk: e.dma_start(
                        out=Uall[d0:d0 + (r1 - r0), blk * 16:(blk + 1) * 16, :], in_=u[r0:r1, :, :])), reads=[ru], writes=[r_uall])

        def finish(extra=()):
            P.barrier()
            P.wait_on(SP, [r_ockv, r_okpe, r_gate_d, r_uall] + list(extra))
            P.wait_on(ACT, [r_ckvT, r_kpeT])
            P.emit()
            raise _Done()

        if STAGE == 0:
            finish()
        A.mark()
        lnB = make_ln_bufs()
        ckB = make_ck_bufs()
        ropek = A.f32([128, 8, 64]); r_ropek = Res()
        rtmp = A.f32([128, 4, 32]); r_rtmp = Res()
        P.dma(SP, lambda e: e.dma_start(out=ropek, in_=rope_k.rearrange("(a p) d -> p a d", p=128)), writes=[r_ropek])
        hTa = A.bf16([128, KC, 512]); r_hTa = [Res("hTa0"), Res("hTa1")]
        ustA = [A.bf16([64, 16, 128]) for _ in range(2)]; r_ustA = [Res(), Res()]
        bgb = [A.f32([2, 256]) for _ in range(2)]; r_bgb = [Res(), Res()]
        gsb = [A.f32([2, 256]) for _ in range(2)]; r_gsb = [Res(), Res()]
        for half in range(2):
            for t in range(4):
                ln_tile(lnB, x_aux, half * 512 + t * 128, hTa, r_hTa, t * 128, 1)
            if STAGE == 11:
                finish()
            for t in range(4):
                a = half * 4 + t
                ckv_kpe_tile(ckB, hTa, r_hTa, t * 128, 512 + a * 128, None, make_rope(ropek, r_ropek, rtmp, r_rtmp, a))
            if STAGE == 12:
                finish()
            ub_proj(hTa, r_hTa, 512, ustA, r_ustA, [(0, 64, 64 + half * 64)])
            if half == 0:
                for blk in range(16):
                    gi_ = blk % 2
                    cs_ = slice(blk * 256, (blk + 1) * 256)
                    P.dma(SP, (lambda e, gi_=gi_, cs_=cs_: e.dma_start(out=bgb[gi_], in_=b_gate[:, cs_])), writes=[r_bgb[gi_]])
                    wv, rw = load_w(w_ada, 0, KC, 2 * D + blk * 256, 256)
                    ps, rps = next_mm()
                    for kc in range(KC):
                        P.op(PE, (lambda e, ps=ps, wv=wv, kc=kc: e.matmul(ps[0:2, 0:256], lhsT=scT[:, kc, :], rhs=wv[:, kc, :],
                                                                         start=(kc == 0), stop=(kc == KC - 1))),
                             reads=[rw, r_scT], writes=[rps], signal=(kc == KC - 1))
                    P.op(DVE, (lambda e, ps=ps, gi_=gi_: e.tensor_tensor(out=gsb[gi_], in0=ps[0:2, 0:256], in1=bgb[gi_], op=ALU.add)),
                         reads=[rps, r_bgb[gi_]], writes=[r_gsb[gi_]])
                    P.dma(SP, (lambda e, gi_=gi_, cs_=cs_: e.dma_start(out=gate_d[:, cs_], in_=gsb[gi_])), reads=[r_gsb[gi_]], writes=[r_gate_d])
            if STAGE == 13:
                finish()
        for t in range(4):
            i = t % 2
            ck_f, r_ckf, ck_b, r_ckb = ckB["ck_f"][i], ckB["r_ckf"][i], ckB["ck_b"][i], ckB["r_ckb"][i]
            kp_f, r_kpf, kp_b, r_kpb = ckB["kp_f"][i], ckB["r_kpf"][i], ckB["kp_b"][i], ckB["r_kpb"][i]
            P.dma(SP, (lambda e, t=t, ck_f=ck_f: e.dma_start(out=ck_f, in_=cache_ckv[t * 128:(t + 1) * 128, :])), writes=[r_ckf])
            P.dma(SP, (lambda e, t=t, kp_f=kp_f: e.dma_start(out=kp_f, in_=cache_kpe[t * 128:(t + 1) * 128, :])), writes=[r_kpf])
            P.op(DVE, (lambda e, ck_f=ck_f, ck_b=ck_b: e.tensor_copy(out=ck_b, in_=ck_f)), reads=[r_ckf], writes=[r_ckb])
            P.op(DVE, (lambda e, kp_f=kp_f, kp_b=kp_b: e.tensor_copy(out=kp_b, in_=kp_f)), reads=[r_kpf], writes=[r_kpb])
            pt, rpt = next_tr()
            for kc in range(4):
                P.op(PE, (lambda e, kc=kc, pt=pt, ck_b=ck_b: e.transpose(out=pt[:, kc * 128:(kc + 1) * 128], in_=ck_b[:, kc * 128:(kc + 1) * 128],
                                                                        identity=ident_b)),
                     reads=[r_ckb, r_ident_b], writes=[rpt], signal=False)
            P.op(PE, (lambda e, pt=pt, kp_b=kp_b: e.transpose(out=pt[0:64, 512:640], in_=kp_b, identity=ident_b)),
                 reads=[r_kpb, r_ident_b], writes=[rpt])
            key0 = 1536 + t * 128
            P.op(ACT, (lambda e, pt=pt, key0=key0: e.copy(out=ckvT_keys[:, :, key0:key0 + 128],
                                                         in_=pt[:, 0:512].rearrange("p (k t) -> p k t", k=4))), reads=[rpt], writes=[r_ckvT])
            P.op(ACT, (lambda e, pt=pt, key0=key0: e.copy(out=kpeT_keys[0:64, key0:key0 + 128], in_=pt[0:64, 512:640])), reads=[rpt], writes=[r_kpeT])
        if STAGE == 1:
            finish()
        P.barrier()
        A.release()

        cqnT = A.hi([128, 8, TO], BF16); r_cqn = Res("cqn")
        A.mark()
        hT = A.bf16([128, KC, TO]); r_hT = [Res("hT0"), Res("hT1")]
        A.mark()
        lnB = make_ln_bufs(1)
        ckB = make_ck_bufs()
        for t in range(6):
            ln_tile(lnB, x_own, t * 128, hT, r_hT, t * 128, 0 if t < 4 else 1)
        for t in range(4):
            ckv_kpe_tile(ckB, hT, r_hT, t * 128, t * 128, t * 128, None)
        P.barrier()
        A.release()

        if STAGE == 2:
            finish()
        SEGS = [(0, 512), (512, 256)]

        def linear_fm(W, c0, nchunks, xT, r_xT, nk, evac):
            per = max(1, min(2, 8192 // (nk * 128) // 1))
            per = 2 if nk * 256 <= 8192 else 1
            ch = 0
            while ch < nchunks:
                nb = min(per, nchunks - ch)
                wv, rw = load_w(W, 0, nk, c0 + ch * 128, nb * 128)
                for c2 in range(nb):
                    for (n0, n) in SEGS:
                        ps, rps = next_mm()
                        for kc in range(nk):
                            P.op(PE, (lambda e, ps=ps, wv=wv, kc=kc, c2=c2, n0=n0, n=n: e.matmul(
                                ps[:, 0:n], lhsT=wv[:, kc, c2 * 128:(c2 + 1) * 128], rhs=xT[:, kc, n0:n0 + n],
                                start=(kc == 0), stop=(kc == nk - 1))),
                                reads=[rw, *r_xT], writes=[rps], signal=(kc == nk - 1))
                        evac(ch + c2, n0, n, ps, rps)
                ch += nb

        A.mark()
        sq = A.bf16([128, 8, TO]); r_sq = Res()
        rstd_b = A.f32([128, TO]); r_rstd = Res()
        gq = A.f32([128, 8]); r_gq = Res()
        stage = [A.bf16([128, TO]) for _ in range(3)]; r_stage = [Res() for _ in range(3)]
        ustO = [A.bf16([96, 16, 128]) for _ in range(2)]; r_ustO = [Res(), Res()]
        P.dma(SP, lambda e: e.dma_start(out=gq, in_=g_qn_fm), writes=[r_gq])

        def evac_cq(ch, n0, n, ps, rps):
            P.op(ACT, lambda e: e.copy(out=cqnT[:, ch, n0:n0 + n], in_=ps[:, 0:n]), reads=[rps], writes=[r_cqn])
            P.op(ACT, lambda e: e.activation(out=sq[:, ch, n0:n0 + n], in_=ps[:, 0:n], func=AF.Square), reads=[rps], writes=[r_sq])

        linear_fm(w_in, C_CQ, 8, hT, r_hT, KC, evac_cq)
        for (n0, n) in SEGS:
            ps, rps = next_mm()
            for ch in range(8):
                P.op(PE, (lambda e, ps=ps, ch=ch, n0=n0, n=n: e.matmul(ps[:, 0:n], lhsT=ones_b, rhs=sq[:, ch, n0:n0 + n],
                                                                      start=(ch == 0), stop=(ch == 7))),
                     reads=[r_ones, r_sq], writes=[rps], signal=(ch == 7))
            P.op(DVE, (lambda e, ps=ps, n0=n0, n=n: e.tensor_scalar(out=rstd_b[:, n0:n0 + n], in0=ps[:, 0:n], scalar1=1.0 / 1024, scalar2=EPS,
                                                                   op0=ALU.mult, op1=ALU.add)), reads=[rps], writes=[r_rstd])
        P.op(ACT, lambda e: e.activation(out=rstd_b, in_=rstd_b, func=AF.Sqrt), reads=[r_rstd], writes=[r_rstd])
        P.op(DVE, lambda e: e.reciprocal(out=rstd_b, in_=rstd_b), reads=[r_rstd], writes=[r_rstd])
        for ch in range(8):
            P.op(DVE, (lambda e, ch=ch: e.scalar_tensor_tensor(out=cqnT[:, ch, :], in0=cqnT[:, ch, :], scalar=gq[:, ch:ch + 1], in1=rstd_b,
                                                              op0=ALU.mult, op1=ALU.mult)), reads=[r_cqn, r_gq, r_rstd], writes=[r_cqn])

        if STAGE == 3:
            finish()
        stc = {"i": 0}

        def make_evac_act(func, dst, r_dst):
            def evac(ch, n0, n, ps, rps):
                i = stc["i"] % 3
                sg, rsg = stage[i], r_stage[i]
                P.op(ACT, lambda e: e.activation(out=sg[:, n0:n0 + n], in_=ps[:, 0:n], func=func), reads=[rps], writes=[rsg])
                if n0 + n == TO:
                    P.dma(SP, lambda e: e.dma_start(out=dst[ch], in_=sg), reads=[rsg], writes=[r_dst])
                    stc["i"] += 1
            return evac

        r_za, r_zb, r_ga, r_gb = Res("ZA"), Res("ZB"), Res("GA"), Res("GB")
        linear_fm(w_in, C_ZA, 16, hT, r_hT, KC, make_evac_act(AF.Silu, ZA, r_za))
        if STAGE == 4:
            finish([r_za])
        ub_proj(hT, r_hT, TO, ustO, r_ustO, [(0, 64, 0), (64, 96, 192)])
        if STAGE == 5:
            finish([r_za])
        linear_fm(w_in, C_ZB, 16, hT, r_hT, KC, make_evac_act(AF.Silu, ZB, r_zb))
        linear_fm(w_in, C_GA, 32, hT, r_hT, KC, make_evac_act(AF.Sigmoid, GA, r_ga))
        linear_fm(w_in, C_GB, 32, hT, r_hT, KC, make_evac_act(AF.Sigmoid, GB, r_gb))
        P.barrier()
        A.release()
        A.release()

        P.barrier()
        SAVE_TOP = A.top
        A.top = OVERLAY0
        s5yT = A.hi([128, 16, TO], BF16); r_s5y = Res("s5y")
        I32 = mybir.dt.int32
        NK = 59
        G8 = 8
        TWO_PI = 6.283185307179586

        def OP(eng, meth, reads, writes, relax=False, **kw):
            P.op(eng, (lambda e, meth=meth, kw=kw: getattr(e, meth)(**kw)), reads=reads, writes=writes, relax=relax)

        def TT(eng, out, in0, in1, op, reads, writes):
            OP(eng, "tensor_tensor", reads, writes, out=out, in0=in0, in1=in1, op=op)

        def TS(eng, out, in0, s1, s2, op0, op1, reads, writes):
            if s2 is None:
                OP(eng, "tensor_scalar", reads, writes, out=out, in0=in0, scalar1=s1, scalar2=None, op0=op0)
            else:
                OP(eng, "tensor_scalar", reads, writes, out=out, in0=in0, scalar1=s1, scalar2=s2, op0=op0, op1=op1)

        def bc(ap, axis, shape):
            return ap.unsqueeze(axis).broadcast_to(shape)

        kt_sb = A.f32([128, 2, NK]); sgn = A.f32([128, 1]); maskM = A.f32([128, 2, 128]); dcol = A.f32([128, 128])
        h0_sb = A.f32([64, 2, 2, 128]); oh_sb = A.f32([64, 4]); FIN = A.f32([64, 2, 2, 2, 128])
        r_c5 = Res("c5"); r_fin = Res("fin")
        for dst, src in [(kt_sb, kt_in), (sgn, sgn_in), (maskM, mask_in), (dcol, dcol_in), (h0_sb, h0_in), (oh_sb, oh_in)]:
            P.dma(SP, (lambda e, dst=dst, src=src: e.dma_start(out=dst, in_=src)), writes=[r_c5])
        A3 = A.f32([128, 3, 2, G8]); B12 = A.f32([128, 2, 2, G8 * 16]); C12 = A.f32([128, 2, 2, G8 * 16]); r_par = Res("par")
        T = [A.f32([128, 16, NK]) for _ in range(5)]
        TI = A.f32([128, 16, NK]).bitcast(I32)
        PR = A.f32([128, 16, NK]); PI = A.f32([128, 16, NK])
        PRSq = A.f32([128, 16, 16]); PISp = A.f32([128, 16, 8])
        sm5 = A.f32([128, 12, 16])
        BB = A.f32([128, 16, 16]); BBs = A.f32([128, 16, 16])
        r_tab = Res("tab")
        Pt = A.f32([128, G8, 128]); Qt = A.f32([128, G8, 128]); Ft = A.f32([128, G8, 128]); F2t = A.f32([128, G8, 128])
        t1 = A.f32([128, G8, 128]); t2 = A.f32([128, G8, 128]); r_pq = Res("pq")
        Mb = A.bf16([128, 2, G8, 128]); Eb = A.bf16([128, 2, G8, 128]); Fb = A.bf16([128, 2, G8, 128]); F2b = A.bf16([128, 2, G8, 128])
        r_M = Res("M"); r_E = Res("E"); r_F = Res("F")
        Mtmp = A.f32([128, 4, 128]); r_mtmp = Res()
        U1 = A.bf16([128, 8, 128]); U2 = A.bf16([128, 8, 128]); r_U1 = Res(); r_U2 = Res()
        Ug = A.bf16([128, G8, 224]); r_Ug = Res("Ug")
        Sloc = [A.f32([64, 2, G8, 7, 32]) for _ in range(2)]; r_S = [Res("S0"), Res("S1")]
        stA = [A.f32([64, 2, G8, 7]) for _ in range(2)]; stB = [A.f32([64, 2, G8, 7]) for _ in range(2)]
        XE = [A.f32([64, 4, 2, G8]) for _ in range(2)]; Xin = [A.f32([64, 2, G8]) for _ in range(2)]
        ct1 = [A.f32([64, 2, G8]) for _ in range(2)]; ct2 = [A.f32([64, 2, G8]) for _ in range(2)]
        W1 = [A.f32([64, 2, G8, 32]) for _ in range(2)]; W2 = [A.f32([64, 2, G8, 32]) for _ in range(2)]
        Xp = A.bf16([64, 2, 2, G8, 96]); r_Xp = [Res("Xp0"), Res("Xp1")]
        Ysb = A.f32([128, G8, 96]); r_Ysb = Res()
        YTM = A.f32([96, 8, 128]); r_YTM = Res()
        SE = [DVE, POOL]
        r_tAB = [[Res(), Res()], [Res(), Res()]]

        for gb in range(16):
            g0 = gb * G8
            for d in range(2):
                c0 = d * 128 + g0
                P.dma(SP, (lambda e, d=d, c0=c0: e.dma_start(out=A3[:, :, d, :], in_=s5A[:, :, c0:c0 + G8])), writes=[r_par])
                for w in range(2):
                    P.dma(SP, (lambda e, d=d, w=w, c0=c0: e.dma_start(out=B12[:, w, d, :].rearrange("p (g c) -> p g c", c=16),
                                                                     in_=s5B[:, w, c0:c0 + G8, :])), writes=[r_par])
                    P.dma(SP, (lambda e, d=d, w=w, c0=c0: e.dma_start(out=C12[:, w, d, :].rearrange("p (g c) -> p g c", c=16),
                                                                     in_=s5C[:, w, c0:c0 + G8, :])), writes=[r_par])
            RT = [r_par, r_c5, r_tab]
            WT = [r_tab]
            are = A3[:, 0].rearrange("p d g -> p (d g)"); aim = A3[:, 1].rearrange("p d g -> p (d g)"); ldt = A3[:, 2].rearrange("p d g -> p (d g)")
            B1 = B12[:, 0].rearrange("p d (g c) -> p (d g) c", c=16); B2 = B12[:, 1].rearrange("p d (g c) -> p (d g) c", c=16)
            C1 = C12[:, 0].rearrange("p d (g c) -> p (d g) c", c=16); C2 = C12[:, 1].rearrange("p d (g c) -> p (d g) c", c=16)
            dt_, dar, dai = sm5[:, 0], sm5[:, 1], sm5[:, 2]
            OP(ACT, "activation", RT, WT, out=dt_, in_=ldt, func=AF.Exp)
            TT(DVE, dar, dt_, are, ALU.mult, RT, WT)
            TT(DVE, dai, dt_, aim, ALU.mult, RT, WT)
            ktb = kt_sb.unsqueeze(2).broadcast_to([128, 2, G8, NK])
            v4 = lambda x: x.rearrange("p (d g) k -> p d g k", d=2)
            c4 = lambda x: x.rearrange("p (d g) -> p d g", d=2).unsqueeze(3).broadcast_to([128, 2, G8, NK])
            TT(DVE, v4(T[0]), c4(dai), ktb, ALU.mult, RT, WT)
            TT(DVE, v4(T[1]), c4(dar), ktb, ALU.mult, RT, WT)
            OP(ACT, "activation", RT, WT, out=T[1], in_=T[1], func=AF.Exp)
            for (dstT, off) in [(PI, 64.0), (PR, 64.25)]:
                TS(DVE, T[2], T[0], 1.0 / TWO_PI, off, ALU.mult, ALU.add, RT, WT)
                OP(DVE, "tensor_copy", RT, WT, out=TI, in_=T[2])
                OP(DVE, "tensor_copy", RT, WT, out=T[3], in_=TI)
                TT(DVE, T[2], T[2], T[3], ALU.subtract, RT, WT)
                TS(DVE, T[3], T[2], 0.5, None, ALU.is_gt, None, RT, WT)
                TT(DVE, T[2], T[2], T[3], ALU.subtract, RT, WT)
                OP(ACT, "activation", RT, WT, out=T[4], in_=T[2], func=AF.Sin, scale=6.28318)
                TT(DVE, dstT, T[1], T[4], ALU.mult, RT, WT)
            TS(DVE, PRSq, PR[:, :, 8:24], sgn[:, 0:1], None, ALU.mult, None, RT, WT)
            TS(DVE, PISp, PI[:, :, 0:8], sgn[:, 0:1], None, ALU.mult, None, RT, WT)
            abr, abi = PR[:, :, 24], PI[:, :, 24]
            den, pre, qre, qim, u1, u2 = sm5[:, 3], sm5[:, 4], sm5[:, 5], sm5[:, 6], sm5[:, 7], sm5[:, 8]
            TT(DVE, den, are, are, ALU.mult, RT, WT)
            TT(DVE, u1, aim, aim, ALU.mult, RT, WT)
            TT(DVE, den, den, u1, ALU.add, RT, WT)
            OP(DVE, "reciprocal", RT, WT, out=den, in_=den)
            TS(DVE, pre, abr, -1.0, None, ALU.add, None, RT, WT)
            TT(DVE, u1, pre, are, ALU.mult, RT, WT)
            TT(DVE, u2, abi, aim, ALU.mult, RT, WT)
            TT(DVE, u1, u1, u2, ALU.add, RT, WT)
            TT(DVE, qre, u1, den, ALU.mult, RT, WT)
            TT(DVE, u1, abi, are, ALU.mult, RT, WT)
            TT(DVE, u2, pre, aim, ALU.mult, RT, WT)
            TT(DVE, u1, u1, u2, ALU.subtract, RT, WT)
            TT(DVE, qim, u1, den, ALU.mult, RT, WT)
            TS(DVE, qim, qim, sgn[:, 0:1], None, ALU.mult, None, RT, WT)
            qreb = qre.unsqueeze(2).broadcast_to([128, 16, 16]); qimb = qim.unsqueeze(2).broadcast_to([128, 16, 16])
            x1 = T[2][:, :, 0:16]; x2 = T[2][:, :, 16:32]
            TT(DVE, x1, qreb, B1, ALU.mult, RT, WT)
            TT(DVE, x2, qimb, B2, ALU.mult, RT, WT)
            TT(DVE, BB, x1, x2, ALU.add, RT, WT)
            TT(DVE, x1, qreb, B2, ALU.mult, RT, WT)
            TT(DVE, x2, qimb, B1, ALU.mult, RT, WT)
            TT(DVE, BBs, x1, x2, ALU.subtract, RT, WT)

            for d in range(2):
                dsl = slice(d * G8, (d + 1) * G8)
                kb = lambda tab, k0: tab[:, dsl, k0:k0 + 8].unsqueeze(3).broadcast_to([128, G8, 8, 16])
                cb = lambda tab: tab[:, dsl, :].unsqueeze(2).broadcast_to([128, G8, 8, 16])
                v = lambda x: x.rearrange("p g (s c) -> p g s c", c=16)
                RP = [r_tab, r_pq, r_par]
                WP = [r_pq]
                TT(DVE, v(t1), kb(PR, 0), cb(BB), ALU.mult, RP, WP)
                TT(DVE, v(t2), kb(PISp, 0), cb(BBs), ALU.mult, RP, WP)
                TT(DVE, Pt, t1, t2, ALU.add, RP, WP)
                TT(DVE, v(t1), kb(PRSq, 0), cb(C1), ALU.mult, RP, WP)
                TT(DVE, v(t2), kb(PI, 8), cb(C2), ALU.mult, RP, WP)
                OP(DVE, "scalar_tensor_tensor", RP, WP, out=Qt, in0=t1, scalar=-1.0, in1=t2, op0=ALU.mult, op1=ALU.subtract)
                TT(DVE, v(t1), kb(PRSq, 8), cb(C1), ALU.mult, RP, WP)
                TT(DVE, v(t2), kb(PI, 16), cb(C2), ALU.mult, RP, WP)
                OP(DVE, "scalar_tensor_tensor", RP, WP, out=Ft, in0=t1, scalar=-1.0, in1=t2, op0=ALU.mult, op1=ALU.subtract)
                TT(DVE, v(t1), kb(PRSq, 8), cb(C2), ALU.mult, RP, WP)
                TT(DVE, v(t2), kb(PI, 16), cb(C1), ALU.mult, RP, WP)
                TT(DVE, F2t, t1, t2, ALU.subtract, RP, WP)
                OP(ACT, "copy", [r_pq], [r_F], out=Fb[0:64, d], in_=Ft[0:64])
                OP(ACT, "copy", [r_pq], [r_F], out=F2b[0:64, d], in_=F2t[0:64])
                for h4 in range(2):
                    ps, rps = next_mm()
                    for i4 in range(4):
                        i = h4 * 4 + i4
                        P.op(PE, (lambda e, ps=ps, i=i, i4=i4: e.matmul(ps[:, i4 * 128:(i4 + 1) * 128], lhsT=Pt[:, i, :], rhs=Qt[:, i, :],
                                                                       start=True, stop=True)),
                             reads=[r_pq], writes=[rps], signal=(i4 == 3))
                    TT(DVE, Mtmp, ps[:, 0:512].rearrange("p (i m) -> p i m", i=4), maskM[:, d, :].unsqueeze(1).broadcast_to([128, 4, 128]),
                       ALU.mult, [rps, r_c5], [r_mtmp])
                    for i4 in range(4):
                        i = h4 * 4 + i4
                        if d == 0:
                            OP(DVE, "scalar_tensor_tensor", [r_mtmp, r_c5, r_ident_f], [r_M], out=Mb[:, d, i, :], in0=ident_f,
                               scalar=dcol[:, g0 + i:g0 + i + 1], in1=Mtmp[:, i4, :], op0=ALU.mult, op1=ALU.add)
                        elif i4 == 0:
                            OP(ACT, "copy", [r_mtmp], [r_M], out=Mb[:, d, h4 * 4:(h4 + 1) * 4, :], in_=Mtmp)
                    ps, rps = next_mm()
                    for i4 in range(4):
                        i = h4 * 4 + i4
                        P.op(PE, (lambda e, ps=ps, i=i, i4=i4: e.transpose(out=ps[:, i4 * 128:(i4 + 1) * 128], in_=Pt[:, i, :], identity=ident_f)),
                             reads=[r_pq, r_ident_f], writes=[rps], signal=(i4 == 3))
                    OP(ACT, "copy", [rps], [r_E], out=Eb[:, d, h4 * 4:(h4 + 1) * 4, :], in_=ps[:, 0:512].rearrange("p (i m) -> p i m", i=4))

            P.dma(SP, (lambda e, g0=g0: e.dma_start(out=U1, in_=Uall[0:128, g0:g0 + G8, :])), reads=[r_uall], writes=[r_U1])
            P.dma(SP, (lambda e, g0=g0: e.dma_start(out=U2[0:96], in_=Uall[128:224, g0:g0 + G8, :])), reads=[r_uall], writes=[r_U2])
            for h4 in range(2):
                pt, rpt = next_tr()
                for i4 in range(4):
                    i = h4 * 4 + i4
                    P.op(PE, (lambda e, pt=pt, i=i, i4=i4: e.transpose(out=pt[:, i4 * 224:i4 * 224 + 128], in_=U1[:, i, :], identity=ident_b)),
                         reads=[r_U1, r_ident_b], writes=[rpt], signal=False)
                    P.op(PE, (lambda e, pt=pt, i=i, i4=i4: e.transpose(out=pt[:, i4 * 224 + 128:i4 * 224 + 224], in_=U2[0:96, i, :],
                                                                      identity=ident_b[0:96, 0:96])),
                         reads=[r_U2, r_ident_b], writes=[rpt], signal=(i4 == 3))
                OP(ACT, "copy", [rpt], [r_Ug], out=Ug[:, h4 * 4:(h4 + 1) * 4, :], in_=pt[:, 0:896].rearrange("p (i j) -> p i j", i=4))

            for d in range(2):
                for i in range(G8):
                    ps, rps = next_mm()
                    P.op(PE, (lambda e, ps=ps, d=d, i=i: e.matmul(ps[0:64, 0:224], lhsT=Eb[:, d, i, 0:64], rhs=Ug[:, i, :], start=True, stop=True)),
                         reads=[r_E, r_Ug], writes=[rps], signal=False)
                    P.op(PE, (lambda e, ps=ps, d=d, i=i: e.matmul(ps[0:64, 224:448], lhsT=Eb[:, d, i, 64:128], rhs=Ug[:, i, :], start=True, stop=True)),
                         reads=[r_E, r_Ug], writes=[rps], signal=True)
                    OP(ACT, "copy", [rps], [r_S[d]], relax=(i > 0), out=Sloc[d][:, :, i].rearrange("p r s j -> p r (s j)"),
                       in_=ps[0:64, 0:448].rearrange("p (r j) -> p r j", r=2))

            for d in range(2):
                E_ = SE[d]
                S = Sloc[d]
                RS = [r_S[d], r_tab, r_c5]
                WS = [r_S[d]]
                dsl = slice(d * G8, (d + 1) * G8)
                a8r = PR[0:64, dsl, 25]; a8i = PI[0:64, dsl, 25]
                a8rb = a8r.unsqueeze(1).unsqueeze(3).broadcast_to([64, 2, G8, 7])
                a8ib = a8i.unsqueeze(1).unsqueeze(3).broadcast_to([64, 2, G8, 7])
                tA, tB = stA[d], stB[d]
                OP(E_, "tensor_copy", RS, WS, out=ct2[d][:, 0], in_=a8i)
                TS(E_, ct2[d][:, 1], a8i, -1.0, None, ALU.mult, None, RS, WS)
                a8isb = ct2[d].unsqueeze(3).broadcast_to([64, 2, G8, 7])
                steps = range(1, 32) if d == 0 else range(30, -1, -1)
                for j in steps:
                    jp = j - 1 if d == 0 else j + 1
                    Xprev = S[:, :, :, :, jp]
                    Xc = S[:, :, :, :, j]
                    TT(E_, tA, a8rb, Xprev, ALU.mult, RS, [r_tAB[d][0]])
                    TT(E_, tB, a8isb, Xprev, ALU.mult, RS, [r_tAB[d][1]])
                    TT(E_, Xc, Xc, tA, ALU.add, RS + [r_tAB[d][0]], WS)
                    TT(E_, Xc, Xc, tB[:, ::-1], ALU.add, RS + [r_tAB[d][1]], WS)
                jl = 31 if d == 0 else 0
                a256r = PR[0:64, dsl, 26].unsqueeze(1).broadcast_to([64, 2, G8]); a256i = PI[0:64, dsl, 26].unsqueeze(1).broadcast_to([64, 2, G8])
                OP(E_, "tensor_copy", RS, WS, out=XE[d][:, 0], in_=h0_sb[:, d, :, g0:g0 + G8])
                for k in range(3):
                    qq = k if d == 0 else 3 - k
                    TT(E_, ct1[d], a256r, XE[d][:, k], ALU.mult, RS, WS)
                    TT(E_, ct2[d], a256i, XE[d][:, k], ALU.mult, RS, WS)
                    TT(E_, XE[d][:, k + 1], ct1[d], S[:, :, :, 2 + qq, jl], ALU.add, RS, WS)
                    TT(E_, XE[d][:, k + 1, 0], XE[d][:, k + 1, 0], ct2[d][:, 1], ALU.subtract, RS, WS)
                    TT(E_, XE[d][:, k + 1, 1], XE[d][:, k + 1, 1], ct2[d][:, 0], ALU.add, RS, WS)
                for q in range(4):
                    k = q if d == 0 else 3 - q
                    if q == 0:
                        TS(E_, Xin[d], XE[d][:, k], oh_sb[:, q:q + 1], None, ALU.mult, None, RS, WS)
                    else:
                        TS(E_, ct1[d], XE[d][:, k], oh_sb[:, q:q + 1], None, ALU.mult, None, RS, WS)
                        TT(E_, Xin[d], Xin[d], ct1[d], ALU.add, RS, WS)
                prj = PR[0:64, dsl, 27:59].unsqueeze(1).broadcast_to([64, 2, G8, 32]); pij = PI[0:64, dsl, 27:59].unsqueeze(1).broadcast_to([64, 2, G8, 32])
                xib = Xin[d].unsqueeze(3).broadcast_to([64, 2, G8, 32])
                So = S[:, :, :, 6, :]
                TT(E_, W1[d], prj, xib, ALU.mult, RS, WS)
                TT(E_, W2[d], pij, xib, ALU.mult, RS, WS)
                TT(E_, So, So, W1[d], ALU.add, RS, WS)
                TT(E_, So[:, 0], So[:, 0], W2[d][:, 1], ALU.subtract, RS, WS)
                TT(E_, So[:, 1], So[:, 1], W2[d][:, 0], ALU.add, RS, WS)
                RX = RS + [r_Xp[d]]
                WX = [r_Xp[d], r_S[d]]
                xp = Xp[:, d]
                P.op(E_, (lambda e, xp=xp: e.memset(xp, 0.0)), reads=RX, writes=WX)
                for r in range(2):
                    src_p = S[:, r, :, 0:2, 0:31] if d == 0 else S[:, r, :, 0:2, 1:32]
                    dst_p = xp[:, r, :, 0:64].rearrange("p g (s j) -> p g s j", s=2)
                    dst_p = dst_p[:, :, :, 1:32] if d == 0 else dst_p[:, :, :, 0:31]
                    OP(E_, "tensor_copy", RX, WX, out=dst_p, in_=src_p)
                src_o = S[:, :, :, 6, 0:31] if d == 0 else S[:, :, :, 6, 1:32]
                dst_o = xp[:, :, :, 65:96] if d == 0 else xp[:, :, :, 64:95]
                OP(E_, "tensor_copy", RX, WX, out=dst_o, in_=src_o)
                jo = 64 if d == 0 else 95
                OP(E_, "tensor_copy", RX, WX, out=xp[:, :, :, jo], in_=Xin[d])
                OP(E_, "tensor_copy", RX + [r_fin], WX + [r_fin], out=FIN[:, :, d, :, g0:g0 + G8],
                   in_=S[:, :, :, 0:2, jl].rearrange("p r g s -> p s r g"))

            for h4 in range(2):
                ps, rps = next_mm()
                for i4 in range(4):
                    i = h4 * 4 + i4
                    for (cs, ce, os_) in [(0, 64, 0), (192, 224, 64)]:
                        n = ce - cs
                        oc = i4 * 96 + os_
                        for d in range(2):
                            last = (d == 1)
                            P.op(PE, (lambda e, ps=ps, d=d, i=i, oc=oc, n=n, cs=cs, ce=ce: e.matmul(ps[:, oc:oc + n], lhsT=Mb[:, d, i, :], rhs=Ug[:, i, cs:ce],
                                                                                                     start=(d == 0), stop=False)),
                                 reads=[r_M, r_Ug], writes=[rps], signal=False)
                            P.op(PE, (lambda e, ps=ps, d=d, i=i, oc=oc, n=n, os_=os_: e.matmul(ps[:, oc:oc + n], lhsT=Fb[0:64, d, i, :],
                                                                                              rhs=Xp[0:64, d, 0, i, os_:os_ + n], start=False, stop=False)),
                                 reads=[r_F, r_Xp[d]], writes=[rps], signal=False)
                            P.op(PE, (lambda e, ps=ps, d=d, i=i, oc=oc, n=n, os_=os_, last=last: e.matmul(ps[:, oc:oc + n], lhsT=F2b[0:64, d, i, :],
                                                                                                          rhs=Xp[0:64, d, 1, i, os_:os_ + n], start=False, stop=last)),
                                 reads=[r_F, r_Xp[d]], writes=[rps], signal=(last and i4 == 3 and os_ == 64))
                OP(ACT, "copy", [rps], [r_Ysb], out=Ysb[:, h4 * 4:(h4 + 1) * 4, :], in_=ps[:, 0:384].rearrange("p (i j) -> p i j", i=4))
                ps, rps = next_mm()
                for i4 in range(4):
                    i = h4 * 4 + i4
                    P.op(PE, (lambda e, ps=ps, i=i, i4=i4: e.transpose(out=ps[0:96, i4 * 128:(i4 + 1) * 128], in_=Ysb[:, i, :], identity=ident_f)),
                         reads=[r_Ysb, r_ident_f], writes=[rps], signal=(i4 == 3))
                OP(ACT, "copy", [rps], [r_YTM], out=YTM[0:96, :, h4 * 64:(h4 + 1) * 64].rearrange("p t (i c) -> p i t c", c=16),
                   in_=ps[0:96, 0:512].rearrange("p (i t c) -> p i t c", i=4, c=16))
            for h in range(2):
                ps, rps = next_mm()
                for t4 in range(4):
                    tau = h * 4 + t4
                    P.op(PE, (lambda e, ps=ps, tau=tau, t4=t4: e.transpose(out=ps[:, t4 * 96:(t4 + 1) * 96], in_=YTM[0:96, tau, :], identity=ident_f[0:96, 0:96])),
                         reads=[r_YTM, r_ident_f], writes=[rps], signal=(t4 == 3))
                OP(ACT, "copy", [rps], [r_s5y], out=s5yT[:, gb, :].rearrange("p (r t) -> p t r", t=8)[:, h * 4:(h + 1) * 4, :],
                   in_=ps[:, 0:384].rearrange("p (t r) -> p t r", t=4))

        for sqi in range(2):
            for d in range(2):
                ps, rps = next_mm()
                for r in range(2):
                    P.op(PE, (lambda e, ps=ps, sq=sqi, d=d, r=r: e.transpose(out=ps[:, r * 64:(r + 1) * 64], in_=FIN[0:64, sq, d, r, :], identity=ident_f[0:64, 0:64])),
                         reads=[r_fin, r_ident_f], writes=[rps], signal=(r == 1))
                fo = t1[:, 0, :]
                OP(ACT, "copy", [rps, r_pq], [r_pq], out=fo, in_=ps[:, 0:128])
                P.dma(SP, (lambda e, sq=sqi, d=d, fo=fo: e.dma_start(out=o_sre[sq, d], in_=fo[:, 0:64])), reads=[r_pq], writes=[r_osre])
                P.dma(SP, (lambda e, sq=sqi, d=d, fo=fo: e.dma_start(out=o_sim[sq, d], in_=fo[:, 64:128])), reads=[r_pq], writes=[r_osim])
        P.barrier()
        A.top = SAVE_TOP


        A.top = OVERLAY0
        wbuf[:] = [A.bf16([128, 8192]) for _ in range(NWB)]
        LOW0 = A.top
        branch_b = A.hi([128, 16, TO], BF16); r_bb = Res("bb")
        branch_a = A.hi([128, 16, TO], BF16); r_ba = Res("ba")
        A.mark()
        gAs = [A.f32([128, TO]) for _ in range(2)]; gBs = [A.f32([128, TO]) for _ in range(2)]; r_gs = [Res(), Res()]
        bglu = A.f32([128, 16]); r_bglu = Res()
        P.dma(SP, lambda e: e.dma_start(out=bglu, in_=bglu_fm), writes=[r_bglu])
        for ch in range(16):
            y = s5yT[:, ch, :]
            gA, gB, r_g = gAs[ch % 2], gBs[ch % 2], r_gs[ch % 2]
            R_ = [r_s5y, r_g]
            W_ = [r_g]
            TT(DVE, gA, y, y, ALU.mult, R_, W_)
            TS(DVE, gA, gA, 0.044715, 1.0, ALU.mult, ALU.add, R_, W_)
            TT(DVE, gA, gA, y, ALU.mult, R_, W_)
            OP(ACT, "activation", R_, W_, out=gB, in_=gA, func=AF.Sigmoid, scale=1.5957691216057308)
            OP(DVE, "tensor_tensor", R_, [r_s5y], relax=True, out=y, in0=y, in1=gB, op=ALU.mult)
        zt = [A.bf16([128, TO]) for _ in range(2)]; r_zt = [Res(), Res()]
        sgt = [A.bf16([128, 512]) for _ in range(2)]; r_sgt = [Res(), Res()]
        zc = {"i": 0, "ch": -1}

        def evac_glu(ch, n0, n, ps, rps):
            if ch != zc["ch"]:
                zc["ch"] = ch
                zc["i"] += 1
                i = zc["i"] % 2
                P.dma(SP, (lambda e, i=i, ch=ch: e.dma_start(out=zt[i], in_=ZB[ch])), reads=[r_zb], writes=[r_zt[i]])
            i = zc["i"] % 2
            k = (0 if n0 == 0 else 1)
            OP(ACT, "activation", [rps, r_bglu], [r_sgt[k]], out=sgt[k][:, 0:n], in_=ps[:, 0:n], func=AF.Sigmoid, bias=bglu[:, ch:ch + 1])
            TT(DVE, sgt[k][:, 0:n], sgt[k][:, 0:n], s5yT[:, ch, n0:n0 + n], ALU.mult, [r_sgt[k], r_s5y], [r_sgt[k]])
            TT(DVE, branch_b[:, ch, n0:n0 + n], sgt[k][:, 0:n], zt[i][:, n0:n0 + n], ALU.mult, [r_sgt[k], r_zt[i]], [r_bb])

        linear_fm(w_glu, 0, 16, s5yT, [r_s5y], 16, evac_glu)
        P.barrier()
        A.release()

        q_peT = s5yT
        r_qpe = Res("qpe")
        A.mark()
        ropeq = A.f32([128, 6, 64]); r_ropeq = Res()
        P.dma(SP, lambda e: e.dma_start(out=ropeq, in_=rope_q.rearrange("(a p) d -> p a d", p=128)), writes=[r_ropeq])
        qpf = A.f32([128, 16, 64]); r_qpf = Res()
        qpb = A.bf16([128, 16, 64]); r_qpb = Res()
        rt = [A.f32([128, 16, 32]) for _ in range(4)]; r_rt = Res()
        wi = ctr["w"] % NWB
        ctr["w"] += 1
        wpe = wbuf[wi].rearrange("p (hh k c) -> p hh k c", hh=2, k=8)
        for hh in range(2):
            for kc in range(8):
                P.dma(POOL, (lambda e, hh=hh, kc=kc: e.dma_start(
                    out=wpe[:, hh, kc, :].rearrange("p (h d) -> p h d", d=64),
                    in_=w_uq[kc * 128:(kc + 1) * 128, :].rearrange("p (h d) -> p h d", d=192)[:, hh * 8:(hh + 1) * 8, 128:192])), writes=[r_wbuf[wi]])
        for tt_ in range(6):
            for hh in range(2):
                ps, rps = next_mm()
                for kc in range(8):
                    P.op(PE, (lambda e, ps=ps, kc=kc, hh=hh, tt_=tt_: e.matmul(ps[:, 0:512], lhsT=cqnT[:, kc, tt_ * 128:(tt_ + 1) * 128], rhs=wpe[:, hh, kc, :],
                                                                              start=(kc == 0), stop=(kc == 7))),
                         reads=[r_cqn, r_wbuf[wi]], writes=[rps], signal=(kc == 7))
                OP(ACT, "copy", [rps], [r_qpf], out=qpf[:, hh * 8:(hh + 1) * 8, :].rearrange("p h d -> p (h d)"), in_=ps[:, 0:512])
            sv = qpf.rearrange("p h (x s d) -> p h x s d", x=2, s=2)
            dv = qpb.rearrange("p h (x s d) -> p h x s d", x=2, s=2)
            cosb = ropeq[:, tt_, 0:32].rearrange("p (x d) -> p x d", x=2).unsqueeze(1).broadcast_to([128, 16, 2, 16])
            sinb = ropeq[:, tt_, 32:64].rearrange("p (x d) -> p x d", x=2).unsqueeze(1).broadcast_to([128, 16, 2, 16])
            x1, x2 = sv[:, :, :, 0, :], sv[:, :, :, 1, :]
            tv = [t_.rearrange("p h (x d) -> p h x d", x=2) for t_ in rt]
            RR = [r_qpf, r_ropeq, r_rt]
            TT(DVE, tv[0], x1, cosb, ALU.mult, RR, [r_rt])
            TT(DVE, tv[1], x2, sinb, ALU.mult, RR, [r_rt])
            TT(DVE, tv[2], x2, cosb, ALU.mult, RR, [r_rt])
            TT(DVE, tv[3], x1, sinb, ALU.mult, RR, [r_rt])
            TT(DVE, dv[:, :, :, 0, :], tv[0], tv[1], ALU.subtract, [r_rt, r_qpb], [r_qpb])
            TT(DVE, dv[:, :, :, 1, :], tv[2], tv[3], ALU.add, [r_rt, r_qpb], [r_qpb])
            for hh in range(2):
                pt, rpt = next_tr()
                for h8 in range(8):
                    h = hh * 8 + h8
                    P.op(PE, (lambda e, pt=pt, h=h, h8=h8: e.transpose(out=pt[0:64, h8 * 128:(h8 + 1) * 128], in_=qpb[:, h, :], identity=ident_b)),
                         reads=[r_qpb, r_ident_b], writes=[rpt], signal=(h8 == 7))
                OP(ACT, "copy", [rpt], [r_qpe], out=q_peT[0:64, hh * 8:(hh + 1) * 8, tt_ * 128:(tt_ + 1) * 128],
                   in_=pt[0:64, :].rearrange("p (h t) -> p h t", h=8))
        P.barrier()
        A.release()

        A.mark()
        qT = A.bf16([128, TO]); r_qT = Res()
        kT = A.bf16([128, NKEY]); r_kT = Res()
        Vh = A.bf16([128, 16, 128]); r_Vh = Res()
        Pe = A.bf16([128, 1536]); r_Pe = Res()
        PT = A.bf16([128, 12, 128]); r_PT = Res()
        stt_ = A.f32([128, 16]); r_stt = Res()
        zat = [A.bf16([128, TO]) for _ in range(2)]; r_zat = [Res(), Res()]
        SCALE = 192.0 ** -0.5
        KR = [(0, 256), (0, 256), (256, 256), (256, 256), (512, 1536), (512, 1536)]
        for h in range(16):
            zi = h % 2
            P.dma(SP, (lambda e, zi=zi, h=h: e.dma_start(out=zat[zi], in_=ZA[h])), reads=[r_za], writes=[r_zat[zi]])
            wq, rwq = load_w(w_uq, 0, 8, h * 192, 128)
            wkv, rwkv = load_w(w_ukv, 0, 4, h * 256, 256)
            for (n0, n) in SEGS:
                ps, rps = next_mm()
                for kc in range(8):
                    P.op(PE, (lambda e, ps=ps, kc=kc, n0=n0, n=n, wq=wq: e.matmul(ps[:, 0:n], lhsT=wq[:, kc, :], rhs=cqnT[:, kc, n0:n0 + n],
                                                                                 start=(kc == 0), stop=(kc == 7))),
                         reads=[rwq, r_cqn], writes=[rps], signal=(kc == 7))
                OP(ACT, "copy", [rps], [r_qT], relax=(n0 > 0), out=qT[:, n0:n0 + n], in_=ps[:, 0:n])
            for s4 in range(4):
                ps, rps = next_mm()
                for kc in range(4):
                    P.op(PE, (lambda e, ps=ps, kc=kc, s4=s4, wkv=wkv: e.matmul(ps[:, 0:512], lhsT=wkv[:, kc, 0:128], rhs=ckvT_keys[:, kc, s4 * 512:(s4 + 1) * 512],
                                                                              start=(kc == 0), stop=(kc == 3))),
                         reads=[rwkv, r_ckvT], writes=[rps], signal=(kc == 3))
                OP(ACT, "copy", [rps], [r_kT], relax=(s4 > 0), out=kT[:, s4 * 512:(s4 + 1) * 512], in_=ps[:, 0:512])
            for b4 in range(4):
                ps, rps = next_mm()
                for bi in range(4):
                    blk = b4 * 4 + bi
                    for kc in range(4):
                        P.op(PE, (lambda e, ps=ps, kc=kc, bi=bi, blk=blk, wkv=wkv: e.matmul(ps[:, bi * 128:(bi + 1) * 128], lhsT=ckvT_keys[:, kc, blk * 128:(blk + 1) * 128],
                                                                                           rhs=wkv[:, kc, 128:256], start=(kc == 0), stop=(kc == 3))),
                             reads=[rwkv, r_ckvT], writes=[rps], signal=(kc == 3 and bi == 3))
                OP(DVE, "tensor_copy", [rps], [r_Vh], relax=(b4 > 0), out=Vh[:, b4 * 4:(b4 + 1) * 4, :], in_=ps[:, 0:512].rearrange("p (b d) -> p b d", b=4))
            for qt in range(6):
                k0, nk = KR[qt]
                nb = (nk + 511) // 512
                qs = slice(qt * 128, (qt + 1) * 128)
                banks = [(psf[3 + b], r_psf[3 + b]) for b in range(nb)]
                for b, (ps, rps) in enumerate(banks):
                    w_ = min(512, nk - b * 512)
                    ks = slice(k0 + b * 512, k0 + b * 512 + w_)
                    P.op(PE, (lambda e, ps=ps, w_=w_, ks=ks, qs=qs: e.matmul(ps[:, 0:w_], lhsT=qT[:, qs], rhs=kT[:, ks], start=True, stop=False)),
                         reads=[r_qT, r_kT], writes=[rps], signal=False)
                    P.op(PE, (lambda e, ps=ps, w_=w_, ks=ks, qs=qs, h=h: e.matmul(ps[:, 0:w_], lhsT=q_peT[0:64, h, qs], rhs=kpeT_keys[0:64, ks], start=False, stop=True)),
                         reads=[r_qpe, r_kpeT], writes=[rps], signal=True)
                RSt = [r_stt]
                for b, (ps, rps) in enumerate(banks):
                    w_ = min(512, nk - b * 512)
                    OP(DVE, "tensor_reduce", [rps] + RSt, [r_stt], out=stt_[:, b:b + 1], in_=ps[:, 0:w_], axis=AX.X, op=ALU.max)
                if nb > 1:
                    OP(DVE, "tensor_reduce", RSt, [r_stt], out=stt_[:, 4:5], in_=stt_[:, 0:nb], axis=AX.X, op=ALU.max)
                    mcol = stt_[:, 4:5]
                else:
                    mcol = stt_[:, 0:1]
                TS(DVE, stt_[:, 5:6], mcol, -SCALE, None, ALU.mult, None, RSt, [r_stt])
                for b, (ps, rps) in enumerate(banks):
                    w_ = min(512, nk - b * 512)
                    OP(ACT, "activation", [rps, r_stt, r_Pe], [r_Pe, r_stt], out=Pe[:, b * 512:b * 512 + w_], in_=ps[:, 0:w_], func=AF.Exp, scale=SCALE,
                       bias=stt_[:, 5:6], accum_out=stt_[:, 8 + b:9 + b])
                if nb > 1:
                    OP(DVE, "tensor_reduce", RSt, [r_stt], out=stt_[:, 6:7], in_=stt_[:, 8:8 + nb], axis=AX.X, op=ALU.add)
                    scol = stt_[:, 6:7]
                else:
                    scol = stt_[:, 8:9]
                OP(DVE, "reciprocal", RSt, [r_stt], out=stt_[:, 7:8], in_=scol)
                TS(DVE, Pe[:, 0:nk], Pe[:, 0:nk], stt_[:, 7:8], None, ALU.mult, None, [r_Pe, r_stt], [r_Pe])
                nblk = nk // 128
                for g8 in range((nblk + 7) // 8):
                    pt, rpt = next_tr()
                    nn = min(8, nblk - g8 * 8)
                    for bi in range(nn):
                        blk = g8 * 8 + bi
                        P.op(PE, (lambda e, pt=pt, bi=bi, blk=blk: e.transpose(out=pt[:, bi * 128:(bi + 1) * 128], in_=Pe[:, blk * 128:(blk + 1) * 128], identity=ident_b)),
                             reads=[r_Pe, r_ident_b], writes=[rpt], signal=(bi == nn - 1))
                    OP(ACT, "copy", [rpt], [r_PT], out=PT[:, g8 * 8:g8 * 8 + nn, :], in_=pt[:, 0:nn * 128].rearrange("p (b t) -> p b t", b=nn))
                ps, rps = next_mm()
                kb0 = k0 // 128
                for blk in range(nblk):
                    P.op(PE, (lambda e, ps=ps, blk=blk, kb0=kb0, nblk=nblk: e.matmul(ps[:, 0:128], lhsT=Vh[:, kb0 + blk, :], rhs=PT[:, blk, :],
                                                                        start=(blk == 0), stop=(blk == nblk - 1))),
                         reads=[r_Vh, r_PT], writes=[rps], signal=(blk == nblk - 1))
                TT(DVE, branch_a[:, h, qs], ps[:, 0:128], zat[zi][:, qs], ALU.mult, [rps, r_zat[zi]], [r_ba])
        P.barrier()
        A.release()

        ctr["mm_n"] = 6
        A.top = CONST_TOP
        wbuf[:] = [A.bf16([128, 8192]) for _ in range(NWB)]
        LOW1 = A.top
        merged = A.bf16([128, KC, TO]); r_mg = Res("merged")
        A.mark()
        gat = [A.bf16([128, TO]) for _ in range(2)]; r_gat = [Res(), Res()]
        gbt = [A.bf16([128, TO]) for _ in range(2)]; r_gbt = [Res(), Res()]
        mt = [A.f32([128, 512]) for _ in range(2)]; r_mt = [Res(), Res()]
        for j2 in range(16):
            wa, rwa = load_w(w_pa, 0, 16, j2 * 256, 256)
            wb, rwb = load_w(w_pb, 0, 16, j2 * 256, 256)
            for c2 in range(2):
                j = j2 * 2 + c2
                gi = j % 2
                P.dma(SP, (lambda e, gi=gi, j=j: e.dma_start(out=gat[gi], in_=GA[j])), reads=[r_ga], writes=[r_gat[gi]])
                P.dma(SP, (lambda e, gi=gi, j=j: e.dma_start(out=gbt[gi], in_=GB[j])), reads=[r_gb], writes=[r_gbt[gi]])
                for si, (n0, n) in enumerate(SEGS):
                    psa, rpsa = next_mm()
                    for kc in range(16):
                        P.op(PE, (lambda e, ps=psa, kc=kc, c2=c2, n0=n0, n=n, wa=wa: e.matmul(ps[:, 0:n], lhsT=wa[:, kc, c2 * 128:(c2 + 1) * 128], rhs=branch_a[:, kc, n0:n0 + n],
                                                                                             start=(kc == 0), stop=(kc == 15))),
                             reads=[rwa, r_ba], writes=[rpsa], signal=(kc == 15))
                    psb_, rpsb = next_mm()
                    for kc in range(16):
                        P.op(PE, (lambda e, ps=psb_, kc=kc, c2=c2, n0=n0, n=n, wb=wb: e.matmul(ps[:, 0:n], lhsT=wb[:, kc, c2 * 128:(c2 + 1) * 128], rhs=branch_b[:, kc, n0:n0 + n],
                                                                                              start=(kc == 0), stop=(kc == 15))),
                             reads=[rwb, r_bb], writes=[rpsb], signal=(kc == 15))
                    TT(DVE, mt[0][:, 0:n], psa[:, 0:n], gat[gi][:, n0:n0 + n], ALU.mult, [rpsa, r_gat[gi], r_mt[0]], [r_mt[0]])
                    TT(DVE, mt[1][:, 0:n], psb_[:, 0:n], gbt[gi][:, n0:n0 + n], ALU.mult, [rpsb, r_gbt[gi], r_mt[1]], [r_mt[1]])
                    TT(DVE, merged[:, j, n0:n0 + n], mt[0][:, 0:n], mt[1][:, 0:n], ALU.add, [r_mt[0], r_mt[1]], [r_mg])
        P.barrier()
        A.release()

        A.n = NW
        yacc = A.f32([128, 6, D]); r_yacc = Res("yacc")
        A.mark()
        gtl = [A.f32([128, 2, 256]) for _ in range(2)]; r_gtl = [Res(), Res()]
        for cb in range(16):
            wv, rw = load_w(w_o, 0, KC, cb * 256, 256)
            gi = cb % 2
            for m in range(2):
                P.dma(SP, (lambda e, gi=gi, m=m, cb=cb: e.dma_start(out=gtl[gi][:, m, :], in_=gate_d[m:m + 1, cb * 256:(cb + 1) * 256].partition_broadcast(128))),
                      reads=[r_gate_d], writes=[r_gtl[gi]])
            for tt_ in range(6):
                m = 0 if tt_ < 4 else 1
                ps, rps = next_mm()
                for kc in range(KC):
                    P.op(PE, (lambda e, ps=ps, kc=kc, tt_=tt_, wv=wv: e.matmul(ps[:, 0:256], lhsT=merged[:, kc, tt_ * 128:(tt_ + 1) * 128], rhs=wv[:, kc, :],
                                                                              start=(kc == 0), stop=(kc == KC - 1))),
                         reads=[rw, r_mg], writes=[rps], signal=(kc == KC - 1))
                TT(DVE, yacc[:, tt_, cb * 256:(cb + 1) * 256], ps[:, 0:256], gtl[gi][:, m, :], ALU.mult, [rps, r_gtl[gi]], [r_yacc])
        P.barrier()
        A.release()
        A.top = CONST_TOP
        lng = A.f32([128, D]); lnb = A.f32([128, D]); r_ln = Res()
        P.dma(SP, lambda e: e.dma_start(out=lng, in_=ln_g.partition_broadcast(128)), writes=[r_ln])
        P.dma(SP, lambda e: e.dma_start(out=lnb, in_=ln_b.partition_broadcast(128)), writes=[r_ln])
        xr = A.f32([128, D]); r_xr = Res()
        fs = A.f32([128, 64]); r_fs = Res()
        r_yout = Res("yout")
        for tt_ in range(6):
            z = yacc[:, tt_, :]
            P.dma(SP, (lambda e, tt_=tt_: e.dma_start(out=xr, in_=x_own[tt_ * 128:(tt_ + 1) * 128, :])), writes=[r_xr])
            OP(DVE, "scalar_tensor_tensor", [r_xr, r_yacc], [r_yacc], out=z, in0=xr, scalar=ALPHA, in1=z, op0=ALU.mult, op1=ALU.add)
            for c in range(8):
                OP(DVE, "bn_stats", [r_yacc, r_fs], [r_fs], out=fs[:, c * 6:(c + 1) * 6], in_=z[:, c * 512:(c + 1) * 512])
            OP(DVE, "bn_aggr", [r_fs], [r_fs], out=fs[:, 48:50], in_=fs[:, 0:48])
            TS(DVE, fs[:, 50:51], fs[:, 49:50], EPS, None, ALU.add, None, [r_fs], [r_fs])
            OP(ACT, "activation", [r_fs], [r_fs], out=fs[:, 51:52], in_=fs[:, 50:51], func=AF.Sqrt)
            OP(DVE, "reciprocal", [r_fs], [r_fs], out=fs[:, 52:53], in_=fs[:, 51:52])
            TS(DVE, fs[:, 53:54], fs[:, 48:49], fs[:, 52:53], -1.0, ALU.mult, ALU.mult, [r_fs], [r_fs])
            OP(ACT, "activation", [r_fs, r_yacc], [r_yacc], out=z, in_=z, func=AF.Identity, scale=fs[:, 52:53], bias=fs[:, 53:54])
            TT(DVE, z, z, lng, ALU.mult, [r_yacc, r_ln], [r_yacc])
            TT(DVE, z, z, lnb, ALU.add, [r_yacc, r_ln], [r_yacc])
            P.dma(SP, (lambda e, tt_=tt_, z=z: e.dma_start(out=y_out[tt_ * 128:(tt_ + 1) * 128, :], in_=z)), reads=[r_yacc], writes=[r_yout])

        P.wait_on(SP, [r_ockv, r_okpe, r_gate_d, r_uall, r_za, r_zb, r_ga, r_gb, r_osre, r_osim, r_yout])
        P.wait_on(ACT, [r_ckvT, r_kpeT])
        P.barrier()
        P.emit()
    return nc


def kernel(x_prompt, x_sample, cache_ckv, cache_kpe, state_s5_re, state_s5_im, c, c_ctx,
           w_ada, b_ada, w_in, g_qn, w_uq, g_kvn, w_ukv,
           s5_a_re, s5_a_im, s5_log_dt, s5_b_re, s5_b_im, s5_c_re, s5_c_im, s5_d,
           w_glu, b_glu, w_pa, w_pb, w_o, ln_g, ln_b):
    f = lambda a: np.ascontiguousarray(np.asarray(a, dtype=np.float32))
    x_prompt, x_sample = f(x_prompt), f(x_sample)
    if "nc" not in _NC_CACHE:
        try:
            build_program()
        except _Done:
            pass
    nc = _NC_CACHE["nc"]
    w_ada0, w_in0 = f(w_ada)[0], f(w_in)[0]
    b0 = f(b_ada)[0]
    b_fm = np.ascontiguousarray(b0[:2 * D].reshape(64, 128).T)
    b_gate = np.ascontiguousarray(np.broadcast_to(b0[2 * D:][None, :], (2, D)))
    g_qn_fm = np.ascontiguousarray(f(g_qn)[0].reshape(8, 128).T)
    tt = np.arange(TA)
    inv = np.power(10000.0, -np.arange(16, dtype=np.float32) / 16).astype(np.float32)
    ang_r = (tt // 64).astype(np.float32)[:, None] * inv[None, :]
    ang_c = (tt % 64).astype(np.float32)[:, None] * inv[None, :]
    rope_k = np.ascontiguousarray(np.concatenate([np.cos(ang_r), np.cos(ang_c), np.sin(ang_r), np.sin(ang_c)], axis=1).astype(np.float32))
    are_, aim_, ldt_ = f(s5_a_re)[0], f(s5_a_im)[0], f(s5_log_dt)[0]
    a_t = lambda a: a.transpose(2, 0, 1).reshape(64, 256)
    s5A = np.zeros((128, 3, 256), np.float32)
    s5A[:, 0] = np.concatenate([a_t(are_), a_t(are_)], 0)
    s5A[:, 1] = np.concatenate([a_t(aim_), a_t(aim_)], 0)
    s5A[:, 2] = np.broadcast_to(ldt_.reshape(1, 256), (128, 256))
    bre, bim = f(s5_b_re)[0], f(s5_b_im)[0]
    b_t = lambda a: a.transpose(2, 0, 1, 3).reshape(64, 256, 16)
    s5B = np.stack([np.concatenate([b_t(bre), b_t(bim)], 0), np.concatenate([b_t(bim), b_t(bre)], 0)], axis=1)
    cre, cim = f(s5_c_re)[0], f(s5_c_im)[0]
    c_t = lambda a: a.transpose(3, 0, 1, 2).reshape(64, 256, 16)
    s5C = np.stack([np.concatenate([c_t(cre), c_t(cim)], 0), np.concatenate([c_t(cim), c_t(cre)], 0)], axis=1)
    s5A, s5B, s5C = map(np.ascontiguousarray, (s5A, s5B.astype(np.float32), s5C.astype(np.float32)))
    ar8 = np.arange(8, dtype=np.float32); ar32 = np.arange(32, dtype=np.float32)
    kt = np.zeros((2, 59), np.float32)
    kt[0] = np.concatenate([7 - ar8, ar8 - 7, ar8 + 1, [1, 8, 256], 8 * (ar32 + 1)])
    kt[1] = np.concatenate([ar8, -ar8, 8 - ar8, [1, 8, 256], 8 * (32 - ar32)])
    kt_in = np.ascontiguousarray(np.broadcast_to(kt[None], (128, 2, 59)))
    sgn_in = np.concatenate([-np.ones((64, 1), np.float32), np.ones((64, 1), np.float32)], 0)
    ss = np.repeat(np.arange(8), 16)
    mask_in = np.ascontiguousarray(np.stack([(ss[:, None] <= ss[None, :]), (ss[:, None] >= ss[None, :])], axis=1).astype(np.float32))
    dcol_in = np.ascontiguousarray(np.tile(f(s5_d)[0].reshape(128, 16).T, (8, 1)))
    W_UQ, W_UKV, W_GLU, W_PA, W_PB, W_O = f(w_uq)[0], f(w_ukv)[0], f(w_glu)[0], f(w_pa)[0], f(w_pb)[0], f(w_o)[0]
    bglu_fm = np.ascontiguousarray(f(b_glu)[0].reshape(16, 128).T)
    rope_id = np.concatenate([np.ones((512, 32), np.float32), np.zeros((512, 32), np.float32)], axis=1)
    in_maps = []
    for i in range(8):
        b, q = i // 4, i % 4
        x_own = np.concatenate([x_prompt[2 * i], x_prompt[2 * i + 1], x_sample[b, q * 256:(q + 1) * 256]], axis=0)
        cond = np.stack([f(c_ctx), f(c)[b]], axis=0)
        condT = np.ascontiguousarray(cond.reshape(2, KC, 128).transpose(2, 1, 0))
        in_maps.append({
            "x_own": np.ascontiguousarray(x_own), "x_aux": np.ascontiguousarray(x_sample[b]),
            "condT": condT, "b_fm": b_fm, "b_gate": b_gate, "w_ada": w_ada0, "w_in": w_in0,
            "g_kvn": f(g_kvn), "g_qn_fm": g_qn_fm,
            "s5A": s5A, "s5B": s5B, "s5C": s5C, "kt_in": kt_in, "sgn_in": sgn_in, "mask_in": mask_in, "dcol_in": dcol_in,
            "h0_in": np.ascontiguousarray(np.stack([f(state_s5_re)[b, 0], f(state_s5_im)[b, 0]], axis=1).transpose(3, 0, 1, 2)),
            "oh_in": np.ascontiguousarray(np.broadcast_to(np.eye(4, dtype=np.float32)[q][None, :], (64, 4))),
            "w_uq": W_UQ, "w_ukv": W_UKV, "w_glu": W_GLU, "w_pa": W_PA, "w_pb": W_PB, "w_o": W_O, "bglu_fm": bglu_fm,
            "rope_q": np.ascontiguousarray(np.concatenate([rope_id, rope_k[q * 256:(q + 1) * 256]], axis=0)),
            "ln_g": f(ln_g), "ln_b": f(ln_b),
            "cache_ckv": f(cache_ckv)[b, 0], "cache_kpe": f(cache_kpe)[b, 0], "rope_k": rope_k,
        })
    ncores = int(os.environ.get("KCORES", "8"))
    res = run_bass_kernel_spmd(nc, in_maps[:ncores], core_ids=list(range(ncores)))
    R = list(res.results) + [res.results[0]] * (8 - ncores)
    y_prompt = np.zeros((16, 256, D), np.float32)
    y_sample = np.zeros((2, 1024, D), np.float32)
    n_ckv = np.zeros((16, 1, 256, 512), np.float32)
    n_kpe = np.zeros((16, 1, 256, 64), np.float32)
    n_sre = np.zeros((16, 1, 2, 128, 64), np.float32)
    n_sim = np.zeros((16, 1, 2, 128, 64), np.float32)
    for i in range(8):
        n_ckv[2 * i:2 * i + 2, 0] = R[i]["o_ckv"].reshape(2, 256, 512)
        n_kpe[2 * i:2 * i + 2, 0] = R[i]["o_kpe"].reshape(2, 256, 64)
        n_sre[2 * i:2 * i + 2, 0] = R[i]["o_sre"]
        n_sim[2 * i:2 * i + 2, 0] = R[i]["o_sim"]
        yo = R[i]["y_out"]
        y_prompt[2 * i] = yo[0:256]
        y_prompt[2 * i + 1] = yo[256:512]
        y_sample[i // 4, (i % 4) * 256:(i % 4 + 1) * 256] = yo[512:768]
    global DBG
    DBG = R
    return (y_prompt, y_sample, n_ckv, n_kpe, n_sre, n_sim)
```
